# Optimizing a Trainium2 kernel written in Bass

```python
import math
import jax, jax.numpy as jnp
from jax import lax
import numpy as np

D_MODEL = 1024
BATCH = 16
SEQ = 256
DEPTH = 4
DEC_BATCH = 4
DEC_SEQ = 1024
PAST_LEN = 512

GRID_W = 64
N_MIXERS = 2
N_HYENA_LAYERS = (DEPTH + N_MIXERS - 1) // N_MIXERS
N_ATTN_LAYERS = DEPTH // N_MIXERS
HEAD_DIM = 128
N_HEADS = D_MODEL // HEAD_DIM
N_KV_HEADS = 2
GROUP = N_HEADS // N_KV_HEADS
QKV_DIM = (N_HEADS + 2 * N_KV_HEADS) * HEAD_DIM
Q_BLOCK = 128
ROPE_THETA = 10000.0
D_FF = ((8 * D_MODEL + 3 * 256 - 1) // (3 * 256)) * 256
N_BANDS = 16
FILTER_EMB = 1 + 2 * N_BANDS
FILTER_HIDDEN = 64
DECAY_TARGET = 1e-2
FAST_DECAY_PCT = 0.3
SLOW_DECAY_PCT = 1.5
MIN_DECAY = math.log(DECAY_TARGET) / SLOW_DECAY_PCT
MAX_DECAY = math.log(DECAY_TARGET) / FAST_DECAY_PCT
EPS = 1e-6

kernel_name = "hyena_gqa_prefix_dit_step"


def rmsnorm(x, g):
    x32 = x.astype(jnp.float32)
    y = x32 * lax.rsqrt(jnp.mean(x32 * x32, axis=-1, keepdims=True) + EPS)
    return (y * g.astype(jnp.float32)).astype(x.dtype)


def adaln(cond, w, b):
    m = jax.nn.silu(cond) @ w + b
    return jnp.split(m[..., None, :], 6, axis=-1)


def modulate(x, g, shift, scale):
    return rmsnorm(x, g) * (1.0 + scale) + shift


def short_conv3(z, w, b):
    zp = jnp.pad(z, ((0, 0), (1, 1), (0, 0)))
    return zp[:, :-2] * w[0] + zp[:, 1:-1] * w[1] + zp[:, 2:] * w[2] + b


def hyena_filters(L, w1, b1, freq, w2, b2, w3):
    f32 = jnp.float32
    t = jnp.arange(L, dtype=f32) / L
    bands = jnp.arange(1, N_BANDS + 1, dtype=f32)
    ang = 2.0 * math.pi * t[:, None] * bands[None, :]
    feats = jnp.concatenate([t[:, None], jnp.cos(ang), jnp.sin(ang)], axis=-1)
    freq = freq.astype(f32)
    h = jnp.sin(freq[0] * (feats @ w1.astype(f32) + b1.astype(f32)))
    h = jnp.sin(freq[1] * (h @ w2.astype(f32) + b2.astype(f32)))
    h = h @ w3.astype(f32)
    deltas = jnp.abs(jnp.linspace(MIN_DECAY, MAX_DECAY, D_MODEL, dtype=f32))
    window = jnp.exp(-t[:, None] * deltas[None, :])
    h_f = h[:, :D_MODEL] * window
    h_b = h[:, D_MODEL:] * window * (jnp.arange(L) > 0).astype(f32)[:, None]
    norm = jnp.sqrt(jnp.sum(h_f * h_f + h_b * h_b, axis=0, keepdims=True) + EPS)
    return h_f / norm, h_b / norm


def bidir_long_conv(u, h_f, h_b, bias):
    L = u.shape[1]
    n = 2 * L
    u32 = u.astype(jnp.float32)
    Hf = jnp.fft.rfft(h_f, n=n, axis=0)
    Hb = jnp.fft.rfft(h_b, n=n, axis=0)
    y_f = jnp.fft.irfft(jnp.fft.rfft(u32, n=n, axis=1) * Hf, n=n, axis=1)[:, :L]
    y_b = jnp.flip(jnp.fft.irfft(jnp.fft.rfft(jnp.flip(u32, axis=1), n=n, axis=1) * Hb, n=n, axis=1)[:, :L], axis=1)
    return (y_f + y_b + u32 * bias.astype(jnp.float32)).astype(u.dtype)


def hyena_mixer(h, w_in, conv_w, conv_b, f_w1, f_b1, f_freq, f_w2, f_b2, f_w3, bias, w_out):
    L = h.shape[1]
    z = short_conv3(h @ w_in, conv_w, conv_b)
    x0, x1, v = jnp.split(z, 3, axis=-1)
    h_f, h_b = hyena_filters(L, f_w1, f_b1, f_freq, f_w2, f_b2, f_w3)
    y = x0 * bidir_long_conv(x1 * v, h_f, h_b, bias)
    return y @ w_out


def qkv_heads(h, w_qkv, q_g, k_g):
    B, T, _ = h.shape
    qkv = h @ w_qkv
    q = qkv[..., :N_HEADS * HEAD_DIM].reshape(B, T, N_HEADS, HEAD_DIM)
    k = qkv[..., N_HEADS * HEAD_DIM:(N_HEADS + N_KV_HEADS) * HEAD_DIM].reshape(B, T, N_KV_HEADS, HEAD_DIM)
    v = qkv[..., (N_HEADS + N_KV_HEADS) * HEAD_DIM:].reshape(B, T, N_KV_HEADS, HEAD_DIM)
    return rmsnorm(q, q_g), rmsnorm(k, k_g), v


def axial_rope_tables(T):
    ROWS = T // GRID_W
    row = jnp.repeat(jnp.arange(ROWS), GRID_W).astype(jnp.float32)
    col = jnp.tile(jnp.arange(GRID_W), ROWS).astype(jnp.float32)
    half = HEAD_DIM // 2
    freqs = ROPE_THETA ** (-jnp.arange(0, half, 2, dtype=jnp.float32) / half)
    ang = jnp.concatenate([row[:, None] * freqs[None, :], col[:, None] * freqs[None, :]], axis=-1)
    return jnp.cos(ang), jnp.sin(ang)


def apply_rope(x, cos, sin):
    B, T, H, Dh = x.shape
    xr = x.astype(jnp.float32).reshape(B, T, H, Dh // 2, 2)
    a, b = xr[..., 0], xr[..., 1]
    c = cos[None, :, None, :]
    s = sin[None, :, None, :]
    out = jnp.stack([a * c - b * s, a * s + b * c], axis=-1)
    return out.reshape(B, T, H, Dh).astype(x.dtype)


def block_attention(q, k, v):
    B, Tq, _, _ = q.shape
    nblk = Tq // Q_BLOCK
    scale = HEAD_DIM ** -0.5
    qb = q.reshape(B, nblk, Q_BLOCK, N_KV_HEADS, GROUP, HEAD_DIM).swapaxes(0, 1)
    k32 = k.astype(jnp.float32)

    def one_block(qblk):
        s = jnp.einsum('bqkgd,bskd->bkgqs', qblk.astype(jnp.float32), k32) * scale
        p = jax.nn.softmax(s, axis=-1)
        return jnp.einsum('bkgqs,bskd->bqkgd', p.astype(v.dtype), v)

    o = lax.map(one_block, qb)
    return o.swapaxes(0, 1).reshape(B, Tq, N_HEADS * HEAD_DIM)


def swiglu(h, wg, wu, wd):
    return (jax.nn.silu(h @ wg) * (h @ wu)) @ wd


def setup_inputs(seed: int = 0) -> dict:
    key = jax.random.key(seed)
    ks = jax.random.split(key, 32)
    f32 = jnp.float32

    def nrm(k, shape, s):
        return jax.random.normal(k, shape, f32) * s

    D = D_MODEL
    NH, NA = N_HYENA_LAYERS, N_ATTN_LAYERS
    return {
        "x_prompt": nrm(ks[0], (BATCH, SEQ, D), 1.0),
        "x_sample": nrm(ks[1], (DEC_BATCH, DEC_SEQ, D), 1.0),
        "cache_k": nrm(ks[2], (DEC_BATCH, NA, PAST_LEN, N_KV_HEADS, HEAD_DIM), 1.0),
        "cache_v": nrm(ks[3], (DEC_BATCH, NA, PAST_LEN, N_KV_HEADS, HEAD_DIM), 1.0),
        "c": nrm(ks[4], (DEC_BATCH, D), 1.0),
        "c_ctx": nrm(ks[5], (D,), 1.0),
        "mod_w": nrm(ks[6], (DEPTH, D, 6 * D), 0.5 * D ** -0.5),
        "mod_b": nrm(ks[7], (DEPTH, 6 * D), 0.02),
        "norm_mix": 1.0 + nrm(ks[8], (DEPTH, D), 0.02),
        "norm_ffn": 1.0 + nrm(ks[9], (DEPTH, D), 0.02),
        "hy_w_in": nrm(ks[10], (NH, D, 3 * D), D ** -0.5),
        "hy_conv_w": nrm(ks[11], (NH, 3, 3 * D), 3 ** -0.5),
        "hy_conv_b": nrm(ks[12], (NH, 3 * D), 0.02),
        "hy_f_w1": nrm(ks[13], (NH, FILTER_EMB, FILTER_HIDDEN), FILTER_EMB ** -0.5),
        "hy_f_b1": nrm(ks[14], (NH, FILTER_HIDDEN), 0.02),
        "hy_f_freq": 1.0 + nrm(ks[15], (NH, 2, FILTER_HIDDEN), 0.02),
        "hy_f_w2": nrm(ks[16], (NH, FILTER_HIDDEN, FILTER_HIDDEN), FILTER_HIDDEN ** -0.5),
        "hy_f_b2": nrm(ks[17], (NH, FILTER_HIDDEN), 0.02),
        "hy_f_w3": nrm(ks[18], (NH, FILTER_HIDDEN, 2 * D), FILTER_HIDDEN ** -0.5),
        "hy_bias": nrm(ks[19], (NH, D), 0.1),
        "hy_w_out": nrm(ks[20], (NH, D, D), D ** -0.5),
        "at_w_qkv": nrm(ks[21], (NA, D, QKV_DIM), D ** -0.5),
        "at_q_norm": 1.0 + nrm(ks[22], (NA, HEAD_DIM), 0.02),
        "at_k_norm": 1.0 + nrm(ks[23], (NA, HEAD_DIM), 0.02),
        "at_w_out": nrm(ks[24], (NA, N_HEADS * HEAD_DIM, D), (N_HEADS * HEAD_DIM) ** -0.5),
        "ffn_w_gate": nrm(ks[25], (DEPTH, D, D_FF), D ** -0.5),
        "ffn_w_up": nrm(ks[26], (DEPTH, D, D_FF), D ** -0.5),
        "ffn_w_down": nrm(ks[27], (DEPTH, D_FF, D), D_FF ** -0.5),
        "final_norm": 1.0 + nrm(ks[28], (D,), 0.02),
    }


def reference(x_prompt, x_sample, cache_k, cache_v, c, c_ctx,
              mod_w, mod_b, norm_mix, norm_ffn,
              hy_w_in, hy_conv_w, hy_conv_b, hy_f_w1, hy_f_b1, hy_f_freq, hy_f_w2, hy_f_b2, hy_f_w3,
              hy_bias, hy_w_out,
              at_w_qkv, at_q_norm, at_k_norm, at_w_out,
              ffn_w_gate, ffn_w_up, ffn_w_down, final_norm):
    y_p = x_prompt
    y_s = x_sample
    cos_s, sin_s = axial_rope_tables(y_s.shape[1])
    new_k_list, new_v_list = [], []

    for i in range(DEPTH):
        sh_p, sc_p, g_p, sh2_p, sc2_p, g2_p = adaln(c_ctx, mod_w[i], mod_b[i])
        sh_s, sc_s, g_s, sh2_s, sc2_s, g2_s = adaln(c, mod_w[i], mod_b[i])
        h_p = modulate(y_p, norm_mix[i], sh_p, sc_p)
        h_s = modulate(y_s, norm_mix[i], sh_s, sc_s)
        j = i // N_MIXERS
        if i % N_MIXERS == 0:
            hy = (hy_w_in[j], hy_conv_w[j], hy_conv_b[j], hy_f_w1[j], hy_f_b1[j], hy_f_freq[j],
                  hy_f_w2[j], hy_f_b2[j], hy_f_w3[j], hy_bias[j], hy_w_out[j])
            m_p = hyena_mixer(h_p, *hy)
            m_s = hyena_mixer(h_s, *hy)
        else:
            q_p, k_p, v_p = qkv_heads(h_p, at_w_qkv[j], at_q_norm[j], at_k_norm[j])
            new_k_list.append(k_p)
            new_v_list.append(v_p)
            m_p = block_attention(q_p, k_p, v_p) @ at_w_out[j]
            q_s, k_s, v_s = qkv_heads(h_s, at_w_qkv[j], at_q_norm[j], at_k_norm[j])
            q_s = apply_rope(q_s, cos_s, sin_s)
            k_s = apply_rope(k_s, cos_s, sin_s)
            k_all = jnp.concatenate([k_s, cache_k[:, j]], axis=1)
            v_all = jnp.concatenate([v_s, cache_v[:, j]], axis=1)
            m_s = block_attention(q_s, k_all, v_all) @ at_w_out[j]
        y_p = y_p + g_p * m_p
        y_s = y_s + g_s * m_s
        f_p = modulate(y_p, norm_ffn[i], sh2_p, sc2_p)
        f_s = modulate(y_s, norm_ffn[i], sh2_s, sc2_s)
        y_p = y_p + g2_p * swiglu(f_p, ffn_w_gate[i], ffn_w_up[i], ffn_w_down[i])
        y_s = y_s + g2_s * swiglu(f_s, ffn_w_gate[i], ffn_w_up[i], ffn_w_down[i])

    y_prompt = rmsnorm(y_p, final_norm)
    y_sample = rmsnorm(y_s, final_norm)
    new_k = jnp.stack(new_k_list, axis=1)
    new_v = jnp.stack(new_v_list, axis=1)
    return (y_prompt, y_sample, new_k, new_v)
```

```python
from contextlib import ExitStack
import concourse.bass as bass
import concourse.mybir as mybir

F32 = mybir.dt.float32
BF16 = mybir.dt.bfloat16
ACT = mybir.ActivationFunctionType
ALU = mybir.AluOpType

SAME_ENG_SYNC = True


import types


def _freeze(fn):
    cl = fn.__closure__
    if not cl:
        return fn
    cells = []
    for c in cl:
        try:
            cells.append(types.CellType(c.cell_contents))
        except ValueError:
            cells.append(c)
    g = types.FunctionType(fn.__code__, fn.__globals__, fn.__name__, fn.__defaults__, tuple(cells))
    g.__kwdefaults__ = fn.__kwdefaults__
    return g


class Tok:
    __slots__ = ("name", "w", "readers", "dma_w", "sem", "dma_r", "rsem", "n_dma_w", "n_dma_r")

    def __init__(self, name):
        self.name = name
        self.w = None
        self.readers = []
        self.dma_w = []
        self.sem = None
        self.rsem = None
        self.n_dma_w = 0
        self.n_dma_r = 0


class Op:
    __slots__ = ("eng", "fn", "deps", "dmadeps", "signal", "signo", "is_dma", "dma_sem", "dma_val", "idx")

    def __init__(self, eng, fn):
        self.eng = eng
        self.fn = _freeze(fn)
        self.deps = set()
        self.dmadeps = {}
        self.signal = False
        self.signo = 0
        self.is_dma = False
        self.dma_sem = None
        self.dma_val = 0


class Prog:
    ENGS = ("tensor", "vector", "scalar", "gpsimd", "sync")

    def __init__(self, nc):
        self.nc = nc
        self.es = ExitStack()
        self.ops = []
        self.toks = {}
        self.final_waits = {}
        self.nsem = 0

    def sb(self, name, shape, dt):
        return self.es.enter_context(self.nc.sbuf_tensor(name, shape, dt))

    def ps(self, name, shape, dt):
        return self.es.enter_context(self.nc.psum_tensor(name, shape, dt))

    def newsem(self, name):
        self.nsem += 1
        return self.es.enter_context(self.nc.semaphore(name))

    def tok(self, key):
        t = self.toks.get(key)
        if t is None:
            t = Tok(key)
            self.toks[key] = t
        return t

    def _toks(self, lst):
        out = []
        for k in lst:
            out.append(k if isinstance(k, Tok) else self.tok(k))
        return out

    def op(self, eng, fn, reads=(), writes=()):
        o = Op(eng, fn)
        o.idx = len(self.ops)
        for t in self._toks(reads):
            if t.w is not None:
                o.deps.add(t.w)
            for d in t.dma_w:
                o.dmadeps[d.dma_sem] = max(o.dmadeps.get(d.dma_sem, 0), d.dma_val)
            t.readers = [r for r in t.readers if r.is_dma or r.eng != o.eng] + [o]
        for t in self._toks(writes):
            if t.w is not None:
                o.deps.add(t.w)
            for d in t.dma_w:
                o.dmadeps[d.dma_sem] = max(o.dmadeps.get(d.dma_sem, 0), d.dma_val)
            for r in t.readers:
                if r is o:
                    continue
                if r.is_dma:
                    o.dmadeps[r.dma_sem] = max(o.dmadeps.get(r.dma_sem, 0), r.dma_val)
                else:
                    o.deps.add(r)
            t.w = o
            t.readers = []
            t.dma_w = []
        self.ops.append(o)
        return o

    def dma_in(self, eng, fn, dst, reads=()):
        t = self._toks([dst])[0]
        o = Op(eng, fn)
        o.idx = len(self.ops)
        o.is_dma = True
        if t.sem is None:
            t.sem = {}
        if eng not in t.sem:
            t.sem[eng] = [self.newsem("d_" + eng[0] + str(t.name)), 0]
        if t.w is not None:
            o.deps.add(t.w)
        for r in t.readers:
            if r.is_dma:
                o.dmadeps[r.dma_sem] = max(o.dmadeps.get(r.dma_sem, 0), r.dma_val)
            else:
                o.deps.add(r)
        t.sem[eng][1] += 1
        o.dma_sem = t.sem[eng][0]
        o.dma_val = 16 * t.sem[eng][1]
        t.w = None
        t.readers = []
        t.dma_w = [d for d in t.dma_w if d.dma_sem is not o.dma_sem] + [o]
        self.ops.append(o)
        return o

    def dma_out(self, eng, fn, src):
        t = self._toks([src])[0]
        o = Op(eng, fn)
        o.idx = len(self.ops)
        o.is_dma = True
        if t.rsem is None:
            t.rsem = self.newsem("r_" + str(t.name))
        if t.w is not None:
            o.deps.add(t.w)
        for d in t.dma_w:
            o.dmadeps[d.dma_sem] = max(o.dmadeps.get(d.dma_sem, 0), d.dma_val)
        t.n_dma_r += 1
        o.dma_sem = t.rsem
        o.dma_val = 16 * t.n_dma_r
        t.readers.append(o)
        self.final_waits[t.rsem] = o.dma_val
        self.ops.append(o)
        return o

    def emit(self):
        nc = self.nc
        esem = {e: self.newsem("e_" + e) for e in self.ENGS}
        for o in self.ops:
            for d in o.deps:
                if d.eng == o.eng and (o.eng == "tensor" or not SAME_ENG_SYNC):
                    continue
                d.signal = True
        cnt = {e: 0 for e in self.ENGS}
        per = {e: [] for e in self.ENGS}
        for o in self.ops:
            if o.signal and not o.is_dma:
                cnt[o.eng] += 1
                o.signo = cnt[o.eng]
            per[o.eng].append(o)
        final_waits = self.final_waits
        stats = {e: [0, 0] for e in self.ENGS}

        def run(engname, eng):
            seen = {}
            for o in per[engname]:
                waits = {}
                for d in o.deps:
                    if d.eng == engname and (engname == "tensor" or not SAME_ENG_SYNC):
                        continue
                    s = esem[d.eng]
                    waits[s] = max(waits.get(s, 0), d.signo)
                for s, v in o.dmadeps.items():
                    waits[s] = max(waits.get(s, 0), v)
                for s, v in waits.items():
                    if seen.get(s, 0) >= v:
                        continue
                    eng.wait_ge(s, v)
                    seen[s] = v
                    stats[engname][1] += 1
                ins = o.fn(eng)
                stats[engname][0] += 1
                if o.is_dma:
                    ins.then_inc(o.dma_sem, 16)
                elif o.signal:
                    ins.then_inc(esem[engname], 1)
            if engname == "sync":
                for s, v in final_waits.items():
                    eng.wait_ge(s, v)

        with nc.Block() as block:
            @block.tensor
            def _(e):
                run("tensor", e)

            @block.vector
            def _(e):
                run("vector", e)

            @block.scalar
            def _(e):
                run("scalar", e)

            @block.gpsimd
            def _(e):
                run("gpsimd", e)

            @block.sync
            def _(e):
                run("sync", e)
        self.stats = stats
        self.es.close()


import math
import numpy as np
import ml_dtypes
from concourse.bass_utils import run_bass_kernel_spmd

NLAYER = 4
EPS = 1e-6
D_FF = 2816
NFF = 22
SM_SCALE = 128 ** -0.5

R_CVEC = 0
R_MODB = 8
R_NMIX = 200
R_NFFN = 232
R_FIN = 264
R_CONVW = 272
R_CONVB = 416
R_QN = 464
R_KN = 466
R_FB1 = 468
R_FREQ = 470
R_FB2 = 474
NROW = 512


def build(nsteps=8):
    nc = bass.Bass("TRN2", target_bir_lowering=False)

    def DI(name, shape, dt=F32):
        return nc.dram_tensor(name, shape, dt, kind="ExternalInput").ap()

    x_d = DI("x", [1024, 1024])
    pvec_d = DI("pvec", [NROW, 128])
    mod_w = DI("mod_w", [4, 1024, 6144])
    hy_w_in = DI("hy_w_in", [2, 1024, 3072])
    hy_w_out = DI("hy_w_out", [2, 1024, 1024])
    at_w_qkv = DI("at_w_qkv", [2, 1024, 1536])
    at_w_out = DI("at_w_out", [2, 1024, 1024])
    ffn_wg = DI("ffn_wg", [4, 1024, D_FF])
    ffn_wu = DI("ffn_wu", [4, 1024, D_FF])
    ffn_wd = DI("ffn_wd", [4, D_FF, 1024])
    f_w1 = DI("f_w1", [2, 33, 64])
    f_w2 = DI("f_w2", [2, 64, 64])
    f_w3 = DI("f_w3", [2, 64, 2048])
    hy_bias = DI("hy_bias", [2, 1024])
    cache_k = DI("cache_k", [2, 512, 256])
    cache_v = DI("cache_v", [2, 512, 256])
    ident_d = DI("ident", [128, 128])
    rotT_d = DI("rotT", [128, 128])
    cosF_d = DI("cosF", [128, 1024])
    sinF_d = DI("sinF", [128, 1024])
    maskb_d = DI("maskb", [128, 48])
    cst_d = DI("cst", [128, 8])
    featsT_d = DI("featsT", [33, 1024])
    win_d = DI("win", [1024, 2, 1024])
    sumw_d = DI("sumw", [128, 1024], BF16)
    CT_d = DI("CT", [1024, 1024], BF16)
    ST_d = DI("ST", [1024, 1024], BF16)
    CI_d = DI("CI", [1024, 1024], BF16)
    SI_d = DI("SI", [1024, 1024], BF16)

    y_d = nc.dram_tensor("y", [1024, 1024], F32, kind="ExternalOutput").ap()
    nk_d = nc.dram_tensor("new_k", [2, 1024, 256], F32, kind="ExternalOutput").ap()
    nv_d = nc.dram_tensor("new_v", [2, 1024, 256], F32, kind="ExternalOutput").ap()

    P = Prog(nc)
    Y = P.sb("Y", [128, 8, 1024], F32)
    H = P.sb("H", [128, 8, 1024], BF16)
    BIG = P.sb("BIG", [128, 40, 1024], BF16)
    WS = [P.sb(f"WS{i}", [128, 4096], BF16) for i in range(4)]
    PV = P.sb("PV", [128, 512], F32)
    MODA = [P.sb(f"MODA{i}", [128, 48], F32) for i in range(2)]
    AG = [P.sb(f"AG{i}", [128, 16], F32) for i in range(2)]
    ident = P.sb("identf", [128, 128], F32)
    identb = P.sb("identb", [128, 128], BF16)
    rotT = P.sb("rotTs", [128, 128], F32)
    onesb = P.sb("onesb", [128, 128], BF16)
    onesf = P.sb("onesf", [128, 128], F32)
    maskb = P.sb("maskbs", [128, 48], F32)
    cst = P.sb("csts", [128, 8], F32)
    scT = P.sb("scT", [128, 8], BF16)
    RSB = P.sb("RSB", [128, 1024], F32)
    RS = [RSB[:, 0:512], RSB[:, 512:1024]]
    LN = P.sb("LNt", [128, 512], F32)
    SQ = [P.sb(f"SQ{i}", [128, 512], BF16) for i in range(3)]
    TM = [P.sb(f"TM{i}", [128, 1032], F32) for i in range(4)]
    EB = [P.sb(f"EB{i}", [128, 512], BF16) for i in range(4)]
    UT = P.sb("UTt", [128, 1024], BF16)
    BIASB = P.sb("BIASB", [128, 1024], BF16)
    MS = [P.sb(f"MS{i}", [128, 8, 128], BF16) for i in range(2)]
    PB = [P.ps(f"PB{i}", [128, 512], F32) for i in range(8)]

    def pb(i):
        return ("pb", i)

    def bg(g):
        return ("B", g)

    def Yt(c, tt):
        return ("Y", c, tt)

    def Ht(c, tt):
        return ("H", c, tt)

    ws_state = {"n": 0}

    def wslot():
        s = ws_state["n"] % 4
        ws_state["n"] += 1
        return s

    def wload_cast(src_view, a, b):
        s = wslot()
        v = WS[s][:, 0:a * b].rearrange("p (a b) -> p a b", a=a)
        P.dma_in("gpsimd", lambda e: e.dma_start(out=v, in_=src_view), ("ws", s))
        return v, ("ws", s)

    def wload_bf(src_view, a, b):
        s = wslot()
        v = WS[s][:, 0:a * b].rearrange("p (a b) -> p a b", a=a)
        P.dma_in("sync", lambda e: e.dma_start(out=v, in_=src_view), ("ws", s))
        return v, ("ws", s)

    def wload_f32(src_view, a, b):
        s = wslot()
        v = WS[s][:].bitcast(F32)[:, 0:a * b].rearrange("p (a b) -> p a b", a=a)
        P.dma_in("sync", lambda e: e.dma_start(out=v, in_=src_view), ("ws", s))
        return v, ("ws", s)

    def kview(w2d):
        return w2d.rearrange("(k p) n -> p k n", p=128)

    def V_(fn, reads, writes):
        return P.op("vector", fn, reads, writes)

    def A_(fn, reads, writes):
        return P.op("scalar", fn, reads, writes)

    def T_(fn, reads, writes):
        return P.op("tensor", fn, reads, writes)

    def G_(fn, reads, writes):
        return P.op("gpsimd", fn, reads, writes)

    def mm(out, lhsT, rhs, start, stop, reads, wtok):
        T_(lambda e: e.matmul(out, lhsT, rhs, start=start, stop=stop), reads, [wtok])

    P.dma_in("sync", lambda e: e.dma_start(out=ident[:], in_=ident_d), "ident")
    P.dma_in("sync", lambda e: e.dma_start(out=rotT[:], in_=rotT_d), "rotT")
    P.dma_in("sync", lambda e: e.dma_start(out=maskb[:], in_=maskb_d), "maskb")
    P.dma_in("sync", lambda e: e.dma_start(out=cst[:], in_=cst_d), "cst")
    V_(lambda e: e.memset(onesb[:], 1.0), [], ["onesb"])
    V_(lambda e: e.memset(onesf[:], 1.0), [], ["onesf"])
    V_(lambda e: e.tensor_copy(out=identb[:], in_=ident[:]), ["ident"], ["identb"])
    pvs = TM[0][:, 0:512].rearrange("p (j d) -> p j d", j=4)
    P.dma_in("sync", lambda e: e.dma_start(out=pvs, in_=pvec_d.rearrange("(j p) d -> p j d", p=128)), "TM0")
    for j in range(4):
        T_(lambda e, j=j: e.transpose(PB[0][:, j * 128:(j + 1) * 128], pvs[:, j, :], ident[:]), ["TM0", "ident"], [pb(0)])
    V_(lambda e: e.tensor_copy(out=PV[:], in_=PB[0][:]), [pb(0)], ["PV"])
    A_(lambda e: e.activation(out=scT[:], in_=PV[:, R_CVEC:R_CVEC + 8], func=ACT.Silu), ["PV"], ["scT"])

    BIGflat = BIG[:].rearrange("p a b -> p (a b)")
    BIGF = BIGflat.bitcast(F32)
    XS = BIGF[:, 0:8192].rearrange("p (j d) -> p j d", j=8)
    for j in range(8):
        P.dma_in("sync", lambda e, j=j: e.dma_start(out=XS[:, j, :], in_=x_d[j * 128:(j + 1) * 128, :]), bg(2 * j))
    nb = 0
    for c in range(8):
        for tt in range(2):
            b = nb % 4
            nb += 1
            for j4 in range(4):
                j = tt * 4 + j4
                T_(lambda e, b=b, j4=j4, j=j, c=c: e.transpose(PB[b][:, j4 * 128:(j4 + 1) * 128], XS[:, j, c * 128:(c + 1) * 128], ident[:]),
                   [bg(2 * j), bg(2 * j + 1), "ident"], [pb(b)])
            if (c + tt) % 2 == 0:
                V_(lambda e, b=b, c=c, tt=tt: e.tensor_copy(out=Y[:, c, tt * 512:(tt + 1) * 512], in_=PB[b][:]), [pb(b)], [Yt(c, tt)])
            else:
                A_(lambda e, b=b, c=c, tt=tt: e.activation(out=Y[:, c, tt * 512:(tt + 1) * 512], in_=PB[b][:], func=ACT.Copy), [pb(b)], [Yt(c, tt)])

    def mod_tasks(i):
        par = i % 2
        tasks = []

        def blk_task(blk):
            def run():
                v, tk = wload_cast(kview(mod_w[i])[:, :, blk * 512:(blk + 1) * 512], 8, 512)
                for n4 in range(4):
                    n = blk * 4 + n4
                    for k in range(8):
                        mm(PB[7][:, n:n + 1], v[:, k, n4 * 128:(n4 + 1) * 128], scT[:, k:k + 1], k == 0, k == 7, [tk, "scT"], pb(7))
                if blk == 11:
                    V_(lambda e: e.tensor_tensor(out=MODA[par][:], in0=PB[7][:, 0:48], in1=PV[:, R_MODB + i * 48:R_MODB + (i + 1) * 48], op=ALU.add),
                       [pb(7), "PV"], [("MODA", par)])
                    V_(lambda e: e.scalar_tensor_tensor(out=AG[par][:, 0:8], in0=MODA[par][:, 8:16], scalar=1.0, in1=PV[:, R_NMIX + i * 8:R_NMIX + i * 8 + 8], op0=ALU.add, op1=ALU.mult),
                       [("MODA", par), "PV"], [("AG", par)])
                    V_(lambda e: e.scalar_tensor_tensor(out=AG[par][:, 8:16], in0=MODA[par][:, 32:40], scalar=1.0, in1=PV[:, R_NFFN + i * 8:R_NFFN + i * 8 + 8], op0=ALU.add, op1=ALU.mult),
                       [("MODA", par), "PV"], [("AG", par)])
            return run
        for blk in range(12):
            tasks.append(blk_task(blk))
        return tasks

    ms_state = {"n": 0}

    def mod_tasks_small(i):
        par = i % 2
        tasks = []

        def mk(n):
            def run():
                sl = ms_state["n"] % 2
                ms_state["n"] += 1
                v = MS[sl]
                tk = ("ms", sl)
                P.dma_in("gpsimd", lambda e: e.dma_start(out=v[:], in_=kview(mod_w[i])[:, :, n * 128:(n + 1) * 128]), tk)
                for k in range(8):
                    mm(PB[7][:, n:n + 1], v[:, k, :], scT[:, k:k + 1], k == 0, k == 7, [tk, "scT"], pb(7))
                if n == 47:
                    V_(lambda e: e.tensor_tensor(out=MODA[par][:], in0=PB[7][:, 0:48], in1=PV[:, R_MODB + i * 48:R_MODB + (i + 1) * 48], op=ALU.add),
                       [pb(7), "PV"], [("MODA", par)])
                    V_(lambda e: e.scalar_tensor_tensor(out=AG[par][:, 0:8], in0=MODA[par][:, 8:16], scalar=1.0, in1=PV[:, R_NMIX + i * 8:R_NMIX + i * 8 + 8], op0=ALU.add, op1=ALU.mult),
                       [("MODA", par), "PV"], [("AG", par)])
                    V_(lambda e: e.scalar_tensor_tensor(out=AG[par][:, 8:16], in0=MODA[par][:, 32:40], scalar=1.0, in1=PV[:, R_NFFN + i * 8:R_NFFN + i * 8 + 8], op0=ALU.add, op1=ALU.mult),
                       [("MODA", par), "PV"], [("AG", par)])
            return run
        for n in range(48):
            tasks.append(mk(n))
        return tasks

    pend_tasks = []

    def pump(k):
        for _ in range(k):
            if pend_tasks:
                pend_tasks.pop(0)()

    def norm_mod(agcol, shcol, par):
        for tt in range(2):
            b = 6
            for c in range(8):
                s = SQ[c % 3]
                A_(lambda e, s=s, c=c, tt=tt: e.activation(out=s[:], in_=Y[:, c, tt * 512:(tt + 1) * 512], func=ACT.Square), [Yt(c, tt)], [("SQ", c % 3)])
                mm(PB[b][:], onesb[:], s[:], c == 0, c == 7, [("SQ", c % 3), "onesb"], pb(b))
            A_(lambda e: e.activation(out=LN[:], in_=PB[b][:], func=ACT.Ln, scale=1.0 / 1024, bias=cst[:, 1:2]), [pb(b), "cst"], ["LN"])
            A_(lambda e, tt=tt: e.activation(out=RS[tt][:], in_=LN[:], func=ACT.Exp, scale=-0.5), ["LN"], [("RS", tt)])
            for c in range(8):
                tm = TM[2 + c % 2]
                V_(lambda e, tm=tm, c=c, tt=tt: e.scalar_tensor_tensor(out=tm[:, 0:512], in0=Y[:, c, tt * 512:(tt + 1) * 512], scalar=AG[par][:, agcol + c:agcol + c + 1],
                                                                     in1=RS[tt][:], op0=ALU.mult, op1=ALU.mult),
                   [Yt(c, tt), ("AG", par), ("RS", tt)], [f"TM{2 + c % 2}"])
                A_(lambda e, tm=tm, c=c, tt=tt: e.activation(out=H[:, c, tt * 512:(tt + 1) * 512], in_=tm[:, 0:512], func=ACT.Identity,
                                                             bias=MODA[par][:, shcol + c:shcol + c + 1], scale=1.0),
                   [f"TM{2 + c % 2}", ("MODA", par)], [Ht(c, tt)])

    def out_proj(w2d, src_view, src_toks, gcol, par):
        vs = [wload_cast(kview(w2d)[:, :, half * 512:(half + 1) * 512], 8, 512) for half in range(2)]
        nb = 0
        for tt in range(2):
            for n in range(8):
                v, tk = vs[n // 4]
                n4 = n % 4
                b = nb % 4
                nb += 1
                for k in range(8):
                    mm(PB[b][:], v[:, k, n4 * 128:(n4 + 1) * 128], src_view(k, tt), k == 0, k == 7, [tk, src_toks(k, tt)], pb(b))
                V_(lambda e, b=b, n=n, tt=tt: e.scalar_tensor_tensor(out=Y[:, n, tt * 512:(tt + 1) * 512], in0=PB[b][:], scalar=MODA[par][:, gcol + n:gcol + n + 1],
                                                                     in1=Y[:, n, tt * 512:(tt + 1) * 512], op0=ALU.mult, op1=ALU.add),
                   [pb(b), ("MODA", par), Yt(n, tt)], [Yt(n, tt)])

    def ffn(i, par, inter_tasks):
        groups = [(0, 512), (512, 512), (1024, 512), (1536, 512), (2048, 512), (2560, 256)]
        pre = {0: (wload_cast(kview(ffn_wg[i])[:, :, 0:512], 8, 512), wload_cast(kview(ffn_wu[i])[:, :, 0:512], 8, 512))}
        norm_mod(8, 24, par)
        Abuf = BIG
        nb = 0
        for (c0, cw) in groups:
            if c0 in pre:
                (vg, tg), (vu, tu) = pre[c0]
            else:
                vg, tg = wload_cast(kview(ffn_wg[i])[:, :, c0:c0 + cw], 8, cw)
                vu, tu = wload_cast(kview(ffn_wu[i])[:, :, c0:c0 + cw], 8, cw)
            for tt in range(2):
                for f4 in range(cw // 128):
                    f = c0 // 128 + f4
                    bgt = nb % 2
                    but = 2 + nb % 2
                    nb += 1
                    for k in range(8):
                        mm(PB[bgt][:], vg[:, k, f4 * 128:(f4 + 1) * 128], H[:, k, tt * 512:(tt + 1) * 512], k == 0, k == 7, [tg, Ht(k, tt)], pb(bgt))
                    for k in range(8):
                        mm(PB[but][:], vu[:, k, f4 * 128:(f4 + 1) * 128], H[:, k, tt * 512:(tt + 1) * 512], k == 0, k == 7, [tu, Ht(k, tt)], pb(but))
                    sg = TM[2 + nb % 2]
                    A_(lambda e, sg=sg, bgt=bgt: e.activation(out=sg[:, 0:512], in_=PB[bgt][:], func=ACT.Silu), [pb(bgt)], [f"TM{2 + nb % 2}"])
                    V_(lambda e, sg=sg, but=but, f=f, tt=tt: e.tensor_tensor(out=Abuf[:, f, tt * 512:(tt + 1) * 512], in0=PB[but][:], in1=sg[:, 0:512], op=ALU.mult),
                       [pb(but), f"TM{2 + nb % 2}"], [bg(f)])
        nb = 0

        def down_one(v, tk, n, tt):
            nonlocal nb
            b = 4 + nb % 2
            nb += 1
            for f in range(NFF):
                mm(PB[b][:], v[:, f, :], Abuf[:, f, tt * 512:(tt + 1) * 512], f == 0, f == NFF - 1, [tk, bg(f)], pb(b))
            V_(lambda e: e.scalar_tensor_tensor(out=Y[:, n, tt * 512:(tt + 1) * 512], in0=PB[b][:], scalar=MODA[par][:, 40 + n:41 + n],
                                                in1=Y[:, n, tt * 512:(tt + 1) * 512], op0=ALU.mult, op1=ALU.add),
               [pb(b), ("MODA", par), Yt(n, tt)], [Yt(n, tt)])

        for n in range(4):
            v, tk = wload_cast(kview(ffn_wd[i])[:, :, n * 128:(n + 1) * 128], 22, 128)
            for tt in range(2):
                down_one(v, tk, n, tt)
        vs = [wload_cast(kview(ffn_wd[i])[:, :, n * 128:(n + 1) * 128], 22, 128) for n in range(4, 8)]
        for tt in range(2):
            for n in range(4, 8):
                down_one(vs[n - 4][0], vs[n - 4][1], n, tt)

    def filter_gen(j, inter):
        ft = TM[0]
        P.dma_in("sync", lambda e: e.dma_start(out=ft[0:33, 0:1024], in_=featsT_d), "TM0")
        w1s = TM[3]
        P.dma_in("sync", lambda e: e.dma_start(out=w1s[0:33, 0:64], in_=f_w1[j]), "TM3")
        P.dma_in("sync", lambda e: e.dma_start(out=w1s[0:64, 64:128], in_=f_w2[j]), "TM3")
        P.dma_in("gpsimd", lambda e: e.dma_start(out=BIASB[:], in_=hy_bias[j:j + 1, :].partition_broadcast(128)), "BIASB")
        w3b = BIGflat[:, 32 * 1024:34 * 1024]
        w3t = bg(32)
        P.dma_in("gpsimd", lambda e: e.dma_start(out=w3b[0:64, :], in_=f_w3[j]), w3t)

        def sin_layer(src, lhsT, K, bcol, fcol, dst, srctok, dsttok):
            for tt in range(2):
                b = tt
                mm(PB[b][0:64, :], lhsT, src[0:K, tt * 512:(tt + 1) * 512], True, True, ["TM3", srctok], pb(b))
                V_(lambda e, b=b, tt=tt: e.tensor_scalar(out=dst[0:64, tt * 512:(tt + 1) * 512], in0=PB[b][0:64, :], scalar1=PV[0:64, bcol:bcol + 1], scalar2=PV[0:64, fcol:fcol + 1],
                                                         op0=ALU.add, op1=ALU.mult), [pb(b), "PV"], [dsttok])
            A_(lambda e: e.activation(out=dst[0:64, 0:1024], in_=dst[0:64, 0:1024], func=ACT.Sin, scale=1.0 / 3.0), [dsttok], [dsttok])
            s2 = TM[2]
            V_(lambda e: e.tensor_tensor(out=s2[0:64, 0:1024], in0=dst[0:64, 0:1024], in1=dst[0:64, 0:1024], op=ALU.mult), [dsttok], ["TM2"])
            V_(lambda e: e.tensor_scalar(out=s2[0:64, 0:1024], in0=s2[0:64, 0:1024], scalar1=-4.0, scalar2=3.0, op0=ALU.mult, op1=ALU.add), ["TM2"], ["TM2"])
            V_(lambda e: e.tensor_tensor(out=dst[0:64, 0:1024], in0=dst[0:64, 0:1024], in1=s2[0:64, 0:1024], op=ALU.mult), [dsttok, "TM2"], [dsttok])

        h1 = TM[1]
        sin_layer(ft, w1s[0:33, 0:64], 33, R_FB1 + j, R_FREQ + 2 * j, h1, "TM0", "TM1")
        h2 = TM[0]
        sin_layer(h1, w1s[0:64, 64:128], 64, R_FB2 + j, R_FREQ + 2 * j + 1, h2, "TM1", "TM0")
        for tt in range(2):
            V_(lambda e, tt=tt: e.tensor_copy(out=EB[tt][0:64, :], in_=h2[0:64, tt * 512:(tt + 1) * 512]), ["TM0"], [("EB", tt)])
        sumw = UT
        P.dma_in("sync", lambda e: e.dma_start(out=sumw[:], in_=sumw_d), "UT")
        wins = {}
        fbuf = [(TM[2][:, 0:512], "TM2", TM[3][:, 0:512], "TM3", SQ[0], ("SQ", 0), SQ[1], ("SQ", 1)),
                (ZV[:, 0:512], "ZV", ZV[:, 512:1024], "ZV", SQ[2], ("SQ", 2), EB[2], ("EB", 2))]

        def fA(it):
            tc, ct = it // 2, it % 2
            if ct == 0:
                wins[tc] = wload_f32(win_d[tc * 128:(tc + 1) * 128, :, :], 2, 1024)
            wv, wt = wins[tc]
            hf, hft, hb, hbt, sq0, sq0t, sq1, sq1t = fbuf[it % 2]
            bf_, bb_ = 2 * (it % 2), 2 * (it % 2) + 1
            h2b = EB[tc // 4][0:64, (tc % 4) * 128:(tc % 4 + 1) * 128]
            mm(PB[bf_][:], h2b, w3b[0:64, ct * 512:(ct + 1) * 512], True, True, [("EB", tc // 4), w3t, bg(33)], pb(bf_))
            mm(PB[bb_][:], h2b, w3b[0:64, 1024 + ct * 512:1024 + (ct + 1) * 512], True, True, [("EB", tc // 4), w3t, bg(33)], pb(bb_))
            V_(lambda e: e.tensor_tensor(out=hf, in0=PB[bf_][:], in1=wv[:, 0, ct * 512:(ct + 1) * 512], op=ALU.mult), [pb(bf_), wt], [hft])
            V_(lambda e: e.tensor_tensor(out=hb, in0=PB[bb_][:], in1=wv[:, 1, ct * 512:(ct + 1) * 512], op=ALU.mult), [pb(bb_), wt], [hbt])
            A_(lambda e: e.activation(out=sq0[:], in_=hf, func=ACT.Square), [hft], [sq0t])
            A_(lambda e: e.activation(out=sq1[:], in_=hb, func=ACT.Square), [hbt], [sq1t])
            V_(lambda e: e.tensor_tensor(out=BIG[:, 16 + tc, ct * 512:(ct + 1) * 512], in0=hf, in1=hb, op=ALU.add), [hft, hbt], [bg(16 + tc)])
            V_(lambda e: e.tensor_tensor(out=BIG[:, 24 + tc, ct * 512:(ct + 1) * 512], in0=hb, in1=hf, op=ALU.subtract), [hft, hbt], [bg(24 + tc)])

        def fB(it):
            tc, ct = it // 2, it % 2
            hf, hft, hb, hbt, sq0, sq0t, sq1, sq1t = fbuf[it % 2]
            bn = 4 + ct
            mm(PB[bn][:], sumw[:, tc * 128:(tc + 1) * 128], sq0[:], tc == 0, False, ["UT", sq0t], pb(bn))
            mm(PB[bn][:], sumw[:, tc * 128:(tc + 1) * 128], sq1[:], False, tc == 7, ["UT", sq1t], pb(bn))

        for it in range(17):
            if it < 16:
                fA(it)
            if it >= 1:
                fB(it - 1)
            if it % 2 == 1 and inter:
                inter.pop(0)()
        RN = TM[1]
        for ct in range(2):
            A_(lambda e, ct=ct: e.activation(out=LN[:], in_=PB[4 + ct][:], func=ACT.Ln, scale=1.0, bias=cst[:, 1:2]), [pb(4 + ct), "cst"], ["LN"])
            A_(lambda e, ct=ct: e.activation(out=RN[:, ct * 512:(ct + 1) * 512], in_=LN[:], func=ACT.Exp, scale=-0.5), ["LN"], ["TM1"])

        while inter:
            inter.pop(0)()

    def hyena(i, j, par, prefilt=False):
        X0 = BIG
        nslot = {}

        def get_w(n):
            s = n // 4
            if s not in nslot:
                nslot[s] = wload_cast(kview(hy_w_in[j])[:, :, s * 512:(s + 1) * 512], 8, 512)
            return nslot[s]
        norm_mod(0, 0, par)
        if not prefilt:
            filter_gen(j, [])
        RN = TM[1]
        wv3 = []
        for s in range(6):
            wv3.append(None)
        PSB = [TM[2], TM[3]]
        X1 = TM[0]
        npb = 0
        def conv_chunk(n, dst_fn, dst_tok, idx):
            v, tk = get_w(n)
            n4 = n % 4
            psb = PSB[idx % 2]
            ptok = f"TM{2 + idx % 2}"
            pv = psb[:, 0:1032].rearrange("p (b w) -> p b w", b=4)
            for tt in range(2):
                b = idx % 2 * 2 + tt
                for k in range(8):
                    mm(PB[b][:], v[:, k, n4 * 128:(n4 + 1) * 128], H[:, k, tt * 512:(tt + 1) * 512], k == 0, k == 7, [tk, Ht(k, tt)], pb(b))
                A_(lambda e, b=b, tt=tt: e.activation(out=pv[:, 2 * tt:2 * tt + 2, 1:257], in_=PB[b][:].rearrange("p (b w) -> p b w", b=2), func=ACT.Copy), [pb(b)], [ptok])
                A_(lambda e, b=b, tt=tt: e.activation(out=RSB[:, tt * 512:(tt + 1) * 512].rearrange("p (b w) -> p b w", b=2), in_=PB[b][:].rearrange("p (b w) -> p b w", b=2),
                                                     func=ACT.Identity, scale=PV[:, R_CONVW + j * 72 + 24 + n:R_CONVW + j * 72 + 25 + n],
                                                     bias=PV[:, R_CONVB + j * 24 + n:R_CONVB + j * 24 + n + 1]), [pb(b), "PV"], [("RS", tt)])
            V_(lambda e: e.memset(pv[:, 0:1, 0:1], 0.0), [], [ptok])
            V_(lambda e: e.memset(pv[:, 3:4, 257:258], 0.0), [], [ptok])
            V_(lambda e: e.tensor_scalar(out=pv[:, 1:4, 0:1], in0=pv[:, 0:3, 256:257], scalar1=cst[:, 0:1], scalar2=None, op0=ALU.mult), [ptok, "cst"], [ptok])
            V_(lambda e: e.tensor_scalar(out=pv[:, 0:3, 257:258], in0=pv[:, 1:4, 1:2], scalar1=cst[:, 0:1], scalar2=None, op0=ALU.mult), [ptok, "cst"], [ptok])
            cw = R_CONVW + j * 72
            acc = TM[1] if False else None
            dst = dst_fn()
            accv = ACC[:, 0:1024].rearrange("p (b w) -> p b w", b=4)
            V_(lambda e: e.scalar_tensor_tensor(out=accv, in0=pv[:, :, 0:256], scalar=PV[:, cw + n:cw + n + 1], in1=accv, op0=ALU.mult, op1=ALU.add), [ptok, "PV", ("RS", 0), ("RS", 1)], [("RS", 0), ("RS", 1)])
            V_(lambda e: e.scalar_tensor_tensor(out=dst, in0=pv[:, :, 2:258], scalar=PV[:, cw + 48 + n:cw + 49 + n], in1=accv, op0=ALU.mult, op1=ALU.add), [ptok, "PV", ("RS", 0), ("RS", 1)], dst_tok)
            pump(1)

        ACC = RSB
        idx = 0
        pendT = None

        def mkT(c):
            def run():
                bT = 4 + c % 2
                pbf = PB[bT][:].bitcast(BF16)
                for tcn in range(8):
                    T_(lambda e, tcn=tcn: e.transpose(pbf[:, tcn * 128:(tcn + 1) * 128], UT[:, tcn * 128:(tcn + 1) * 128], identb[:]), ["UT", "identb"], [pb(bT)])
                A_(lambda e: e.activation(out=BIG[:, 8:16, c * 128:(c + 1) * 128], in_=pbf.rearrange("p (j w) -> p j w", j=8), func=ACT.Copy),
                   [pb(bT)], [bg(8 + q) for q in range(8)])
            return run

        for c in range(8):
            conv_chunk(c, lambda c=c: X0[:, c, :].rearrange("p (b w) -> p b w", b=4), [bg(c)], idx); idx += 1
            if pendT is not None:
                pendT()
                pendT = None
            conv_chunk(8 + c, lambda: X1[:, 0:1024].rearrange("p (b w) -> p b w", b=4), ["TM0"], idx); idx += 1
            conv_chunk(16 + c, lambda: ZV[:, 0:1024].rearrange("p (b w) -> p b w", b=4), ["ZV"], idx); idx += 1
            V_(lambda e: e.tensor_tensor(out=UT[:], in0=ZV[:, 0:1024], in1=X1[:, 0:1024], op=ALU.mult), ["ZV", "TM0"], ["UT"])
            pendT = mkT(c)
        pendT()

        YR = H
        cslots = {}

        def get_ct(fc):
            s = fc // 4
            if s not in cslots:
                a = wload_bf(kview(CT_d)[:, :, s * 512:(s + 1) * 512], 8, 512)
                b_ = wload_bf(kview(ST_d)[:, :, s * 512:(s + 1) * 512], 8, 512)
                cslots[s] = (a, b_)
            return cslots[s]

        it = 0
        for fc in range(8):
            (cv, ctk), (sv, stk) = get_ct(fc)
            f4 = fc % 4
            for ct in range(2):
                bA, bB, bP, bQ = (0, 1, 2, 3) if it % 2 == 0 else (4, 5, 6, 3)
                it += 1
                cs = slice(ct * 512, (ct + 1) * 512)
                for m in range(8):
                    lc = cv[:, m, f4 * 128:(f4 + 1) * 128]
                    ls = sv[:, m, f4 * 128:(f4 + 1) * 128]
                    mm(PB[bA][:], lc, BIG[:, 8 + m, cs], m == 0, m == 7, [ctk, bg(8 + m)], pb(bA))
                    mm(PB[bP][:], lc, BIG[:, 16 + m, cs], m == 0, m == 7, [ctk, bg(16 + m)], pb(bP))
                    mm(PB[bB][:], ls, BIG[:, 8 + m, cs], m == 0, m == 7, [stk, bg(8 + m)], pb(bB))
                    mm(PB[bQ][:], ls, BIG[:, 24 + m, cs], m == 0, m == 7, [stk, bg(24 + m)], pb(bQ))
                Pn = TM[2]
                Qn = TM[3]
                V_(lambda e, bP=bP, cs=cs: e.tensor_tensor(out=Pn[:, 0:512], in0=PB[bP][:], in1=RN[:, cs], op=ALU.mult), [pb(bP), "TM1"], ["TM2"])
                V_(lambda e, cs=cs: e.tensor_tensor(out=Pn[:, 0:512], in0=Pn[:, 0:512], in1=BIASB[:, cs], op=ALU.add), ["TM2", "BIASB"], ["TM2"])
                V_(lambda e, bQ=bQ, cs=cs: e.tensor_tensor(out=Qn[:, 0:512], in0=PB[bQ][:], in1=RN[:, cs], op=ALU.mult), [pb(bQ), "TM1"], ["TM3"])
                t1 = TM[0]
                V_(lambda e, bA=bA: e.tensor_tensor(out=t1[:, 0:512], in0=PB[bA][:], in1=Pn[:, 0:512], op=ALU.mult), [pb(bA), "TM2"], ["TM0"])
                V_(lambda e, bB=bB: e.tensor_tensor(out=t1[:, 512:1024], in0=PB[bB][:], in1=Qn[:, 0:512], op=ALU.mult), [pb(bB), "TM3"], ["TM0"])
                V_(lambda e, fc=fc, cs=cs: e.tensor_tensor(out=YR[:, fc, cs], in0=t1[:, 0:512], in1=t1[:, 512:1024], op=ALU.add), ["TM0"], [Ht(fc, ct)])
                t2 = ZV
                V_(lambda e, bB=bB: e.tensor_tensor(out=t2[:, 0:512], in0=PB[bB][:], in1=Pn[:, 0:512], op=ALU.mult), [pb(bB), "TM2"], ["ZV"])
                V_(lambda e, bA=bA: e.tensor_tensor(out=t2[:, 512:1024], in0=PB[bA][:], in1=Qn[:, 0:512], op=ALU.mult), [pb(bA), "TM3"], ["ZV"])
                V_(lambda e, fc=fc, cs=cs: e.tensor_tensor(out=BIG[:, 32 + fc, cs], in0=t2[:, 0:512], in1=t2[:, 512:1024], op=ALU.subtract), ["ZV"], [bg(32 + fc)])
                pump(2)

        nb = 0
        for tt in range(2):
            (civ, citk) = wload_bf(kview(CI_d)[:, :, tt * 512:(tt + 1) * 512], 8, 512)
            (siv, sitk) = wload_bf(kview(SI_d)[:, :, tt * 512:(tt + 1) * 512], 8, 512)
            for c in range(8):
                b = nb % 4
                nb += 1
                for fc in range(8):
                    mm(PB[b][:], YR[:, fc, c * 128:(c + 1) * 128], civ[:, fc, :], fc == 0, False, [citk, Ht(fc, c // 4)], pb(b))
                    mm(PB[b][:], BIG[:, 32 + fc, c * 128:(c + 1) * 128], siv[:, fc, :], False, fc == 7, [sitk, bg(32 + fc)], pb(b))
                V_(lambda e, b=b, c=c, tt=tt: e.tensor_tensor(out=BIG[:, 8 + c, tt * 512:(tt + 1) * 512], in0=PB[b][:], in1=X0[:, c, tt * 512:(tt + 1) * 512], op=ALU.mult),
                   [pb(b), bg(c)], [bg(8 + c)])
        out_proj(hy_w_out[j], lambda k, tt: BIG[:, 8 + k, tt * 512:(tt + 1) * 512], lambda k, tt: bg(8 + k), 16, par)

    def attention(i, j, par):
        wq = [wload_cast(kview(at_w_qkv[j])[:, :, s * 512:(s + 1) * 512], 8, 512) for s in range(3)]
        norm_mod(0, 0, par)
        QT = BIG
        KT = BIGflat[:, 16 * 1024:19 * 1024].rearrange("p (k s) -> p k s", k=2)
        VV = BIGflat[:, 19 * 1024:22 * 1024].rearrange("p (k e) -> p k e", k=12)
        ktoks = [bg(16), bg(17), bg(18)]
        vtoks = [bg(19), bg(20), bg(21)]
        CS = BIGF[:, 24 * 512:28 * 512]
        P.dma_in("sync", lambda e: e.dma_start(out=CS[:, 0:1024], in_=cosF_d), bg(24))
        P.dma_in("sync", lambda e: e.dma_start(out=CS[:, 1024:2048], in_=sinF_d), bg(26))
        cstoks = [bg(24), bg(26)]
        NK = BIGF[:, 28 * 512:32 * 512].rearrange("p (t e) -> p t e", t=8)
        NV = BIGF[:, 32 * 512:36 * 512].rearrange("p (t e) -> p t e", t=8)
        nktoks = [bg(28 + q) for q in range(4)]
        nvtoks = [bg(32 + q) for q in range(4)]
        ckl = TM[3][:, 0:1024].rearrange("p (c e) -> p c e", c=4)
        P.dma_in("sync", lambda e: e.dma_start(out=ckl, in_=cache_k[j].rearrange("(c p) e -> p c e", p=128)), "TM3")
        P.dma_in("gpsimd", lambda e: e.dma_start(out=VV[:, 8:12, :], in_=cache_v[j].rearrange("(c p) e -> p c e", p=128)), bg(21))
        for kv in range(2):
            b = kv
            for cc in range(4):
                T_(lambda e, b=b, cc=cc, kv=kv: e.transpose(PB[b][:, cc * 128:(cc + 1) * 128], ckl[:, cc, kv * 128:(kv + 1) * 128], ident[:]), ["TM3", "ident"], [pb(b)])
            V_(lambda e, b=b, kv=kv: e.tensor_copy(out=KT[:, kv, 1024:1536], in_=PB[b][:]), [pb(b)], ktoks)
        items = [(tt, n) for tt in range(2) for n in range(10)]
        NI = len(items)

        def stageA(it):
            tt, n = items[it]
            v, tk = wq[n // 4]
            n4 = n % 4
            b = it % 3
            for k in range(8):
                mm(PB[b][:], v[:, k, n4 * 128:(n4 + 1) * 128], H[:, k, tt * 512:(tt + 1) * 512], k == 0, k == 7, [tk, Ht(k, tt)], pb(b))
            s = SQ[it % 3]
            stok = ("SQ", it % 3)
            A_(lambda e: e.activation(out=s[:], in_=PB[b][:], func=ACT.Square), [pb(b)], [stok])

        def stageB(it):
            tt, n = items[it]
            gcol = (R_QN + j) if n < 8 else (R_KN + j)
            b = it % 3
            bs = 3 + it % 2
            s = SQ[it % 3]
            stok = ("SQ", it % 3)
            mm(PB[bs][:], onesb[:], s[:], True, True, [stok, "onesb"], pb(bs))
            A_(lambda e: e.activation(out=LN[:], in_=PB[bs][:], func=ACT.Ln, scale=1.0 / 128, bias=cst[:, 1:2]), [pb(bs), "cst"], ["LN"])
            rs = RS[it % 2]
            A_(lambda e: e.activation(out=rs[:], in_=LN[:], func=ACT.Exp, scale=-0.5), ["LN"], [("RS", it % 2)])
            qn = TM[it % 2]
            qtok = f"TM{it % 2}"
            V_(lambda e: e.scalar_tensor_tensor(out=qn[:, 0:512], in0=PB[b][:], scalar=PV[:, gcol:gcol + 1], in1=rs[:], op0=ALU.mult, op1=ALU.mult),
               [pb(b), "PV", ("RS", it % 2)], [qtok])

        def stageC(it):
            tt, n = items[it]
            br = 5
            qn = TM[it % 2]
            qtok = f"TM{it % 2}"
            mm(PB[br][:], rotT[:], qn[:, 0:512], True, True, ["rotT", qtok], pb(br))
            t1 = ZV
            V_(lambda e: e.tensor_tensor(out=t1[:, 0:512], in0=qn[:, 0:512], in1=CS[:, tt * 512:(tt + 1) * 512], op=ALU.mult), [qtok] + cstoks, ["ZV"])
            V_(lambda e: e.tensor_tensor(out=t1[:, 512:1024], in0=PB[br][:], in1=CS[:, 1024 + tt * 512:1024 + (tt + 1) * 512], op=ALU.mult), [pb(br)] + cstoks, ["ZV"])
            if n < 8:
                V_(lambda e: e.tensor_tensor(out=QT[:, n, tt * 512:(tt + 1) * 512], in0=t1[:, 0:512], in1=t1[:, 512:1024], op=ALU.add), ["ZV"], [bg(n)])
            else:
                kv = n - 8
                V_(lambda e: e.tensor_tensor(out=KT[:, kv, tt * 512:(tt + 1) * 512], in0=t1[:, 0:512], in1=t1[:, 512:1024], op=ALU.add), ["ZV"], ktoks)
                bt = 6
                for q in range(4):
                    T_(lambda e, q=q: e.transpose(PB[bt][:, q * 128:(q + 1) * 128], qn[:, q * 128:(q + 1) * 128], ident[:]), [qtok, "ident"], [pb(bt)])
                A_(lambda e: e.activation(out=NK[:, tt * 4:(tt + 1) * 4, kv * 128:(kv + 1) * 128], in_=PB[bt][:].rearrange("p (q d) -> p q d", q=4), func=ACT.Copy),
                   [pb(bt)], nktoks)

        for step in range(NI + 2):
            if step < NI:
                stageA(step)
            if 0 <= step - 1 < NI:
                stageB(step - 1)
            if 0 <= step - 2 < NI:
                stageC(step - 2)
            pump(1)
        v2, tk2 = wq[2]
        for tcn in range(8):
            b = 6 if tcn % 2 == 0 else 4
            for k in range(8):
                mm(PB[b][:, 0:256], H[:, k, tcn * 128:(tcn + 1) * 128], v2[:, k, 256:512], k == 0, k == 7, [tk2, Ht(k, tcn // 4)], pb(b))
            A_(lambda e, b=b, tcn=tcn: e.activation(out=NV[:, tcn, :], in_=PB[b][:, 0:256], func=ACT.Copy), [pb(b)], nvtoks)
            V_(lambda e, b=b, tcn=tcn: e.tensor_copy(out=VV[:, tcn, :], in_=NV[:, tcn, :]), nvtoks, vtoks)
        jj = j
        P.dma_out("sync", lambda e: e.dma_start(out=nk_d[jj].rearrange("(t p) e -> p t e", p=128), in_=NK), nktoks[0])
        P.dma_out("sync", lambda e: e.dma_start(out=nv_d[jj].rearrange("(t p) e -> p t e", p=128), in_=NV), nvtoks[0])
        for tks, oidx in ((nktoks, -2), (nvtoks, -1)):
            o = P.ops[oidx]
            for t in tks[1:]:
                tt_ = P.tok(t)
                if tt_.w is not None:
                    o.deps.add(tt_.w)
                tt_.readers.append(o)

        sb_i = 0
        acc_i = 0
        for kv in range(2):
            for g in range(4):
                h = kv * 4 + g
                for tt in range(2):
                    bO = 4 + acc_i % 2
                    bS = 3 if acc_i % 2 == 0 else 6
                    acc_i += 1
                    qs = slice(tt * 512, (tt + 1) * 512)
                    pend = []

                    def s_step(kc):
                        nonlocal sb_i
                        b = sb_i % 3
                        eb = EB[sb_i % 4]
                        etok = ("EB", sb_i % 4)
                        sb_i += 1
                        mm(PB[b][:], KT[:, kv, kc * 128:(kc + 1) * 128], QT[:, h, qs], True, True, ktoks + [bg(h)], pb(b))
                        for half in range(2):
                            qb = tt * 2 + half
                            A_(lambda e, b=b, eb=eb, half=half, qb=qb, kc=kc: e.activation(out=eb[:, half * 256:(half + 1) * 256], in_=PB[b][:, half * 256:(half + 1) * 256], func=ACT.Exp,
                                                                                         scale=SM_SCALE, bias=maskb[:, kc * 4 + qb:kc * 4 + qb + 1]), [pb(b), "maskb"], [etok])
                        return (kc, eb, etok)

                    def pv_step(kc, eb, etok):
                        mm(PB[bO][:], VV[:, kc, kv * 128:(kv + 1) * 128], eb[:], kc == 0, kc == 11, vtoks + [etok], pb(bO))
                        mm(PB[bS][:], onesb[:], eb[:], kc == 0, kc == 11, ["onesb", etok], pb(bS))

                    for kc in range(12):
                        pend.append(s_step(kc))
                        if len(pend) > 2:
                            pv_step(*pend.pop(0))
                    while pend:
                        pv_step(*pend.pop(0))
                    rsum = TM[acc_i % 2]
                    rtok = f"TM{acc_i % 2}"
                    V_(lambda e, rsum=rsum, bS=bS: e.reciprocal(out=rsum[:, 0:512], in_=PB[bS][:]), [pb(bS)], [rtok])
                    V_(lambda e, rsum=rsum, bO=bO, h=h, qs=qs: e.tensor_tensor(out=BIG[:, 8 + h, qs], in0=PB[bO][:], in1=rsum[:, 0:512], op=ALU.mult), [pb(bO), rtok], [bg(8 + h)])
                    pump(2)
        out_proj(at_w_out[j], lambda k, tt: BIG[:, 8 + k, tt * 512:(tt + 1) * 512], lambda k, tt: bg(8 + k), 16, par)

    ZV = P.sb("ZVt", [128, 1024], F32)
    if nsteps > 0:
        filter_gen(0, mod_tasks(0))
    else:
        for t in mod_tasks(0):
            t()
    for i in range(NLAYER):
        par = i % 2
        j = i // 2
        need_next = (i + 1 < NLAYER) and (2 * (i + 1) < nsteps)
        if need_next:
            pend_tasks.extend(mod_tasks_small(i + 1))
        if 2 * i < nsteps:
            if i % 2 == 0:
                hyena(i, j, par, prefilt=(i == 0))
            else:
                attention(i, j, par)
        pump(1000)
        if 2 * i + 1 < nsteps:
            ffn(i, par, [])

    OS = BIGF[:, 0:8192].rearrange("p (j d) -> p j d", j=8)
    for tt in range(2):
        b = 6
        for c in range(8):
            s = SQ[c % 3]
            A_(lambda e, s=s, c=c, tt=tt: e.activation(out=s[:], in_=Y[:, c, tt * 512:(tt + 1) * 512], func=ACT.Square), [Yt(c, tt)], [("SQ", c % 3)])
            mm(PB[b][:], onesb[:], s[:], c == 0, c == 7, [("SQ", c % 3), "onesb"], pb(b))
        A_(lambda e: e.activation(out=LN[:], in_=PB[b][:], func=ACT.Ln, scale=1.0 / 1024, bias=cst[:, 1:2]), [pb(b), "cst"], ["LN"])
        A_(lambda e, tt=tt: e.activation(out=RS[tt][:], in_=LN[:], func=ACT.Exp, scale=-0.5), ["LN"], [("RS", tt)])
        for c in range(8):
            V_(lambda e, c=c, tt=tt: e.scalar_tensor_tensor(out=Y[:, c, tt * 512:(tt + 1) * 512], in0=Y[:, c, tt * 512:(tt + 1) * 512], scalar=PV[:, R_FIN + c:R_FIN + c + 1],
                                                            in1=RS[tt][:], op0=ALU.mult, op1=ALU.mult), [Yt(c, tt), "PV", ("RS", tt)], [Yt(c, tt)])
    nb = 0
    for tcn in range(8):
        for half in range(2):
            b = nb % 4
            nb += 1
            for c4 in range(4):
                c = half * 4 + c4
                T_(lambda e, b=b, c4=c4, c=c, tcn=tcn: e.transpose(PB[b][:, c4 * 128:(c4 + 1) * 128], Y[:, c, tcn * 128:(tcn + 1) * 128], ident[:]), [Yt(c, tcn // 4), "ident"], [pb(b)])
            if nb % 2 == 0:
                V_(lambda e, b=b, tcn=tcn, half=half: e.tensor_copy(out=OS[:, tcn, half * 512:(half + 1) * 512], in_=PB[b][:]), [pb(b)], [bg(2 * tcn + half)])
            else:
                A_(lambda e, b=b, tcn=tcn, half=half: e.activation(out=OS[:, tcn, half * 512:(half + 1) * 512], in_=PB[b][:], func=ACT.Copy), [pb(b)], [bg(2 * tcn + half)])
        P.dma_out("sync", lambda e, tcn=tcn: e.dma_start(out=y_d[tcn * 128:(tcn + 1) * 128, :], in_=OS[:, tcn, :]), bg(2 * tcn))
        o = P.ops[-1]
        t2_ = P.tok(bg(2 * tcn + 1))
        if t2_.w is not None:
            o.deps.add(t2_.w)
        t2_.readers.append(o)
    P.emit()
    return nc, P


NSTEPS = 8
_CACHE = {}


def _consts(L, nrep):
    n = 2 * L
    f = np.arange(L, dtype=np.float64)
    m = np.arange(L, dtype=np.float64)
    th = np.pi * (2.0 * f[None, :] + 1.0) * m[:, None] / n
    CTb = np.cos(th)
    STb = np.sin(th)
    CIb = (2.0 / n) * np.cos(th.T)
    SIb = (2.0 / n) * np.sin(th.T)

    def bd(B):
        out = np.zeros((1024, 1024), np.float64)
        for r in range(nrep):
            out[r * L:(r + 1) * L, r * L:(r + 1) * L] = B
        return out.astype(ml_dtypes.bfloat16)
    t = np.arange(L, dtype=np.float32) / np.float32(L)
    bands = np.arange(1, 17, dtype=np.float32)
    ang = (2.0 * np.float32(math.pi)) * t[:, None] * bands[None, :]
    feats = np.concatenate([t[:, None], np.cos(ang), np.sin(ang)], axis=-1).astype(np.float32)
    featsT = np.tile(feats.T, (1, nrep)).astype(np.float32)
    MIN_DECAY = math.log(1e-2) / 1.5
    MAX_DECAY = math.log(1e-2) / 0.3
    deltas = np.abs(np.linspace(MIN_DECAY, MAX_DECAY, 1024, dtype=np.float32))
    win = np.exp(-t[:, None] * deltas[None, :]).astype(np.float32)
    winb = win * (np.arange(L) > 0).astype(np.float32)[:, None]
    w2 = np.stack([win, winb], axis=1)
    w2 = np.tile(w2, (nrep, 1, 1)).astype(np.float32)
    sumw = np.zeros((128, 8, 128), np.float32)
    sumw[:, 0:L // 128, :] = 1.0
    sumw = sumw.reshape(128, 1024).astype(ml_dtypes.bfloat16)
    return dict(CT=bd(CTb), ST=bd(STb), CI=bd(CIb), SI=bd(SIb), featsT=featsT, win=w2, sumw=sumw)


def _rope_tables():
    T = 1024
    GRID_W = 64
    ROWS = T // GRID_W
    row = np.repeat(np.arange(ROWS), GRID_W).astype(np.float32)
    col = np.tile(np.arange(GRID_W), ROWS).astype(np.float32)
    half = 64
    freqs = (np.float32(10000.0) ** (-np.arange(0, half, 2, dtype=np.float32) / np.float32(half))).astype(np.float32)
    ang = np.concatenate([row[:, None] * freqs[None, :], col[:, None] * freqs[None, :]], axis=-1)
    cos = np.cos(ang).astype(np.float32)
    sin = np.sin(ang).astype(np.float32)
    cosF = np.repeat(cos.T, 2, axis=0)
    sinF = np.repeat(sin.T, 2, axis=0)
    return np.ascontiguousarray(cosF), np.ascontiguousarray(sinF)


def _static():
    if "st" in _CACHE:
        return _CACHE["st"]
    ident = np.eye(128, dtype=np.float32)
    rotT = np.zeros((128, 128), np.float32)
    for i in range(64):
        rotT[2 * i + 1, 2 * i] = -1.0
        rotT[2 * i, 2 * i + 1] = 1.0
    cosF, sinF = _rope_tables()
    samp = _consts(1024, 1)
    prm = _consts(256, 4)
    maskb_s = np.zeros((128, 48), np.float32)
    maskb_p = np.full((128, 48), -30000.0, np.float32)
    for kc in range(8):
        maskb_p[:, kc * 4 + kc // 2] = 0.0
    cst_s = np.zeros((128, 8), np.float32)
    cst_s[:, 0] = 1.0
    cst_s[:, 1] = EPS
    cst_p = cst_s.copy()
    cst_p[:, 0] = 0.0
    st = dict(ident=ident, rotT=rotT,
              samp=dict(samp, cosF=cosF, sinF=sinF, maskb=maskb_s, cst=cst_s),
              prm=dict(prm, cosF=np.ones_like(cosF), sinF=np.zeros_like(sinF), maskb=maskb_p, cst=cst_p))
    _CACHE["st"] = st
    return st


def _pvec(cvec, mod_b, norm_mix, norm_ffn, final_norm, hy_conv_w, hy_conv_b, at_q_norm, at_k_norm, hy_f_b1, hy_f_freq, hy_f_b2):
    pv = np.zeros((NROW, 128), np.float32)
    pv[R_CVEC:R_CVEC + 8] = cvec.reshape(8, 128)
    pv[R_MODB:R_MODB + 192] = mod_b.reshape(4 * 48, 128)
    pv[R_NMIX:R_NMIX + 32] = norm_mix.reshape(32, 128)
    pv[R_NFFN:R_NFFN + 32] = norm_ffn.reshape(32, 128)
    pv[R_FIN:R_FIN + 8] = final_norm.reshape(8, 128)
    pv[R_CONVW:R_CONVW + 144] = hy_conv_w.reshape(2 * 3 * 24, 128)
    pv[R_CONVB:R_CONVB + 48] = hy_conv_b.reshape(48, 128)
    pv[R_QN:R_QN + 2] = at_q_norm
    pv[R_KN:R_KN + 2] = at_k_norm
    pv[R_FB1:R_FB1 + 2, 0:64] = hy_f_b1
    pv[R_FREQ:R_FREQ + 4, 0:64] = hy_f_freq.reshape(4, 64)
    pv[R_FB2:R_FB2 + 2, 0:64] = hy_f_b2
    return pv


def kernel(x_prompt, x_sample, cache_k, cache_v, c, c_ctx,
           mod_w, mod_b, norm_mix, norm_ffn,
           hy_w_in, hy_conv_w, hy_conv_b, hy_f_w1, hy_f_b1, hy_f_freq, hy_f_w2, hy_f_b2, hy_f_w3,
           hy_bias, hy_w_out,
           at_w_qkv, at_q_norm, at_k_norm, at_w_out,
           ffn_w_gate, ffn_w_up, ffn_w_down, final_norm, _nsteps=None):
    nsteps = NSTEPS if _nsteps is None else _nsteps
    f = lambda a: np.ascontiguousarray(np.asarray(a, dtype=np.float32))
    x_prompt, x_sample, cache_k, cache_v, c, c_ctx = map(f, (x_prompt, x_sample, cache_k, cache_v, c, c_ctx))
    st = _static()
    key = ("nc", nsteps)
    if key not in _CACHE:
        _CACHE[key] = build(nsteps)[0]
    nc = _CACHE[key]
    shared = dict(mod_w=f(mod_w), hy_w_in=f(hy_w_in), hy_w_out=f(hy_w_out), at_w_qkv=f(at_w_qkv), at_w_out=f(at_w_out),
                  ffn_wg=f(ffn_w_gate), ffn_wu=f(ffn_w_up), ffn_wd=f(ffn_w_down), f_w1=f(hy_f_w1), f_w2=f(hy_f_w2), f_w3=f(hy_f_w3),
                  hy_bias=f(hy_bias), ident=st["ident"], rotT=st["rotT"])
    pargs = [f(a) for a in (mod_b, norm_mix, norm_ffn, final_norm, hy_conv_w, hy_conv_b, at_q_norm, at_k_norm, hy_f_b1, hy_f_freq, hy_f_b2)]
    in_maps = []
    zero_cache = np.zeros((2, 512, 256), np.float32)
    for r in range(8):
        m = dict(shared)
        if r < 4:
            m["x"] = x_prompt[4 * r:4 * r + 4].reshape(1024, 1024)
            cv = c_ctx
            tb = st["prm"]
            m["cache_k"] = zero_cache
            m["cache_v"] = zero_cache
        else:
            b = r - 4
            m["x"] = x_sample[b]
            cv = c[b]
            tb = st["samp"]
            m["cache_k"] = np.ascontiguousarray(cache_k[b].reshape(2, 512, 256))
            m["cache_v"] = np.ascontiguousarray(cache_v[b].reshape(2, 512, 256))
        m["pvec"] = _pvec(cv, *pargs)
        for k in ("cosF", "sinF", "maskb", "cst", "featsT", "win", "sumw", "CT", "ST", "CI", "SI"):
            m[k] = tb[k]
        in_maps.append(m)
    res = run_bass_kernel_spmd(nc, in_maps, core_ids=list(range(8)))
    outs = res.results
    y_prompt = np.stack([outs[r]["y"] for r in range(4)], 0).reshape(16, 256, 1024).astype(np.float32)
    y_sample = np.stack([outs[r]["y"] for r in range(4, 8)], 0).reshape(4, 1024, 1024).astype(np.float32)
    nk = np.stack([outs[r]["new_k"] for r in range(4)], 0).reshape(4, 2, 4, 256, 2, 128).transpose(0, 2, 1, 3, 4, 5).reshape(16, 2, 256, 2, 128)
    nv = np.stack([outs[r]["new_v"] for r in range(4)], 0).reshape(4, 2, 4, 256, 2, 128).transpose(0, 2, 1, 3, 4, 5).reshape(16, 2, 256, 2, 128)
    return (y_prompt, y_sample, np.ascontiguousarray(nk, dtype=np.float32), np.ascontiguousarray(nv, dtype=np.float32))
```

```python
from contextlib import ExitStack
import concourse.bass as bass
import concourse.mybir as mybir

F32 = mybir.dt.float32
BF16 = mybir.dt.bfloat16
ACT = mybir.ActivationFunctionType
ALU = mybir.AluOpType

SAME_ENG_SYNC = True


import types


def _freeze(fn):
    cl = fn.__closure__
    if not cl:
        return fn
    cells = []
    for c in cl:
        try:
            cells.append(types.CellType(c.cell_contents))
        except ValueError:
            cells.append(c)
    g = types.FunctionType(fn.__code__, fn.__globals__, fn.__name__, fn.__defaults__, tuple(cells))
    g.__kwdefaults__ = fn.__kwdefaults__
    return g


class Tok:
    __slots__ = ("name", "w", "readers", "dma_w", "sem", "dma_r", "rsem", "n_dma_w", "n_dma_r")

    def __init__(self, name):
        self.name = name
        self.w = None
        self.readers = []
        self.dma_w = []
        self.sem = None
        self.rsem = None
        self.n_dma_w = 0
        self.n_dma_r = 0


class Op:
    __slots__ = ("eng", "fn", "deps", "dmadeps", "signal", "signo", "is_dma", "dma_sem", "dma_val", "idx")

    def __init__(self, eng, fn):
        self.eng = eng
        self.fn = _freeze(fn)
        self.deps = set()
        self.dmadeps = {}
        self.signal = False
        self.signo = 0
        self.is_dma = False
        self.dma_sem = None
        self.dma_val = 0


class Prog:
    ENGS = ("tensor", "vector", "scalar", "gpsimd", "sync")

    def __init__(self, nc):
        self.nc = nc
        self.es = ExitStack()
        self.ops = []
        self.toks = {}
        self.final_waits = {}
        self.nsem = 0

    def sb(self, name, shape, dt):
        return self.es.enter_context(self.nc.sbuf_tensor(name, shape, dt))

    def ps(self, name, shape, dt):
        return self.es.enter_context(self.nc.psum_tensor(name, shape, dt))

    def newsem(self, name):
        self.nsem += 1
        return self.es.enter_context(self.nc.semaphore(name))

    def tok(self, key):
        t = self.toks.get(key)
        if t is None:
            t = Tok(key)
            self.toks[key] = t
        return t

    def _toks(self, lst):
        out = []
        for k in lst:
            out.append(k if isinstance(k, Tok) else self.tok(k))
        return out

    def op(self, eng, fn, reads=(), writes=()):
        o = Op(eng, fn)
        o.idx = len(self.ops)
        for t in self._toks(reads):
            if t.w is not None:
                o.deps.add(t.w)
            for d in t.dma_w:
                o.dmadeps[d.dma_sem] = max(o.dmadeps.get(d.dma_sem, 0), d.dma_val)
            t.readers = [r for r in t.readers if r.is_dma or r.eng != o.eng] + [o]
        for t in self._toks(writes):
            if t.w is not None:
                o.deps.add(t.w)
            for d in t.dma_w:
                o.dmadeps[d.dma_sem] = max(o.dmadeps.get(d.dma_sem, 0), d.dma_val)
            for r in t.readers:
                if r is o:
                    continue
                if r.is_dma:
                    o.dmadeps[r.dma_sem] = max(o.dmadeps.get(r.dma_sem, 0), r.dma_val)
                else:
                    o.deps.add(r)
            t.w = o
            t.readers = []
            t.dma_w = []
        self.ops.append(o)
        return o

    def dma_in(self, eng, fn, dst, reads=()):
        t = self._toks([dst])[0]
        o = Op(eng, fn)
        o.idx = len(self.ops)
        o.is_dma = True
        if t.sem is None:
            t.sem = {}
        if eng not in t.sem:
            t.sem[eng] = [self.newsem("d_" + eng[0] + str(t.name)), 0]
        if t.w is not None:
            o.deps.add(t.w)
        for r in t.readers:
            if r.is_dma:
                o.dmadeps[r.dma_sem] = max(o.dmadeps.get(r.dma_sem, 0), r.dma_val)
            else:
                o.deps.add(r)
        t.sem[eng][1] += 1
        o.dma_sem = t.sem[eng][0]
        o.dma_val = 16 * t.sem[eng][1]
        t.w = None
        t.readers = []
        t.dma_w = [d for d in t.dma_w if d.dma_sem is not o.dma_sem] + [o]
        self.ops.append(o)
        return o

    def dma_out(self, eng, fn, src):
        t = self._toks([src])[0]
        o = Op(eng, fn)
        o.idx = len(self.ops)
        o.is_dma = True
        if t.rsem is None:
            t.rsem = self.newsem("r_" + str(t.name))
        if t.w is not None:
            o.deps.add(t.w)
        for d in t.dma_w:
            o.dmadeps[d.dma_sem] = max(o.dmadeps.get(d.dma_sem, 0), d.dma_val)
        t.n_dma_r += 1
        o.dma_sem = t.rsem
        o.dma_val = 16 * t.n_dma_r
        t.readers.append(o)
        self.final_waits[t.rsem] = o.dma_val
        self.ops.append(o)
        return o

    def emit(self):
        nc = self.nc
        esem = {e: self.newsem("e_" + e) for e in self.ENGS}
        for o in self.ops:
            for d in o.deps:
                if d.eng == o.eng and (o.eng == "tensor" or not SAME_ENG_SYNC):
                    continue
                d.signal = True
        cnt = {e: 0 for e in self.ENGS}
        per = {e: [] for e in self.ENGS}
        for o in self.ops:
            if o.signal and not o.is_dma:
                cnt[o.eng] += 1
                o.signo = cnt[o.eng]
            per[o.eng].append(o)
        final_waits = self.final_waits
        stats = {e: [0, 0] for e in self.ENGS}

        def run(engname, eng):
            seen = {}
            for o in per[engname]:
                waits = {}
                for d in o.deps:
                    if d.eng == engname and (engname == "tensor" or not SAME_ENG_SYNC):
                        continue
                    s = esem[d.eng]
                    waits[s] = max(waits.get(s, 0), d.signo)
                for s, v in o.dmadeps.items():
                    waits[s] = max(waits.get(s, 0), v)
                for s, v in waits.items():
                    if seen.get(s, 0) >= v:
                        continue
                    eng.wait_ge(s, v)
                    seen[s] = v
                    stats[engname][1] += 1
                ins = o.fn(eng)
                stats[engname][0] += 1
                if o.is_dma:
                    ins.then_inc(o.dma_sem, 16)
                elif o.signal:
                    ins.then_inc(esem[engname], 1)
            if engname == "sync":
                for s, v in final_waits.items():
                    eng.wait_ge(s, v)

        with nc.Block() as block:
            @block.tensor
            def _(e):
                run("tensor", e)

            @block.vector
            def _(e):
                run("vector", e)

            @block.scalar
            def _(e):
                run("scalar", e)

            @block.gpsimd
            def _(e):
                run("gpsimd", e)

            @block.sync
            def _(e):
                run("sync", e)
        self.stats = stats
        self.es.close()


import math
import numpy as np
import ml_dtypes
from concourse.bass_utils import run_bass_kernel_spmd

NLAYER = 4
EPS = 1e-6
D_FF = 2816
NFF = 22
SM_SCALE = 128 ** -0.5

R_CVEC = 0
R_MODB = 8
R_NMIX = 200
R_NFFN = 232
R_FIN = 264
R_CONVW = 272
R_CONVB = 416
R_QN = 464
R_KN = 466
R_FB1 = 468
R_FREQ = 470
R_FB2 = 474
NROW = 512


def build(nsteps=8):
    nc = bass.Bass("TRN2", target_bir_lowering=False)

    def DI(name, shape, dt=F32):
        return nc.dram_tensor(name, shape, dt, kind="ExternalInput").ap()

    x_d = DI("x", [1024, 1024])
    pvec_d = DI("pvec", [NROW, 128])
    mod_w = DI("mod_w", [4, 1024, 6144])
    hy_w_in = DI("hy_w_in", [2, 1024, 3072])
    hy_w_out = DI("hy_w_out", [2, 1024, 1024])
    at_w_qkv = DI("at_w_qkv", [2, 1024, 1536])
    at_w_out = DI("at_w_out", [2, 1024, 1024])
    ffn_wg = DI("ffn_wg", [4, 1024, D_FF])
    ffn_wu = DI("ffn_wu", [4, 1024, D_FF])
    ffn_wd = DI("ffn_wd", [4, D_FF, 1024])
    f_w1 = DI("f_w1", [2, 33, 64])
    f_w2 = DI("f_w2", [2, 64, 64])
    f_w3 = DI("f_w3", [2, 64, 2048])
    hy_bias = DI("hy_bias", [2, 1024])
    cache_k = DI("cache_k", [2, 512, 256])
    cache_v = DI("cache_v", [2, 512, 256])
    ident_d = DI("ident", [128, 128])
    rotT_d = DI("rotT", [128, 128])
    cosF_d = DI("cosF", [128, 1024])
    sinF_d = DI("sinF", [128, 1024])
    maskb_d = DI("maskb", [128, 48])
    cst_d = DI("cst", [128, 8])
    featsT_d = DI("featsT", [33, 1024])
    win_d = DI("win", [1024, 2, 1024])
    sumw_d = DI("sumw", [128, 1024], BF16)
    CT_d = DI("CT", [1024, 1024], BF16)
    ST_d = DI("ST", [1024, 1024], BF16)
    CI_d = DI("CI", [1024, 1024], BF16)
    SI_d = DI("SI", [1024, 1024], BF16)

    y_d = nc.dram_tensor("y", [1024, 1024], F32, kind="ExternalOutput").ap()
    nk_d = nc.dram_tensor("new_k", [2, 1024, 256], F32, kind="ExternalOutput").ap()
    nv_d = nc.dram_tensor("new_v", [2, 1024, 256], F32, kind="ExternalOutput").ap()

    P = Prog(nc)
    Y = P.sb("Y", [128, 8, 1024], F32)
    H = P.sb("H", [128, 8, 1024], BF16)
    BIG = P.sb("BIG", [128, 40, 1024], BF16)
    WS = [P.sb(f"WS{i}", [128, 4096], BF16) for i in range(4)]
    PV = P.sb("PV", [128, 512], F32)
    MODA = [P.sb(f"MODA{i}", [128, 48], F32) for i in range(2)]
    AG = [P.sb(f"AG{i}", [128, 16], F32) for i in range(2)]
    ident = P.sb("identf", [128, 128], F32)
    identb = P.sb("identb", [128, 128], BF16)
    rotT = P.sb("rotTs", [128, 128], F32)
    onesb = P.sb("onesb", [128, 128], BF16)
    onesf = P.sb("onesf", [128, 128], F32)
    maskb = P.sb("maskbs", [128, 48], F32)
    cst = P.sb("csts", [128, 8], F32)
    scT = P.sb("scT", [128, 8], BF16)
    RSB = P.sb("RSB", [128, 1024], F32)
    RS = [RSB[:, 0:512], RSB[:, 512:1024]]
    LN = P.sb("LNt", [128, 512], F32)
    SQ = [P.sb(f"SQ{i}", [128, 512], BF16) for i in range(3)]
    TM = [P.sb(f"TM{i}", [128, 1032], F32) for i in range(4)]
    EB = [P.sb(f"EB{i}", [128, 512], BF16) for i in range(4)]
    UT = P.sb("UTt", [128, 1024], BF16)
    BIASB = P.sb("BIASB", [128, 1024], BF16)
    MS = [P.sb(f"MS{i}", [128, 8, 128], BF16) for i in range(2)]
    PB = [P.ps(f"PB{i}", [128, 512], F32) for i in range(8)]

    def pb(i):
        return ("pb", i)

    def bg(g):
        return ("B", g)

    def Yt(c, tt):
        return ("Y", c, tt)

    def Ht(c, tt):
        return ("H", c, tt)

    ws_state = {"n": 0}

    def wslot():
        s = ws_state["n"] % 4
        ws_state["n"] += 1
        return s

    def wload_cast(src_view, a, b):
        s = wslot()
        v = WS[s][:, 0:a * b].rearrange("p (a b) -> p a b", a=a)
        P.dma_in("gpsimd", lambda e: e.dma_start(out=v, in_=src_view), ("ws", s))
        return v, ("ws", s)

    def wload_bf(src_view, a, b):
        s = wslot()
        v = WS[s][:, 0:a * b].rearrange("p (a b) -> p a b", a=a)
        P.dma_in("sync", lambda e: e.dma_start(out=v, in_=src_view), ("ws", s))
        return v, ("ws", s)

    def wload_f32(src_view, a, b):
        s = wslot()
        v = WS[s][:].bitcast(F32)[:, 0:a * b].rearrange("p (a b) -> p a b", a=a)
        P.dma_in("sync", lambda e: e.dma_start(out=v, in_=src_view), ("ws", s))
        return v, ("ws", s)

    def kview(w2d):
        return w2d.rearrange("(k p) n -> p k n", p=128)

    def V_(fn, reads, writes):
        return P.op("vector", fn, reads, writes)

    def A_(fn, reads, writes):
        return P.op("scalar", fn, reads, writes)

    def T_(fn, reads, writes):
        return P.op("tensor", fn, reads, writes)

    def G_(fn, reads, writes):
        return P.op("gpsimd", fn, reads, writes)

    def mm(out, lhsT, rhs, start, stop, reads, wtok):
        T_(lambda e: e.matmul(out, lhsT, rhs, start=start, stop=stop), reads, [wtok])

    P.dma_in("sync", lambda e: e.dma_start(out=ident[:], in_=ident_d), "ident")
    P.dma_in("sync", lambda e: e.dma_start(out=rotT[:], in_=rotT_d), "rotT")
    P.dma_in("sync", lambda e: e.dma_start(out=maskb[:], in_=maskb_d), "maskb")
    P.dma_in("sync", lambda e: e.dma_start(out=cst[:], in_=cst_d), "cst")
    V_(lambda e: e.memset(onesb[:], 1.0), [], ["onesb"])
    V_(lambda e: e.memset(onesf[:], 1.0), [], ["onesf"])
    V_(lambda e: e.tensor_copy(out=identb[:], in_=ident[:]), ["ident"], ["identb"])
    pvs = TM[0][:, 0:512].rearrange("p (j d) -> p j d", j=4)
    P.dma_in("sync", lambda e: e.dma_start(out=pvs, in_=pvec_d.rearrange("(j p) d -> p j d", p=128)), "TM0")
    for j in range(4):
        T_(lambda e, j=j: e.transpose(PB[0][:, j * 128:(j + 1) * 128], pvs[:, j, :], ident[:]), ["TM0", "ident"], [pb(0)])
    V_(lambda e: e.tensor_copy(out=PV[:], in_=PB[0][:]), [pb(0)], ["PV"])
    A_(lambda e: e.activation(out=scT[:], in_=PV[:, R_CVEC:R_CVEC + 8], func=ACT.Silu), ["PV"], ["scT"])

    BIGflat = BIG[:].rearrange("p a b -> p (a b)")
    BIGF = BIGflat.bitcast(F32)
    XS = BIGF[:, 0:8192].rearrange("p (j d) -> p j d", j=8)
    for j in range(8):
        P.dma_in("sync", lambda e, j=j: e.dma_start(out=XS[:, j, :], in_=x_d[j * 128:(j + 1) * 128, :]), bg(2 * j))
    nb = 0
    for c in range(8):
        for tt in range(2):
            b = nb % 4
            nb += 1
            for j4 in range(4):
                j = tt * 4 + j4
                T_(lambda e, b=b, j4=j4, j=j, c=c: e.transpose(PB[b][:, j4 * 128:(j4 + 1) * 128], XS[:, j, c * 128:(c + 1) * 128], ident[:]),
                   [bg(2 * j), bg(2 * j + 1), "ident"], [pb(b)])
            if (c + tt) % 2 == 0:
                V_(lambda e, b=b, c=c, tt=tt: e.tensor_copy(out=Y[:, c, tt * 512:(tt + 1) * 512], in_=PB[b][:]), [pb(b)], [Yt(c, tt)])
            else:
                A_(lambda e, b=b, c=c, tt=tt: e.activation(out=Y[:, c, tt * 512:(tt + 1) * 512], in_=PB[b][:], func=ACT.Copy), [pb(b)], [Yt(c, tt)])

    def mod_tasks(i):
        par = i % 2
        tasks = []

        def blk_task(blk):
            def run():
                v, tk = wload_cast(kview(mod_w[i])[:, :, blk * 512:(blk + 1) * 512], 8, 512)
                for n4 in range(4):
                    n = blk * 4 + n4
                    for k in range(8):
                        mm(PB[7][:, n:n + 1], v[:, k, n4 * 128:(n4 + 1) * 128], scT[:, k:k + 1], k == 0, k == 7, [tk, "scT"], pb(7))
                if blk == 11:
                    V_(lambda e: e.tensor_tensor(out=MODA[par][:], in0=PB[7][:, 0:48], in1=PV[:, R_MODB + i * 48:R_MODB + (i + 1) * 48], op=ALU.add),
                       [pb(7), "PV"], [("MODA", par)])
                    V_(lambda e: e.scalar_tensor_tensor(out=AG[par][:, 0:8], in0=MODA[par][:, 8:16], scalar=1.0, in1=PV[:, R_NMIX + i * 8:R_NMIX + i * 8 + 8], op0=ALU.add, op1=ALU.mult),
                       [("MODA", par), "PV"], [("AG", par)])
                    V_(lambda e: e.scalar_tensor_tensor(out=AG[par][:, 8:16], in0=MODA[par][:, 32:40], scalar=1.0, in1=PV[:, R_NFFN + i * 8:R_NFFN + i * 8 + 8], op0=ALU.add, op1=ALU.mult),
                       [("MODA", par), "PV"], [("AG", par)])
            return run
        for blk in range(12):
            tasks.append(blk_task(blk))
        return tasks

    ms_state = {"n": 0}

    def mod_tasks_small(i):
        par = i % 2
        tasks = []

        def mk(n):
            def run():
                sl = ms_state["n"] % 2
                ms_state["n"] += 1
                v = MS[sl]
                tk = ("ms", sl)
                P.dma_in("gpsimd", lambda e: e.dma_start(out=v[:], in_=kview(mod_w[i])[:, :, n * 128:(n + 1) * 128]), tk)
                for k in range(8):
                    mm(PB[7][:, n:n + 1], v[:, k, :], scT[:, k:k + 1], k == 0, k == 7, [tk, "scT"], pb(7))
                if n == 47:
                    V_(lambda e: e.tensor_tensor(out=MODA[par][:], in0=PB[7][:, 0:48], in1=PV[:, R_MODB + i * 48:R_MODB + (i + 1) * 48], op=ALU.add),
                       [pb(7), "PV"], [("MODA", par)])
                    V_(lambda e: e.scalar_tensor_tensor(out=AG[par][:, 0:8], in0=MODA[par][:, 8:16], scalar=1.0, in1=PV[:, R_NMIX + i * 8:R_NMIX + i * 8 + 8], op0=ALU.add, op1=ALU.mult),
                       [("MODA", par), "PV"], [("AG", par)])
                    V_(lambda e: e.scalar_tensor_tensor(out=AG[par][:, 8:16], in0=MODA[par][:, 32:40], scalar=1.0, in1=PV[:, R_NFFN + i * 8:R_NFFN + i * 8 + 8], op0=ALU.add, op1=ALU.mult),
                       [("MODA", par), "PV"], [("AG", par)])
            return run
        for n in range(48):
            tasks.append(mk(n))
        return tasks

    pend_tasks = []

    def pump(k):
        for _ in range(k):
            if pend_tasks:
                pend_tasks.pop(0)()

    def norm_mod(agcol, shcol, par):
        for tt in range(2):
            b = 6
            for c in range(8):
                s = SQ[c % 3]
                if c % 2 == 0:
                    A_(lambda e, s=s, c=c, tt=tt: e.activation(out=s[:], in_=Y[:, c, tt * 512:(tt + 1) * 512], func=ACT.Square), [Yt(c, tt)], [("SQ", c % 3)])
                else:
                    V_(lambda e, s=s, c=c, tt=tt: e.tensor_tensor(out=s[:], in0=Y[:, c, tt * 512:(tt + 1) * 512], in1=Y[:, c, tt * 512:(tt + 1) * 512], op=ALU.mult), [Yt(c, tt)], [("SQ", c % 3)])
                mm(PB[b][:], onesb[:], s[:], c == 0, c == 7, [("SQ", c % 3), "onesb"], pb(b))
            A_(lambda e: e.activation(out=LN[:], in_=PB[b][:], func=ACT.Ln, scale=1.0 / 1024, bias=cst[:, 1:2]), [pb(b), "cst"], ["LN"])
            A_(lambda e, tt=tt: e.activation(out=RS[tt][:], in_=LN[:], func=ACT.Exp, scale=-0.5), ["LN"], [("RS", tt)])
            for c in range(8):
                tm = TM[2 + c % 2]
                V_(lambda e, tm=tm, c=c, tt=tt: e.scalar_tensor_tensor(out=tm[:, 0:512], in0=Y[:, c, tt * 512:(tt + 1) * 512], scalar=AG[par][:, agcol + c:agcol + c + 1],
                                                                     in1=RS[tt][:], op0=ALU.mult, op1=ALU.mult),
                   [Yt(c, tt), ("AG", par), ("RS", tt)], [f"TM{2 + c % 2}"])
                A_(lambda e, tm=tm, c=c, tt=tt: e.activation(out=H[:, c, tt * 512:(tt + 1) * 512], in_=tm[:, 0:512], func=ACT.Identity,
                                                             bias=MODA[par][:, shcol + c:shcol + c + 1], scale=1.0),
                   [f"TM{2 + c % 2}", ("MODA", par)], [Ht(c, tt)])

    def out_proj(w2d, src_view, src_toks, gcol, par):
        vs = [wload_cast(kview(w2d)[:, :, half * 512:(half + 1) * 512], 8, 512) for half in range(2)]
        nb = 0
        for tt in range(2):
            for n in range(8):
                v, tk = vs[n // 4]
                n4 = n % 4
                b = nb % 4
                nb += 1
                for k in range(8):
                    mm(PB[b][:], v[:, k, n4 * 128:(n4 + 1) * 128], src_view(k, tt), k == 0, k == 7, [tk, src_toks(k, tt)], pb(b))
                V_(lambda e, b=b, n=n, tt=tt: e.scalar_tensor_tensor(out=Y[:, n, tt * 512:(tt + 1) * 512], in0=PB[b][:], scalar=MODA[par][:, gcol + n:gcol + n + 1],
                                                                     in1=Y[:, n, tt * 512:(tt + 1) * 512], op0=ALU.mult, op1=ALU.add),
                   [pb(b), ("MODA", par), Yt(n, tt)], [Yt(n, tt)])

    def ffn(i, par, inter_tasks):
        groups = [(0, 512), (512, 512), (1024, 512), (1536, 512), (2048, 512), (2560, 256)]
        pre = {0: (wload_cast(kview(ffn_wg[i])[:, :, 0:512], 8, 512), wload_cast(kview(ffn_wu[i])[:, :, 0:512], 8, 512))}
        norm_mod(8, 24, par)
        Abuf = BIG
        nb = 0
        for (c0, cw) in groups:
            if c0 in pre:
                (vg, tg), (vu, tu) = pre[c0]
            else:
                vg, tg = wload_cast(kview(ffn_wg[i])[:, :, c0:c0 + cw], 8, cw)
                vu, tu = wload_cast(kview(ffn_wu[i])[:, :, c0:c0 + cw], 8, cw)
            for tt in range(2):
                for f4 in range(cw // 128):
                    f = c0 // 128 + f4
                    bgt = nb % 2
                    but = 2 + nb % 2
                    nb += 1
                    for k in range(8):
                        mm(PB[bgt][:], vg[:, k, f4 * 128:(f4 + 1) * 128], H[:, k, tt * 512:(tt + 1) * 512], k == 0, k == 7, [tg, Ht(k, tt)], pb(bgt))
                    for k in range(8):
                        mm(PB[but][:], vu[:, k, f4 * 128:(f4 + 1) * 128], H[:, k, tt * 512:(tt + 1) * 512], k == 0, k == 7, [tu, Ht(k, tt)], pb(but))
                    sg = TM[2 + nb % 2]
                    A_(lambda e, sg=sg, bgt=bgt: e.activation(out=sg[:, 0:512], in_=PB[bgt][:], func=ACT.Silu), [pb(bgt)], [f"TM{2 + nb % 2}"])
                    V_(lambda e, sg=sg, but=but, f=f, tt=tt: e.tensor_tensor(out=Abuf[:, f, tt * 512:(tt + 1) * 512], in0=PB[but][:], in1=sg[:, 0:512], op=ALU.mult),
                       [pb(but), f"TM{2 + nb % 2}"], [bg(f)])
        nb = 0

        def down_one(v, tk, n, tt):
            nonlocal nb
            b = 4 + nb % 2
            nb += 1
            for f in range(NFF):
                mm(PB[b][:], v[:, f, :], Abuf[:, f, tt * 512:(tt + 1) * 512], f == 0, f == NFF - 1, [tk, bg(f)], pb(b))
            V_(lambda e: e.scalar_tensor_tensor(out=Y[:, n, tt * 512:(tt + 1) * 512], in0=PB[b][:], scalar=MODA[par][:, 40 + n:41 + n],
                                                in1=Y[:, n, tt * 512:(tt + 1) * 512], op0=ALU.mult, op1=ALU.add),
               [pb(b), ("MODA", par), Yt(n, tt)], [Yt(n, tt)])

        for n in range(4):
            v, tk = wload_cast(kview(ffn_wd[i])[:, :, n * 128:(n + 1) * 128], 22, 128)
            for tt in range(2):
                down_one(v, tk, n, tt)
        vs = [wload_cast(kview(ffn_wd[i])[:, :, n * 128:(n + 1) * 128], 22, 128) for n in range(4, 8)]
        for tt in range(2):
            for n in range(4, 8):
                down_one(vs[n - 4][0], vs[n - 4][1], n, tt)

    def filter_gen(j, inter):
        ft = TM[0]
        P.dma_in("sync", lambda e: e.dma_start(out=ft[0:33, 0:1024], in_=featsT_d), "TM0")
        w1s = TM[3]
        P.dma_in("sync", lambda e: e.dma_start(out=w1s[0:33, 0:64], in_=f_w1[j]), "TM3")
        P.dma_in("sync", lambda e: e.dma_start(out=w1s[0:64, 64:128], in_=f_w2[j]), "TM3")
        P.dma_in("gpsimd", lambda e: e.dma_start(out=BIASB[:], in_=hy_bias[j:j + 1, :].partition_broadcast(128)), "BIASB")
        w3b = BIGflat[:, 32 * 1024:34 * 1024]
        w3t = bg(32)
        P.dma_in("gpsimd", lambda e: e.dma_start(out=w3b[0:64, :], in_=f_w3[j]), w3t)

        def sin_layer(src, lhsT, K, bcol, fcol, dst, srctok, dsttok):
            for tt in range(2):
                b = tt
                mm(PB[b][0:64, :], lhsT, src[0:K, tt * 512:(tt + 1) * 512], True, True, ["TM3", srctok], pb(b))
                V_(lambda e, b=b, tt=tt: e.tensor_scalar(out=dst[0:64, tt * 512:(tt + 1) * 512], in0=PB[b][0:64, :], scalar1=PV[0:64, bcol:bcol + 1], scalar2=PV[0:64, fcol:fcol + 1],
                                                         op0=ALU.add, op1=ALU.mult), [pb(b), "PV"], [dsttok])
            A_(lambda e: e.activation(out=dst[0:64, 0:1024], in_=dst[0:64, 0:1024], func=ACT.Sin, scale=1.0 / 3.0), [dsttok], [dsttok])
            s2 = TM[2]
            V_(lambda e: e.tensor_tensor(out=s2[0:64, 0:1024], in0=dst[0:64, 0:1024], in1=dst[0:64, 0:1024], op=ALU.mult), [dsttok], ["TM2"])
            V_(lambda e: e.tensor_scalar(out=s2[0:64, 0:1024], in0=s2[0:64, 0:1024], scalar1=-4.0, scalar2=3.0, op0=ALU.mult, op1=ALU.add), ["TM2"], ["TM2"])
            V_(lambda e: e.tensor_tensor(out=dst[0:64, 0:1024], in0=dst[0:64, 0:1024], in1=s2[0:64, 0:1024], op=ALU.mult), [dsttok, "TM2"], [dsttok])

        h1 = TM[1]
        sin_layer(ft, w1s[0:33, 0:64], 33, R_FB1 + j, R_FREQ + 2 * j, h1, "TM0", "TM1")
        h2 = TM[0]
        sin_layer(h1, w1s[0:64, 64:128], 64, R_FB2 + j, R_FREQ + 2 * j + 1, h2, "TM1", "TM0")
        for tt in range(2):
            V_(lambda e, tt=tt: e.tensor_copy(out=EB[tt][0:64, :], in_=h2[0:64, tt * 512:(tt + 1) * 512]), ["TM0"], [("EB", tt)])
        sumw = UT
        P.dma_in("sync", lambda e: e.dma_start(out=sumw[:], in_=sumw_d), "UT")
        wins = {}
        fbuf = [(TM[2][:, 0:512], "TM2", TM[3][:, 0:512], "TM3", SQ[0], ("SQ", 0), SQ[1], ("SQ", 1)),
                (ZV[:, 0:512], "ZV", ZV[:, 512:1024], "ZV", SQ[2], ("SQ", 2), EB[2], ("EB", 2))]

        def fA(it):
            tc, ct = it // 2, it % 2
            if ct == 0:
                wins[tc] = wload_f32(win_d[tc * 128:(tc + 1) * 128, :, :], 2, 1024)
            wv, wt = wins[tc]
            hf, hft, hb, hbt, sq0, sq0t, sq1, sq1t = fbuf[it % 2]
            bf_, bb_ = 2 * (it % 2), 2 * (it % 2) + 1
            h2b = EB[tc // 4][0:64, (tc % 4) * 128:(tc % 4 + 1) * 128]
            mm(PB[bf_][:], h2b, w3b[0:64, ct * 512:(ct + 1) * 512], True, True, [("EB", tc // 4), w3t, bg(33)], pb(bf_))
            mm(PB[bb_][:], h2b, w3b[0:64, 1024 + ct * 512:1024 + (ct + 1) * 512], True, True, [("EB", tc // 4), w3t, bg(33)], pb(bb_))
            V_(lambda e: e.tensor_tensor(out=hf, in0=PB[bf_][:], in1=wv[:, 0, ct * 512:(ct + 1) * 512], op=ALU.mult), [pb(bf_), wt], [hft])
            V_(lambda e: e.tensor_tensor(out=hb, in0=PB[bb_][:], in1=wv[:, 1, ct * 512:(ct + 1) * 512], op=ALU.mult), [pb(bb_), wt], [hbt])
            A_(lambda e: e.activation(out=sq0[:], in_=hf, func=ACT.Square), [hft], [sq0t])
            A_(lambda e: e.activation(out=sq1[:], in_=hb, func=ACT.Square), [hbt], [sq1t])
            V_(lambda e: e.tensor_tensor(out=BIG[:, 16 + tc, ct * 512:(ct + 1) * 512], in0=hf, in1=hb, op=ALU.add), [hft, hbt], [bg(16 + tc)])
            V_(lambda e: e.tensor_tensor(out=BIG[:, 24 + tc, ct * 512:(ct + 1) * 512], in0=hb, in1=hf, op=ALU.subtract), [hft, hbt], [bg(24 + tc)])

        def fB(it):
            tc, ct = it // 2, it % 2
            hf, hft, hb, hbt, sq0, sq0t, sq1, sq1t = fbuf[it % 2]
            bn = 4 + ct
            mm(PB[bn][:], sumw[:, tc * 128:(tc + 1) * 128], sq0[:], tc == 0, False, ["UT", sq0t], pb(bn))
            mm(PB[bn][:], sumw[:, tc * 128:(tc + 1) * 128], sq1[:], False, tc == 7, ["UT", sq1t], pb(bn))

        for it in range(17):
            if it < 16:
                fA(it)
            if it >= 1:
                fB(it - 1)
            if it % 2 == 1 and inter:
                inter.pop(0)()
        RN = TM[1]
        for ct in range(2):
            A_(lambda e, ct=ct: e.activation(out=LN[:], in_=PB[4 + ct][:], func=ACT.Ln, scale=1.0, bias=cst[:, 1:2]), [pb(4 + ct), "cst"], ["LN"])
            A_(lambda e, ct=ct: e.activation(out=RN[:, ct * 512:(ct + 1) * 512], in_=LN[:], func=ACT.Exp, scale=-0.5), ["LN"], ["TM1"])

        while inter:
            inter.pop(0)()

    def hyena(i, j, par, prefilt=False):
        X0 = BIG
        nslot = {}

        def get_w(n):
            s = n // 4
            if s not in nslot:
                nslot[s] = wload_cast(kview(hy_w_in[j])[:, :, s * 512:(s + 1) * 512], 8, 512)
            return nslot[s]
        norm_mod(0, 0, par)
        if not prefilt:
            filter_gen(j, [])
        RN = TM[1]
        wv3 = []
        for s in range(6):
            wv3.append(None)
        PSB = [TM[2], TM[3]]
        X1 = TM[0]
        npb = 0
        def conv_chunk(n, dst_fn, dst_tok, idx):
            v, tk = get_w(n)
            n4 = n % 4
            psb = PSB[idx % 2]
            ptok = f"TM{2 + idx % 2}"
            pv = psb[:, 0:1032].rearrange("p (b w) -> p b w", b=4)
            for tt in range(2):
                b = idx % 2 * 2 + tt
                for k in range(8):
                    mm(PB[b][:], v[:, k, n4 * 128:(n4 + 1) * 128], H[:, k, tt * 512:(tt + 1) * 512], k == 0, k == 7, [tk, Ht(k, tt)], pb(b))
                A_(lambda e, b=b, tt=tt: e.activation(out=pv[:, 2 * tt:2 * tt + 2, 1:257], in_=PB[b][:].rearrange("p (b w) -> p b w", b=2), func=ACT.Copy), [pb(b)], [ptok])
                A_(lambda e, b=b, tt=tt: e.activation(out=RSB[:, tt * 512:(tt + 1) * 512].rearrange("p (b w) -> p b w", b=2), in_=PB[b][:].rearrange("p (b w) -> p b w", b=2),
                                                     func=ACT.Identity, scale=PV[:, R_CONVW + j * 72 + 24 + n:R_CONVW + j * 72 + 25 + n],
                                                     bias=PV[:, R_CONVB + j * 24 + n:R_CONVB + j * 24 + n + 1]), [pb(b), "PV"], [("RS", tt)])
            V_(lambda e: e.memset(pv[:, 0:1, 0:1], 0.0), [], [ptok])
            V_(lambda e: e.memset(pv[:, 3:4, 257:258], 0.0), [], [ptok])
            V_(lambda e: e.tensor_scalar(out=pv[:, 1:4, 0:1], in0=pv[:, 0:3, 256:257], scalar1=cst[:, 0:1], scalar2=None, op0=ALU.mult), [ptok, "cst"], [ptok])
            V_(lambda e: e.tensor_scalar(out=pv[:, 0:3, 257:258], in0=pv[:, 1:4, 1:2], scalar1=cst[:, 0:1], scalar2=None, op0=ALU.mult), [ptok, "cst"], [ptok])
            cw = R_CONVW + j * 72
            acc = TM[1] if False else None
            dst = dst_fn()
            accv = ACC[:, 0:1024].rearrange("p (b w) -> p b w", b=4)
            V_(lambda e: e.scalar_tensor_tensor(out=accv, in0=pv[:, :, 0:256], scalar=PV[:, cw + n:cw + n + 1], in1=accv, op0=ALU.mult, op1=ALU.add), [ptok, "PV", ("RS", 0), ("RS", 1)], [("RS", 0), ("RS", 1)])
            V_(lambda e: e.scalar_tensor_tensor(out=dst, in0=pv[:, :, 2:258], scalar=PV[:, cw + 48 + n:cw + 49 + n], in1=accv, op0=ALU.mult, op1=ALU.add), [ptok, "PV", ("RS", 0), ("RS", 1)], dst_tok)
            pump(1)

        ACC = RSB
        idx = 0
        pendT = None

        def mkT(c):
            def run():
                bT = 4 + c % 2
                pbf = PB[bT][:].bitcast(BF16)
                for tcn in range(8):
                    T_(lambda e, tcn=tcn: e.transpose(pbf[:, tcn * 128:(tcn + 1) * 128], UT[:, tcn * 128:(tcn + 1) * 128], identb[:]), ["UT", "identb"], [pb(bT)])
                A_(lambda e: e.activation(out=BIG[:, 8:16, c * 128:(c + 1) * 128], in_=pbf.rearrange("p (j w) -> p j w", j=8), func=ACT.Copy),
                   [pb(bT)], [bg(8 + q) for q in range(8)])
            return run

        for c in range(8):
            conv_chunk(c, lambda c=c: X0[:, c, :].rearrange("p (b w) -> p b w", b=4), [bg(c)], idx); idx += 1
            if pendT is not None:
                pendT()
                pendT = None
            conv_chunk(8 + c, lambda: X1[:, 0:1024].rearrange("p (b w) -> p b w", b=4), ["TM0"], idx); idx += 1
            conv_chunk(16 + c, lambda: ZV[:, 0:1024].rearrange("p (b w) -> p b w", b=4), ["ZV"], idx); idx += 1
            V_(lambda e: e.tensor_tensor(out=UT[:], in0=ZV[:, 0:1024], in1=X1[:, 0:1024], op=ALU.mult), ["ZV", "TM0"], ["UT"])
            pendT = mkT(c)
        pendT()

        YR = H
        cslots = {}

        def get_ct(fc):
            s = fc // 4
            if s not in cslots:
                a = wload_bf(kview(CT_d)[:, :, s * 512:(s + 1) * 512], 8, 512)
                b_ = wload_bf(kview(ST_d)[:, :, s * 512:(s + 1) * 512], 8, 512)
                cslots[s] = (a, b_)
            return cslots[s]

        it = 0
        for fc in range(8):
            (cv, ctk), (sv, stk) = get_ct(fc)
            f4 = fc % 4
            for ct in range(2):
                bA, bB, bP, bQ = (0, 1, 2, 3) if it % 2 == 0 else (4, 5, 6, 3)
                it += 1
                cs = slice(ct * 512, (ct + 1) * 512)
                for m in range(8):
                    lc = cv[:, m, f4 * 128:(f4 + 1) * 128]
                    ls = sv[:, m, f4 * 128:(f4 + 1) * 128]
                    mm(PB[bA][:], lc, BIG[:, 8 + m, cs], m == 0, m == 7, [ctk, bg(8 + m)], pb(bA))
                    mm(PB[bP][:], lc, BIG[:, 16 + m, cs], m == 0, m == 7, [ctk, bg(16 + m)], pb(bP))
                    mm(PB[bB][:], ls, BIG[:, 8 + m, cs], m == 0, m == 7, [stk, bg(8 + m)], pb(bB))
                    mm(PB[bQ][:], ls, BIG[:, 24 + m, cs], m == 0, m == 7, [stk, bg(24 + m)], pb(bQ))
                Pn = TM[2]
                Qn = TM[3]
                V_(lambda e, bP=bP, cs=cs: e.tensor_tensor(out=Pn[:, 0:512], in0=PB[bP][:], in1=RN[:, cs], op=ALU.mult), [pb(bP), "TM1"], ["TM2"])
                V_(lambda e, cs=cs: e.tensor_tensor(out=Pn[:, 0:512], in0=Pn[:, 0:512], in1=BIASB[:, cs], op=ALU.add), ["TM2", "BIASB"], ["TM2"])
                V_(lambda e, bQ=bQ, cs=cs: e.tensor_tensor(out=Qn[:, 0:512], in0=PB[bQ][:], in1=RN[:, cs], op=ALU.mult), [pb(bQ), "TM1"], ["TM3"])
                t1 = TM[0]
                V_(lambda e, bA=bA: e.tensor_tensor(out=t1[:, 0:512], in0=PB[bA][:], in1=Pn[:, 0:512], op=ALU.mult), [pb(bA), "TM2"], ["TM0"])
                V_(lambda e, bB=bB: e.tensor_tensor(out=t1[:, 512:1024], in0=PB[bB][:], in1=Qn[:, 0:512], op=ALU.mult), [pb(bB), "TM3"], ["TM0"])
                V_(lambda e, fc=fc, cs=cs: e.tensor_tensor(out=YR[:, fc, cs], in0=t1[:, 0:512], in1=t1[:, 512:1024], op=ALU.add), ["TM0"], [Ht(fc, ct)])
                t2 = ZV
                V_(lambda e, bB=bB: e.tensor_tensor(out=t2[:, 0:512], in0=PB[bB][:], in1=Pn[:, 0:512], op=ALU.mult), [pb(bB), "TM2"], ["ZV"])
                V_(lambda e, bA=bA: e.tensor_tensor(out=t2[:, 512:1024], in0=PB[bA][:], in1=Qn[:, 0:512], op=ALU.mult), [pb(bA), "TM3"], ["ZV"])
                V_(lambda e, fc=fc, cs=cs: e.tensor_tensor(out=BIG[:, 32 + fc, cs], in0=t2[:, 0:512], in1=t2[:, 512:1024], op=ALU.subtract), ["ZV"], [bg(32 + fc)])
                pump(2)

        nb = 0
        for tt in range(2):
            (civ, citk) = wload_bf(kview(CI_d)[:, :, tt * 512:(tt + 1) * 512], 8, 512)
            (siv, sitk) = wload_bf(kview(SI_d)[:, :, tt * 512:(tt + 1) * 512], 8, 512)
            for c in range(8):
                b = nb % 4
                nb += 1
                for fc in range(8):
                    mm(PB[b][:], YR[:, fc, c * 128:(c + 1) * 128], civ[:, fc, :], fc == 0, False, [citk, Ht(fc, c // 4)], pb(b))
                    mm(PB[b][:], BIG[:, 32 + fc, c * 128:(c + 1) * 128], siv[:, fc, :], False, fc == 7, [sitk, bg(32 + fc)], pb(b))
                V_(lambda e, b=b, c=c, tt=tt: e.tensor_tensor(out=BIG[:, 8 + c, tt * 512:(tt + 1) * 512], in0=PB[b][:], in1=X0[:, c, tt * 512:(tt + 1) * 512], op=ALU.mult),
                   [pb(b), bg(c)], [bg(8 + c)])
        out_proj(hy_w_out[j], lambda k, tt: BIG[:, 8 + k, tt * 512:(tt + 1) * 512], lambda k, tt: bg(8 + k), 16, par)

    def attention(i, j, par):
        wq = [wload_cast(kview(at_w_qkv[j])[:, :, s * 512:(s + 1) * 512], 8, 512) for s in range(3)]
        norm_mod(0, 0, par)
        QT = BIG
        KT = BIGflat[:, 16 * 1024:19 * 1024].rearrange("p (k s) -> p k s", k=2)
        VV = BIGflat[:, 19 * 1024:22 * 1024].rearrange("p (k e) -> p k e", k=12)
        ktoks = [bg(16), bg(17), bg(18)]
        vtoks = [bg(19), bg(20), bg(21)]
        CS = BIGF[:, 24 * 512:28 * 512]
        P.dma_in("sync", lambda e: e.dma_start(out=CS[:, 0:1024], in_=cosF_d), bg(24))
        P.dma_in("sync", lambda e: e.dma_start(out=CS[:, 1024:2048], in_=sinF_d), bg(26))
        cstoks = [bg(24), bg(26)]
        NK = BIGF[:, 28 * 512:32 * 512].rearrange("p (t e) -> p t e", t=8)
        NV = BIGF[:, 32 * 512:36 * 512].rearrange("p (t e) -> p t e", t=8)
        nktoks = [bg(28 + q) for q in range(4)]
        nvtoks = [bg(32 + q) for q in range(4)]
        ckl = TM[3][:, 0:1024].rearrange("p (c e) -> p c e", c=4)
        P.dma_in("sync", lambda e: e.dma_start(out=ckl, in_=cache_k[j].rearrange("(c p) e -> p c e", p=128)), "TM3")
        P.dma_in("gpsimd", lambda e: e.dma_start(out=VV[:, 8:12, :], in_=cache_v[j].rearrange("(c p) e -> p c e", p=128)), bg(21))
        for kv in range(2):
            b = kv
            for cc in range(4):
                T_(lambda e, b=b, cc=cc, kv=kv: e.transpose(PB[b][:, cc * 128:(cc + 1) * 128], ckl[:, cc, kv * 128:(kv + 1) * 128], ident[:]), ["TM3", "ident"], [pb(b)])
            V_(lambda e, b=b, kv=kv: e.tensor_copy(out=KT[:, kv, 1024:1536], in_=PB[b][:]), [pb(b)], ktoks)
        items = [(tt, n) for tt in range(2) for n in range(10)]
        NI = len(items)

        def stageA(it):
            tt, n = items[it]
            v, tk = wq[n // 4]
            n4 = n % 4
            b = it % 3
            for k in range(8):
                mm(PB[b][:], v[:, k, n4 * 128:(n4 + 1) * 128], H[:, k, tt * 512:(tt + 1) * 512], k == 0, k == 7, [tk, Ht(k, tt)], pb(b))
            s = SQ[it % 3]
            stok = ("SQ", it % 3)
            A_(lambda e: e.activation(out=s[:], in_=PB[b][:], func=ACT.Square), [pb(b)], [stok])

        def stageB(it):
            tt, n = items[it]
            gcol = (R_QN + j) if n < 8 else (R_KN + j)
            b = it % 3
            bs = 3 + it % 2
            s = SQ[it % 3]
            stok = ("SQ", it % 3)
            mm(PB[bs][:], onesb[:], s[:], True, True, [stok, "onesb"], pb(bs))
            A_(lambda e: e.activation(out=LN[:], in_=PB[bs][:], func=ACT.Ln, scale=1.0 / 128, bias=cst[:, 1:2]), [pb(bs), "cst"], ["LN"])
            rs = RS[it % 2]
            A_(lambda e: e.activation(out=rs[:], in_=LN[:], func=ACT.Exp, scale=-0.5), ["LN"], [("RS", it % 2)])
            qn = TM[it % 2]
            qtok = f"TM{it % 2}"
            V_(lambda e: e.scalar_tensor_tensor(out=qn[:, 0:512], in0=PB[b][:], scalar=PV[:, gcol:gcol + 1], in1=rs[:], op0=ALU.mult, op1=ALU.mult),
               [pb(b), "PV", ("RS", it % 2)], [qtok])

        def stageC(it):
            tt, n = items[it]
            br = 5
            qn = TM[it % 2]
            qtok = f"TM{it % 2}"
            mm(PB[br][:], rotT[:], qn[:, 0:512], True, True, ["rotT", qtok], pb(br))
            t1 = ZV
            V_(lambda e: e.tensor_tensor(out=t1[:, 0:512], in0=qn[:, 0:512], in1=CS[:, tt * 512:(tt + 1) * 512], op=ALU.mult), [qtok] + cstoks, ["ZV"])
            V_(lambda e: e.tensor_tensor(out=t1[:, 512:1024], in0=PB[br][:], in1=CS[:, 1024 + tt * 512:1024 + (tt + 1) * 512], op=ALU.mult), [pb(br)] + cstoks, ["ZV"])
            if n < 8:
                V_(lambda e: e.tensor_tensor(out=QT[:, n, tt * 512:(tt + 1) * 512], in0=t1[:, 0:512], in1=t1[:, 512:1024], op=ALU.add), ["ZV"], [bg(n)])
            else:
                kv = n - 8
                V_(lambda e: e.tensor_tensor(out=KT[:, kv, tt * 512:(tt + 1) * 512], in0=t1[:, 0:512], in1=t1[:, 512:1024], op=ALU.add), ["ZV"], ktoks)
                bt = 6
                for q in range(4):
                    T_(lambda e, q=q: e.transpose(PB[bt][:, q * 128:(q + 1) * 128], qn[:, q * 128:(q + 1) * 128], ident[:]), [qtok, "ident"], [pb(bt)])
                A_(lambda e: e.activation(out=NK[:, tt * 4:(tt + 1) * 4, kv * 128:(kv + 1) * 128], in_=PB[bt][:].rearrange("p (q d) -> p q d", q=4), func=ACT.Copy),
                   [pb(bt)], nktoks)

        for step in range(NI + 2):
            if step < NI:
                stageA(step)
            if 0 <= step - 1 < NI:
                stageB(step - 1)
            if 0 <= step - 2 < NI:
                stageC(step - 2)
            pump(1)
        v2, tk2 = wq[2]
        for tcn in range(8):
            b = 6 if tcn % 2 == 0 else 4
            for k in range(8):
                mm(PB[b][:, 0:256], H[:, k, tcn * 128:(tcn + 1) * 128], v2[:, k, 256:512], k == 0, k == 7, [tk2, Ht(k, tcn // 4)], pb(b))
            A_(lambda e, b=b, tcn=tcn: e.activation(out=NV[:, tcn, :], in_=PB[b][:, 0:256], func=ACT.Copy), [pb(b)], nvtoks)
            V_(lambda e, b=b, tcn=tcn: e.tensor_copy(out=VV[:, tcn, :], in_=NV[:, tcn, :]), nvtoks, vtoks)
        jj = j
        P.dma_out("sync", lambda e: e.dma_start(out=nk_d[jj].rearrange("(t p) e -> p t e", p=128), in_=NK), nktoks[0])
        P.dma_out("sync", lambda e: e.dma_start(out=nv_d[jj].rearrange("(t p) e -> p t e", p=128), in_=NV), nvtoks[0])
        for tks, oidx in ((nktoks, -2), (nvtoks, -1)):
            o = P.ops[oidx]
            for t in tks[1:]:
                tt_ = P.tok(t)
                if tt_.w is not None:
                    o.deps.add(tt_.w)
                tt_.readers.append(o)

        sb_i = 0
        acc_i = 0
        for kv in range(2):
            for g in range(4):
                h = kv * 4 + g
                for tt in range(2):
                    bO = 4 + acc_i % 2
                    bS = 3 if acc_i % 2 == 0 else 6
                    acc_i += 1
                    qs = slice(tt * 512, (tt + 1) * 512)
                    pend = []

                    def s_step(kc):
                        nonlocal sb_i
                        b = sb_i % 3
                        eb = EB[sb_i % 4]
                        etok = ("EB", sb_i % 4)
                        sb_i += 1
                        mm(PB[b][:], KT[:, kv, kc * 128:(kc + 1) * 128], QT[:, h, qs], True, True, ktoks + [bg(h)], pb(b))
                        for half in range(2):
                            qb = tt * 2 + half
                            A_(lambda e, b=b, eb=eb, half=half, qb=qb, kc=kc: e.activation(out=eb[:, half * 256:(half + 1) * 256], in_=PB[b][:, half * 256:(half + 1) * 256], func=ACT.Exp,
                                                                                         scale=SM_SCALE, bias=maskb[:, kc * 4 + qb:kc * 4 + qb + 1]), [pb(b), "maskb"], [etok])
                        return (kc, eb, etok)

                    def pv_step(kc, eb, etok):
                        mm(PB[bO][:], VV[:, kc, kv * 128:(kv + 1) * 128], eb[:], kc == 0, kc == 11, vtoks + [etok], pb(bO))
                        mm(PB[bS][:], onesb[:], eb[:], kc == 0, kc == 11, ["onesb", etok], pb(bS))

                    for kc in range(12):
                        pend.append(s_step(kc))
                        if len(pend) > 2:
                            pv_step(*pend.pop(0))
                    while pend:
                        pv_step(*pend.pop(0))
                    rsum = TM[acc_i % 2]
                    rtok = f"TM{acc_i % 2}"
                    V_(lambda e, rsum=rsum, bS=bS: e.reciprocal(out=rsum[:, 0:512], in_=PB[bS][:]), [pb(bS)], [rtok])
                    V_(lambda e, rsum=rsum, bO=bO, h=h, qs=qs: e.tensor_tensor(out=BIG[:, 8 + h, qs], in0=PB[bO][:], in1=rsum[:, 0:512], op=ALU.mult), [pb(bO), rtok], [bg(8 + h)])
                    pump(2)
        out_proj(at_w_out[j], lambda k, tt: BIG[:, 8 + k, tt * 512:(tt + 1) * 512], lambda k, tt: bg(8 + k), 16, par)

    ZV = P.sb("ZVt", [128, 1024], F32)
    if nsteps > 0:
        filter_gen(0, mod_tasks(0))
    else:
        for t in mod_tasks(0):
            t()
    for i in range(NLAYER):
        par = i % 2
        j = i // 2
        need_next = (i + 1 < NLAYER) and (2 * (i + 1) < nsteps)
        if need_next:
            pend_tasks.extend(mod_tasks_small(i + 1))
        if 2 * i < nsteps:
            if i % 2 == 0:
                hyena(i, j, par, prefilt=(i == 0))
            else:
                attention(i, j, par)
        pump(1000)
        if 2 * i + 1 < nsteps:
            ffn(i, par, [])

    OS = BIGF[:, 0:8192].rearrange("p (j d) -> p j d", j=8)
    for tt in range(2):
        b = 6
        for c in range(8):
            s = SQ[c % 3]
            A_(lambda e, s=s, c=c, tt=tt: e.activation(out=s[:], in_=Y[:, c, tt * 512:(tt + 1) * 512], func=ACT.Square), [Yt(c, tt)], [("SQ", c % 3)])
            mm(PB[b][:], onesb[:], s[:], c == 0, c == 7, [("SQ", c % 3), "onesb"], pb(b))
        A_(lambda e: e.activation(out=LN[:], in_=PB[b][:], func=ACT.Ln, scale=1.0 / 1024, bias=cst[:, 1:2]), [pb(b), "cst"], ["LN"])
        A_(lambda e, tt=tt: e.activation(out=RS[tt][:], in_=LN[:], func=ACT.Exp, scale=-0.5), ["LN"], [("RS", tt)])
        for c in range(8):
            V_(lambda e, c=c, tt=tt: e.scalar_tensor_tensor(out=Y[:, c, tt * 512:(tt + 1) * 512], in0=Y[:, c, tt * 512:(tt + 1) * 512], scalar=PV[:, R_FIN + c:R_FIN + c + 1],
                                                            in1=RS[tt][:], op0=ALU.mult, op1=ALU.mult), [Yt(c, tt), "PV", ("RS", tt)], [Yt(c, tt)])
    nb = 0
    for tcn in range(8):
        for half in range(2):
            b = nb % 4
            nb += 1
            for c4 in range(4):
                c = half * 4 + c4
                T_(lambda e, b=b, c4=c4, c=c, tcn=tcn: e.transpose(PB[b][:, c4 * 128:(c4 + 1) * 128], Y[:, c, tcn * 128:(tcn + 1) * 128], ident[:]), [Yt(c, tcn // 4), "ident"], [pb(b)])
            if nb % 2 == 0:
                V_(lambda e, b=b, tcn=tcn, half=half: e.tensor_copy(out=OS[:, tcn, half * 512:(half + 1) * 512], in_=PB[b][:]), [pb(b)], [bg(2 * tcn + half)])
            else:
                A_(lambda e, b=b, tcn=tcn, half=half: e.activation(out=OS[:, tcn, half * 512:(half + 1) * 512], in_=PB[b][:], func=ACT.Copy), [pb(b)], [bg(2 * tcn + half)])
        P.dma_out("sync", lambda e, tcn=tcn: e.dma_start(out=y_d[tcn * 128:(tcn + 1) * 128, :], in_=OS[:, tcn, :]), bg(2 * tcn))
        o = P.ops[-1]
        t2_ = P.tok(bg(2 * tcn + 1))
        if t2_.w is not None:
            o.deps.add(t2_.w)
        t2_.readers.append(o)
    P.emit()
    return nc, P


NSTEPS = 8
_CACHE = {}


def _consts(L, nrep):
    n = 2 * L
    f = np.arange(L, dtype=np.float64)
    m = np.arange(L, dtype=np.float64)
    th = np.pi * (2.0 * f[None, :] + 1.0) * m[:, None] / n
    CTb = np.cos(th)
    STb = np.sin(th)
    CIb = (2.0 / n) * np.cos(th.T)
    SIb = (2.0 / n) * np.sin(th.T)

    def bd(B):
        out = np.zeros((1024, 1024), np.float64)
        for r in range(nrep):
            out[r * L:(r + 1) * L, r * L:(r + 1) * L] = B
        return out.astype(ml_dtypes.bfloat16)
    t = np.arange(L, dtype=np.float32) / np.float32(L)
    bands = np.arange(1, 17, dtype=np.float32)
    ang = (2.0 * np.float32(math.pi)) * t[:, None] * bands[None, :]
    feats = np.concatenate([t[:, None], np.cos(ang), np.sin(ang)], axis=-1).astype(np.float32)
    featsT = np.tile(feats.T, (1, nrep)).astype(np.float32)
    MIN_DECAY = math.log(1e-2) / 1.5
    MAX_DECAY = math.log(1e-2) / 0.3
    deltas = np.abs(np.linspace(MIN_DECAY, MAX_DECAY, 1024, dtype=np.float32))
    win = np.exp(-t[:, None] * deltas[None, :]).astype(np.float32)
    winb = win * (np.arange(L) > 0).astype(np.float32)[:, None]
    w2 = np.stack([win, winb], axis=1)
    w2 = np.tile(w2, (nrep, 1, 1)).astype(np.float32)
    sumw = np.zeros((128, 8, 128), np.float32)
    sumw[:, 0:L // 128, :] = 1.0
    sumw = sumw.reshape(128, 1024).astype(ml_dtypes.bfloat16)
    return dict(CT=bd(CTb), ST=bd(STb), CI=bd(CIb), SI=bd(SIb), featsT=featsT, win=w2, sumw=sumw)


def _rope_tables():
    T = 1024
    GRID_W = 64
    ROWS = T // GRID_W
    row = np.repeat(np.arange(ROWS), GRID_W).astype(np.float32)
    col = np.tile(np.arange(GRID_W), ROWS).astype(np.float32)
    half = 64
    freqs = (np.float32(10000.0) ** (-np.arange(0, half, 2, dtype=np.float32) / np.float32(half))).astype(np.float32)
    ang = np.concatenate([row[:, None] * freqs[None, :], col[:, None] * freqs[None, :]], axis=-1)
    cos = np.cos(ang).astype(np.float32)
    sin = np.sin(ang).astype(np.float32)
    cosF = np.repeat(cos.T, 2, axis=0)
    sinF = np.repeat(sin.T, 2, axis=0)
    return np.ascontiguousarray(cosF), np.ascontiguousarray(sinF)


def _static():
    if "st" in _CACHE:
        return _CACHE["st"]
    ident = np.eye(128, dtype=np.float32)
    rotT = np.zeros((128, 128), np.float32)
    for i in range(64):
        rotT[2 * i + 1, 2 * i] = -1.0
        rotT[2 * i, 2 * i + 1] = 1.0
    cosF, sinF = _rope_tables()
    samp = _consts(1024, 1)
    prm = _consts(256, 4)
    maskb_s = np.zeros((128, 48), np.float32)
    maskb_p = np.full((128, 48), -30000.0, np.float32)
    for kc in range(8):
        maskb_p[:, kc * 4 + kc // 2] = 0.0
    cst_s = np.zeros((128, 8), np.float32)
    cst_s[:, 0] = 1.0
    cst_s[:, 1] = EPS
    cst_p = cst_s.copy()
    cst_p[:, 0] = 0.0
    st = dict(ident=ident, rotT=rotT,
              samp=dict(samp, cosF=cosF, sinF=sinF, maskb=maskb_s, cst=cst_s),
              prm=dict(prm, cosF=np.ones_like(cosF), sinF=np.zeros_like(sinF), maskb=maskb_p, cst=cst_p))
    _CACHE["st"] = st
    return st


def _pvec(cvec, mod_b, norm_mix, norm_ffn, final_norm, hy_conv_w, hy_conv_b, at_q_norm, at_k_norm, hy_f_b1, hy_f_freq, hy_f_b2):
    pv = np.zeros((NROW, 128), np.float32)
    pv[R_CVEC:R_CVEC + 8] = cvec.reshape(8, 128)
    pv[R_MODB:R_MODB + 192] = mod_b.reshape(4 * 48, 128)
    pv[R_NMIX:R_NMIX + 32] = norm_mix.reshape(32, 128)
    pv[R_NFFN:R_NFFN + 32] = norm_ffn.reshape(32, 128)
    pv[R_FIN:R_FIN + 8] = final_norm.reshape(8, 128)
    pv[R_CONVW:R_CONVW + 144] = hy_conv_w.reshape(2 * 3 * 24, 128)
    pv[R_CONVB:R_CONVB + 48] = hy_conv_b.reshape(48, 128)
    pv[R_QN:R_QN + 2] = at_q_norm
    pv[R_KN:R_KN + 2] = at_k_norm
    pv[R_FB1:R_FB1 + 2, 0:64] = hy_f_b1
    pv[R_FREQ:R_FREQ + 4, 0:64] = hy_f_freq.reshape(4, 64)
    pv[R_FB2:R_FB2 + 2, 0:64] = hy_f_b2
    return pv


def kernel(x_prompt, x_sample, cache_k, cache_v, c, c_ctx,
           mod_w, mod_b, norm_mix, norm_ffn,
           hy_w_in, hy_conv_w, hy_conv_b, hy_f_w1, hy_f_b1, hy_f_freq, hy_f_w2, hy_f_b2, hy_f_w3,
           hy_bias, hy_w_out,
           at_w_qkv, at_q_norm, at_k_norm, at_w_out,
           ffn_w_gate, ffn_w_up, ffn_w_down, final_norm, _nsteps=None):
    nsteps = NSTEPS if _nsteps is None else _nsteps
    f = lambda a: np.ascontiguousarray(np.asarray(a, dtype=np.float32))
    x_prompt, x_sample, cache_k, cache_v, c, c_ctx = map(f, (x_prompt, x_sample, cache_k, cache_v, c, c_ctx))
    st = _static()
    key = ("nc", nsteps)
    if key not in _CACHE:
        _CACHE[key] = build(nsteps)[0]
    nc = _CACHE[key]
    shared = dict(mod_w=f(mod_w), hy_w_in=f(hy_w_in), hy_w_out=f(hy_w_out), at_w_qkv=f(at_w_qkv), at_w_out=f(at_w_out),
                  ffn_wg=f(ffn_w_gate), ffn_wu=f(ffn_w_up), ffn_wd=f(ffn_w_down), f_w1=f(hy_f_w1), f_w2=f(hy_f_w2), f_w3=f(hy_f_w3),
                  hy_bias=f(hy_bias), ident=st["ident"], rotT=st["rotT"])
    pargs = [f(a) for a in (mod_b, norm_mix, norm_ffn, final_norm, hy_conv_w, hy_conv_b, at_q_norm, at_k_norm, hy_f_b1, hy_f_freq, hy_f_b2)]
    in_maps = []
    zero_cache = np.zeros((2, 512, 256), np.float32)
    for r in range(8):
        m = dict(shared)
        if r < 4:
            m["x"] = x_prompt[4 * r:4 * r + 4].reshape(1024, 1024)
            cv = c_ctx
            tb = st["prm"]
            m["cache_k"] = zero_cache
            m["cache_v"] = zero_cache
        else:
            b = r - 4
            m["x"] = x_sample[b]
            cv = c[b]
            tb = st["samp"]
            m["cache_k"] = np.ascontiguousarray(cache_k[b].reshape(2, 512, 256))
            m["cache_v"] = np.ascontiguousarray(cache_v[b].reshape(2, 512, 256))
        m["pvec"] = _pvec(cv, *pargs)
        for k in ("cosF", "sinF", "maskb", "cst", "featsT", "win", "sumw", "CT", "ST", "CI", "SI"):
            m[k] = tb[k]
        in_maps.append(m)
    res = run_bass_kernel_spmd(nc, in_maps, core_ids=list(range(8)))
    outs = res.results
    y_prompt = np.stack([outs[r]["y"] for r in range(4)], 0).reshape(16, 256, 1024).astype(np.float32)
    y_sample = np.stack([outs[r]["y"] for r in range(4, 8)], 0).reshape(4, 1024, 1024).astype(np.float32)
    nk = np.stack([outs[r]["new_k"] for r in range(4)], 0).reshape(4, 2, 4, 256, 2, 128).transpose(0, 2, 1, 3, 4, 5).reshape(16, 2, 256, 2, 128)
    nv = np.stack([outs[r]["new_v"] for r in range(4)], 0).reshape(4, 2, 4, 256, 2, 128).transpose(0, 2, 1, 3, 4, 5).reshape(16, 2, 256, 2, 128)
    return (y_prompt, y_sample, np.ascontiguousarray(nk, dtype=np.float32), np.ascontiguousarray(nv, dtype=np.float32))
```

```python
from contextlib import ExitStack
import concourse.bass as bass
import concourse.mybir as mybir

F32 = mybir.dt.float32
BF16 = mybir.dt.bfloat16
ACT = mybir.ActivationFunctionType
ALU = mybir.AluOpType

SAME_ENG_SYNC = True


import types


def _freeze(fn):
    cl = fn.__closure__
    if not cl:
        return fn
    cells = []
    for c in cl:
        try:
            cells.append(types.CellType(c.cell_contents))
        except ValueError:
            cells.append(c)
    g = types.FunctionType(fn.__code__, fn.__globals__, fn.__name__, fn.__defaults__, tuple(cells))
    g.__kwdefaults__ = fn.__kwdefaults__
    return g


class Tok:
    __slots__ = ("name", "w", "readers", "dma_w", "sem", "dma_r", "rsem", "n_dma_w", "n_dma_r")

    def __init__(self, name):
        self.name = name
        self.w = None
        self.readers = []
        self.dma_w = []
        self.sem = None
        self.rsem = None
        self.n_dma_w = 0
        self.n_dma_r = 0


class Op:
    __slots__ = ("eng", "fn", "deps", "dmadeps", "signal", "signo", "is_dma", "dma_sem", "dma_val", "idx")

    def __init__(self, eng, fn):
        self.eng = eng
        self.fn = _freeze(fn)
        self.deps = set()
        self.dmadeps = {}
        self.signal = False
        self.signo = 0
        self.is_dma = False
        self.dma_sem = None
        self.dma_val = 0


class Prog:
    ENGS = ("tensor", "vector", "scalar", "gpsimd", "sync")

    def __init__(self, nc):
        self.nc = nc
        self.es = ExitStack()
        self.ops = []
        self.toks = {}
        self.final_waits = {}
        self.nsem = 0

    def sb(self, name, shape, dt):
        return self.es.enter_context(self.nc.sbuf_tensor(name, shape, dt))

    def ps(self, name, shape, dt):
        return self.es.enter_context(self.nc.psum_tensor(name, shape, dt))

    def newsem(self, name):
        self.nsem += 1
        return self.es.enter_context(self.nc.semaphore(name))

    def tok(self, key):
        t = self.toks.get(key)
        if t is None:
            t = Tok(key)
            self.toks[key] = t
        return t

    def _toks(self, lst):
        out = []
        for k in lst:
            out.append(k if isinstance(k, Tok) else self.tok(k))
        return out

    def op(self, eng, fn, reads=(), writes=()):
        o = Op(eng, fn)
        o.idx = len(self.ops)
        for t in self._toks(reads):
            if t.w is not None:
                o.deps.add(t.w)
            for d in t.dma_w:
                o.dmadeps[d.dma_sem] = max(o.dmadeps.get(d.dma_sem, 0), d.dma_val)
            t.readers = [r for r in t.readers if r.is_dma or r.eng != o.eng] + [o]
        for t in self._toks(writes):
            if t.w is not None:
                o.deps.add(t.w)
            for d in t.dma_w:
                o.dmadeps[d.dma_sem] = max(o.dmadeps.get(d.dma_sem, 0), d.dma_val)
            for r in t.readers:
                if r is o:
                    continue
                if r.is_dma:
                    o.dmadeps[r.dma_sem] = max(o.dmadeps.get(r.dma_sem, 0), r.dma_val)
                else:
                    o.deps.add(r)
            t.w = o
            t.readers = []
            t.dma_w = []
        self.ops.append(o)
        return o

    def dma_in(self, eng, fn, dst, reads=()):
        t = self._toks([dst])[0]
        o = Op(eng, fn)
        o.idx = len(self.ops)
        o.is_dma = True
        if t.sem is None:
            t.sem = {}
        if eng not in t.sem:
            t.sem[eng] = [self.newsem("d_" + eng[0] + str(t.name)), 0]
        if t.w is not None:
            o.deps.add(t.w)
        for r in t.readers:
            if r.is_dma:
                o.dmadeps[r.dma_sem] = max(o.dmadeps.get(r.dma_sem, 0), r.dma_val)
            else:
                o.deps.add(r)
        t.sem[eng][1] += 1
        o.dma_sem = t.sem[eng][0]
        o.dma_val = 16 * t.sem[eng][1]
        t.w = None
        t.readers = []
        t.dma_w = [d for d in t.dma_w if d.dma_sem is not o.dma_sem] + [o]
        self.ops.append(o)
        return o

    def dma_out(self, eng, fn, src):
        t = self._toks([src])[0]
        o = Op(eng, fn)
        o.idx = len(self.ops)
        o.is_dma = True
        if t.rsem is None:
            t.rsem = self.newsem("r_" + str(t.name))
        if t.w is not None:
            o.deps.add(t.w)
        for d in t.dma_w:
            o.dmadeps[d.dma_sem] = max(o.dmadeps.get(d.dma_sem, 0), d.dma_val)
        t.n_dma_r += 1
        o.dma_sem = t.rsem
        o.dma_val = 16 * t.n_dma_r
        t.readers.append(o)
        self.final_waits[t.rsem] = o.dma_val
        self.ops.append(o)
        return o

    def emit(self):
        nc = self.nc
        esem = {e: self.newsem("e_" + e) for e in self.ENGS}
        for o in self.ops:
            for d in o.deps:
                if d.eng == o.eng and (o.eng == "tensor" or not SAME_ENG_SYNC):
                    continue
                d.signal = True
        cnt = {e: 0 for e in self.ENGS}
        per = {e: [] for e in self.ENGS}
        for o in self.ops:
            if o.signal and not o.is_dma:
                cnt[o.eng] += 1
                o.signo = cnt[o.eng]
            per[o.eng].append(o)
        final_waits = self.final_waits
        stats = {e: [0, 0] for e in self.ENGS}

        def run(engname, eng):
            seen = {}
            for o in per[engname]:
                waits = {}
                for d in o.deps:
                    if d.eng == engname and (engname == "tensor" or not SAME_ENG_SYNC):
                        continue
                    s = esem[d.eng]
                    waits[s] = max(waits.get(s, 0), d.signo)
                for s, v in o.dmadeps.items():
                    waits[s] = max(waits.get(s, 0), v)
                for s, v in waits.items():
                    if seen.get(s, 0) >= v:
                        continue
                    eng.wait_ge(s, v)
                    seen[s] = v
                    stats[engname][1] += 1
                ins = o.fn(eng)
                stats[engname][0] += 1
                if o.is_dma:
                    ins.then_inc(o.dma_sem, 16)
                elif o.signal:
                    ins.then_inc(esem[engname], 1)
            if engname == "sync":
                for s, v in final_waits.items():
                    eng.wait_ge(s, v)

        with nc.Block() as block:
            @block.tensor
            def _(e):
                run("tensor", e)

            @block.vector
            def _(e):
                run("vector", e)

            @block.scalar
            def _(e):
                run("scalar", e)

            @block.gpsimd
            def _(e):
                run("gpsimd", e)

            @block.sync
            def _(e):
                run("sync", e)
        self.stats = stats
        self.es.close()


import math
import numpy as np
import ml_dtypes
from concourse.bass_utils import run_bass_kernel_spmd

NLAYER = 4
EPS = 1e-6
D_FF = 2816
NFF = 22
SM_SCALE = 128 ** -0.5

R_CVEC = 0
R_MODB = 8
R_NMIX = 200
R_NFFN = 232
R_FIN = 264
R_CONVW = 272
R_CONVB = 416
R_QN = 464
R_KN = 466
R_FB1 = 468
R_FREQ = 470
R_FB2 = 474
NROW = 512


def build(nsteps=8):
    nc = bass.Bass("TRN2", target_bir_lowering=False)

    def DI(name, shape, dt=F32):
        return nc.dram_tensor(name, shape, dt, kind="ExternalInput").ap()

    x_d = DI("x", [1024, 1024])
    pvec_d = DI("pvec", [NROW, 128])
    mod_w = DI("mod_w", [4, 1024, 6144])
    hy_w_in = DI("hy_w_in", [2, 1024, 3072])
    hy_w_out = DI("hy_w_out", [2, 1024, 1024])
    at_w_qkv = DI("at_w_qkv", [2, 1024, 1536])
    at_w_out = DI("at_w_out", [2, 1024, 1024])
    ffn_wg = DI("ffn_wg", [4, 1024, D_FF])
    ffn_wu = DI("ffn_wu", [4, 1024, D_FF])
    ffn_wd = DI("ffn_wd", [4, D_FF, 1024])
    f_w1 = DI("f_w1", [2, 33, 64])
    f_w2 = DI("f_w2", [2, 64, 64])
    f_w3 = DI("f_w3", [2, 64, 2048])
    hy_bias = DI("hy_bias", [2, 1024])
    cache_k = DI("cache_k", [2, 512, 256])
    cache_v = DI("cache_v", [2, 512, 256])
    ident_d = DI("ident", [128, 128])
    rotT_d = DI("rotT", [128, 128])
    cosF_d = DI("cosF", [128, 1024])
    sinF_d = DI("sinF", [128, 1024])
    maskb_d = DI("maskb", [128, 48])
    cst_d = DI("cst", [128, 8])
    featsT_d = DI("featsT", [33, 1024])
    win_d = DI("win", [1024, 2, 1024])
    sumw_d = DI("sumw", [128, 1024], BF16)
    CT_d = DI("CT", [1024, 1024], BF16)
    ST_d = DI("ST", [1024, 1024], BF16)
    CI_d = DI("CI", [1024, 1024], BF16)
    SI_d = DI("SI", [1024, 1024], BF16)

    y_d = nc.dram_tensor("y", [1024, 1024], F32, kind="ExternalOutput").ap()
    nk_d = nc.dram_tensor("new_k", [2, 1024, 256], F32, kind="ExternalOutput").ap()
    nv_d = nc.dram_tensor("new_v", [2, 1024, 256], F32, kind="ExternalOutput").ap()

    P = Prog(nc)
    Y = P.sb("Y", [128, 8, 1024], F32)
    H = P.sb("H", [128, 8, 1024], BF16)
    BIG = P.sb("BIG", [128, 40, 1024], BF16)
    WS = [P.sb(f"WS{i}", [128, 4096], BF16) for i in range(4)]
    PV = P.sb("PV", [128, 512], F32)
    MODA = [P.sb(f"MODA{i}", [128, 48], F32) for i in range(2)]
    AG = [P.sb(f"AG{i}", [128, 16], F32) for i in range(2)]
    ident = P.sb("identf", [128, 128], F32)
    identb = P.sb("identb", [128, 128], BF16)
    rotT = P.sb("rotTs", [128, 128], F32)
    onesb = P.sb("onesb", [128, 128], BF16)
    onesf = P.sb("onesf", [128, 128], F32)
    maskb = P.sb("maskbs", [128, 48], F32)
    cst = P.sb("csts", [128, 8], F32)
    scT = P.sb("scT", [128, 8], BF16)
    RSB = P.sb("RSB", [128, 1024], F32)
    RS = [RSB[:, 0:512], RSB[:, 512:1024]]
    LN = P.sb("LNt", [128, 512], F32)
    SQ = [P.sb(f"SQ{i}", [128, 512], BF16) for i in range(3)]
    TM = [P.sb(f"TM{i}", [128, 1032], F32) for i in range(4)]
    EB = [P.sb(f"EB{i}", [128, 512], BF16) for i in range(4)]
    UT = P.sb("UTt", [128, 1024], BF16)
    BIASB = P.sb("BIASB", [128, 1024], BF16)
    MS = [P.sb(f"MS{i}", [128, 8, 128], BF16) for i in range(2)]
    PB = [P.ps(f"PB{i}", [128, 512], F32) for i in range(8)]

    def pb(i):
        return ("pb", i)

    def bg(g):
        return ("B", g)

    def Yt(c, tt):
        return ("Y", c, tt)

    def Ht(c, tt):
        return ("H", c, tt)

    ws_state = {"n": 0}

    def wslot():
        s = ws_state["n"] % 4
        ws_state["n"] += 1
        return s

    def wload_cast(src_view, a, b):
        s = wslot()
        v = WS[s][:, 0:a * b].rearrange("p (a b) -> p a b", a=a)
        P.dma_in("gpsimd", lambda e: e.dma_start(out=v, in_=src_view), ("ws", s))
        return v, ("ws", s)

    def wload_bf(src_view, a, b):
        s = wslot()
        v = WS[s][:, 0:a * b].rearrange("p (a b) -> p a b", a=a)
        P.dma_in("sync", lambda e: e.dma_start(out=v, in_=src_view), ("ws", s))
        return v, ("ws", s)

    def wload_f32(src_view, a, b):
        s = wslot()
        v = WS[s][:].bitcast(F32)[:, 0:a * b].rearrange("p (a b) -> p a b", a=a)
        P.dma_in("sync", lambda e: e.dma_start(out=v, in_=src_view), ("ws", s))
        return v, ("ws", s)

    def kview(w2d):
        return w2d.rearrange("(k p) n -> p k n", p=128)

    def V_(fn, reads, writes):
        return P.op("vector", fn, reads, writes)

    def A_(fn, reads, writes):
        return P.op("scalar", fn, reads, writes)

    def T_(fn, reads, writes):
        return P.op("tensor", fn, reads, writes)

    def G_(fn, reads, writes):
        return P.op("gpsimd", fn, reads, writes)

    def mm(out, lhsT, rhs, start, stop, reads, wtok):
        T_(lambda e: e.matmul(out, lhsT, rhs, start=start, stop=stop), reads, [wtok])

    P.dma_in("sync", lambda e: e.dma_start(out=ident[:], in_=ident_d), "ident")
    P.dma_in("sync", lambda e: e.dma_start(out=rotT[:], in_=rotT_d), "rotT")
    P.dma_in("sync", lambda e: e.dma_start(out=maskb[:], in_=maskb_d), "maskb")
    P.dma_in("sync", lambda e: e.dma_start(out=cst[:], in_=cst_d), "cst")
    V_(lambda e: e.memset(onesb[:], 1.0), [], ["onesb"])
    V_(lambda e: e.memset(onesf[:], 1.0), [], ["onesf"])
    V_(lambda e: e.tensor_copy(out=identb[:], in_=ident[:]), ["ident"], ["identb"])
    pvs = TM[0][:, 0:512].rearrange("p (j d) -> p j d", j=4)
    P.dma_in("sync", lambda e: e.dma_start(out=pvs, in_=pvec_d.rearrange("(j p) d -> p j d", p=128)), "TM0")
    for j in range(4):
        T_(lambda e, j=j: e.transpose(PB[0][:, j * 128:(j + 1) * 128], pvs[:, j, :], ident[:]), ["TM0", "ident"], [pb(0)])
    V_(lambda e: e.tensor_copy(out=PV[:], in_=PB[0][:]), [pb(0)], ["PV"])
    A_(lambda e: e.activation(out=scT[:], in_=PV[:, R_CVEC:R_CVEC + 8], func=ACT.Silu), ["PV"], ["scT"])

    BIGflat = BIG[:].rearrange("p a b -> p (a b)")
    BIGF = BIGflat.bitcast(F32)
    XS = BIGF[:, 0:8192].rearrange("p (j d) -> p j d", j=8)
    for j in range(8):
        P.dma_in("sync", lambda e, j=j: e.dma_start(out=XS[:, j, :], in_=x_d[j * 128:(j + 1) * 128, :]), bg(2 * j))
    nb = 0
    for c in range(8):
        for tt in range(2):
            b = nb % 4
            nb += 1
            for j4 in range(4):
                j = tt * 4 + j4
                T_(lambda e, b=b, j4=j4, j=j, c=c: e.transpose(PB[b][:, j4 * 128:(j4 + 1) * 128], XS[:, j, c * 128:(c + 1) * 128], ident[:]),
                   [bg(2 * j), bg(2 * j + 1), "ident"], [pb(b)])
            if (c + tt) % 2 == 0:
                V_(lambda e, b=b, c=c, tt=tt: e.tensor_copy(out=Y[:, c, tt * 512:(tt + 1) * 512], in_=PB[b][:]), [pb(b)], [Yt(c, tt)])
            else:
                A_(lambda e, b=b, c=c, tt=tt: e.activation(out=Y[:, c, tt * 512:(tt + 1) * 512], in_=PB[b][:], func=ACT.Copy), [pb(b)], [Yt(c, tt)])

    def mod_tasks(i):
        par = i % 2
        tasks = []

        def blk_task(blk):
            def run():
                v, tk = wload_cast(kview(mod_w[i])[:, :, blk * 512:(blk + 1) * 512], 8, 512)
                for n4 in range(4):
                    n = blk * 4 + n4
                    for k in range(8):
                        mm(PB[7][:, n:n + 1], v[:, k, n4 * 128:(n4 + 1) * 128], scT[:, k:k + 1], k == 0, k == 7, [tk, "scT"], pb(7))
                if blk == 11:
                    V_(lambda e: e.tensor_tensor(out=MODA[par][:], in0=PB[7][:, 0:48], in1=PV[:, R_MODB + i * 48:R_MODB + (i + 1) * 48], op=ALU.add),
                       [pb(7), "PV"], [("MODA", par)])
                    V_(lambda e: e.scalar_tensor_tensor(out=AG[par][:, 0:8], in0=MODA[par][:, 8:16], scalar=1.0, in1=PV[:, R_NMIX + i * 8:R_NMIX + i * 8 + 8], op0=ALU.add, op1=ALU.mult),
                       [("MODA", par), "PV"], [("AG", par)])
                    V_(lambda e: e.scalar_tensor_tensor(out=AG[par][:, 8:16], in0=MODA[par][:, 32:40], scalar=1.0, in1=PV[:, R_NFFN + i * 8:R_NFFN + i * 8 + 8], op0=ALU.add, op1=ALU.mult),
                       [("MODA", par), "PV"], [("AG", par)])
            return run
        for blk in range(12):
            tasks.append(blk_task(blk))
        return tasks

    ms_state = {"n": 0}

    def mod_tasks_small(i):
        par = i % 2
        tasks = []

        def mk(n):
            def run():
                sl = ms_state["n"] % 2
                ms_state["n"] += 1
                v = MS[sl]
                tk = ("ms", sl)
                P.dma_in("gpsimd", lambda e: e.dma_start(out=v[:], in_=kview(mod_w[i])[:, :, n * 128:(n + 1) * 128]), tk)
                for k in range(8):
                    mm(PB[7][:, n:n + 1], v[:, k, :], scT[:, k:k + 1], k == 0, k == 7, [tk, "scT"], pb(7))
                if n == 47:
                    V_(lambda e: e.tensor_tensor(out=MODA[par][:], in0=PB[7][:, 0:48], in1=PV[:, R_MODB + i * 48:R_MODB + (i + 1) * 48], op=ALU.add),
                       [pb(7), "PV"], [("MODA", par)])
                    V_(lambda e: e.scalar_tensor_tensor(out=AG[par][:, 0:8], in0=MODA[par][:, 8:16], scalar=1.0, in1=PV[:, R_NMIX + i * 8:R_NMIX + i * 8 + 8], op0=ALU.add, op1=ALU.mult),
                       [("MODA", par), "PV"], [("AG", par)])
                    V_(lambda e: e.scalar_tensor_tensor(out=AG[par][:, 8:16], in0=MODA[par][:, 32:40], scalar=1.0, in1=PV[:, R_NFFN + i * 8:R_NFFN + i * 8 + 8], op0=ALU.add, op1=ALU.mult),
                       [("MODA", par), "PV"], [("AG", par)])
            return run
        for n in range(48):
            tasks.append(mk(n))
        return tasks

    pend_tasks = []

    def pump(k):
        for _ in range(k):
            if pend_tasks:
                pend_tasks.pop(0)()

    def norm_mod(agcol, shcol, par):
        for tt in range(2):
            b = 6
            for c in range(8):
                s = SQ[c % 3]
                A_(lambda e, s=s, c=c, tt=tt: e.activation(out=s[:], in_=Y[:, c, tt * 512:(tt + 1) * 512], func=ACT.Square), [Yt(c, tt)], [("SQ", c % 3)])
                mm(PB[b][:], onesb[:], s[:], c == 0, c == 7, [("SQ", c % 3), "onesb"], pb(b))
            A_(lambda e: e.activation(out=LN[:], in_=PB[b][:], func=ACT.Ln, scale=1.0 / 1024, bias=cst[:, 1:2]), [pb(b), "cst"], ["LN"])
            A_(lambda e, tt=tt: e.activation(out=RS[tt][:], in_=LN[:], func=ACT.Exp, scale=-0.5), ["LN"], [("RS", tt)])
            for c in range(8):
                tm = TM[2 + c % 2]
                V_(lambda e, tm=tm, c=c, tt=tt: e.scalar_tensor_tensor(out=tm[:, 0:512], in0=Y[:, c, tt * 512:(tt + 1) * 512], scalar=AG[par][:, agcol + c:agcol + c + 1],
                                                                     in1=RS[tt][:], op0=ALU.mult, op1=ALU.mult),
                   [Yt(c, tt), ("AG", par), ("RS", tt)], [f"TM{2 + c % 2}"])
                A_(lambda e, tm=tm, c=c, tt=tt: e.activation(out=H[:, c, tt * 512:(tt + 1) * 512], in_=tm[:, 0:512], func=ACT.Identity,
                                                             bias=MODA[par][:, shcol + c:shcol + c + 1], scale=1.0),
                   [f"TM{2 + c % 2}", ("MODA", par)], [Ht(c, tt)])

    def out_proj(w2d, src_view, src_toks, gcol, par):
        vs = [wload_cast(kview(w2d)[:, :, half * 512:(half + 1) * 512], 8, 512) for half in range(2)]
        nb = 0
        for tt in range(2):
            for n in range(8):
                v, tk = vs[n // 4]
                n4 = n % 4
                b = nb % 4
                nb += 1
                for k in range(8):
                    mm(PB[b][:], v[:, k, n4 * 128:(n4 + 1) * 128], src_view(k, tt), k == 0, k == 7, [tk, src_toks(k, tt)], pb(b))
                V_(lambda e, b=b, n=n, tt=tt: e.scalar_tensor_tensor(out=Y[:, n, tt * 512:(tt + 1) * 512], in0=PB[b][:], scalar=MODA[par][:, gcol + n:gcol + n + 1],
                                                                     in1=Y[:, n, tt * 512:(tt + 1) * 512], op0=ALU.mult, op1=ALU.add),
                   [pb(b), ("MODA", par), Yt(n, tt)], [Yt(n, tt)])

    def ffn(i, par, inter_tasks):
        groups = [(0, 512), (512, 512), (1024, 512), (1536, 512), (2048, 512), (2560, 256)]
        pre = {0: (wload_cast(kview(ffn_wg[i])[:, :, 0:512], 8, 512), wload_cast(kview(ffn_wu[i])[:, :, 0:512], 8, 512))}
        norm_mod(8, 24, par)
        Abuf = BIG
        nb = 0
        for (c0, cw) in groups:
            if c0 in pre:
                (vg, tg), (vu, tu) = pre[c0]
            else:
                vg, tg = wload_cast(kview(ffn_wg[i])[:, :, c0:c0 + cw], 8, cw)
                vu, tu = wload_cast(kview(ffn_wu[i])[:, :, c0:c0 + cw], 8, cw)
            for tt in range(2):
                for f4 in range(cw // 128):
                    f = c0 // 128 + f4
                    bgt = nb % 2
                    but = 2 + nb % 2
                    nb += 1
                    for k in range(8):
                        mm(PB[bgt][:], vg[:, k, f4 * 128:(f4 + 1) * 128], H[:, k, tt * 512:(tt + 1) * 512], k == 0, k == 7, [tg, Ht(k, tt)], pb(bgt))
                    for k in range(8):
                        mm(PB[but][:], vu[:, k, f4 * 128:(f4 + 1) * 128], H[:, k, tt * 512:(tt + 1) * 512], k == 0, k == 7, [tu, Ht(k, tt)], pb(but))
                    sg = TM[2 + nb % 2]
                    A_(lambda e, sg=sg, bgt=bgt: e.activation(out=sg[:, 0:512], in_=PB[bgt][:], func=ACT.Silu), [pb(bgt)], [f"TM{2 + nb % 2}"])
                    V_(lambda e, sg=sg, but=but, f=f, tt=tt: e.tensor_tensor(out=Abuf[:, f, tt * 512:(tt + 1) * 512], in0=PB[but][:], in1=sg[:, 0:512], op=ALU.mult),
                       [pb(but), f"TM{2 + nb % 2}"], [bg(f)])
        nb = 0

        def down_one(v, tk, n, tt):
            nonlocal nb
            b = 4 + nb % 2
            nb += 1
            for f in range(NFF):
                mm(PB[b][:], v[:, f, :], Abuf[:, f, tt * 512:(tt + 1) * 512], f == 0, f == NFF - 1, [tk, bg(f)], pb(b))
            V_(lambda e: e.scalar_tensor_tensor(out=Y[:, n, tt * 512:(tt + 1) * 512], in0=PB[b][:], scalar=MODA[par][:, 40 + n:41 + n],
                                                in1=Y[:, n, tt * 512:(tt + 1) * 512], op0=ALU.mult, op1=ALU.add),
               [pb(b), ("MODA", par), Yt(n, tt)], [Yt(n, tt)])

        for n in range(4):
            v, tk = wload_cast(kview(ffn_wd[i])[:, :, n * 128:(n + 1) * 128], 22, 128)
            for tt in range(2):
                down_one(v, tk, n, tt)
        vs = [wload_cast(kview(ffn_wd[i])[:, :, n * 128:(n + 1) * 128], 22, 128) for n in range(4, 8)]
        for tt in range(2):
            for n in range(4, 8):
                down_one(vs[n - 4][0], vs[n - 4][1], n, tt)

    def filter_gen(j, inter):
        ft = TM[0]
        P.dma_in("sync", lambda e: e.dma_start(out=ft[0:33, 0:1024], in_=featsT_d), "TM0")
        w1s = TM[3]
        P.dma_in("sync", lambda e: e.dma_start(out=w1s[0:33, 0:64], in_=f_w1[j]), "TM3")
        P.dma_in("sync", lambda e: e.dma_start(out=w1s[0:64, 64:128], in_=f_w2[j]), "TM3")
        P.dma_in("gpsimd", lambda e: e.dma_start(out=BIASB[:], in_=hy_bias[j:j + 1, :].partition_broadcast(128)), "BIASB")
        w3b = BIGflat[:, 32 * 1024:34 * 1024]
        w3t = bg(32)
        P.dma_in("gpsimd", lambda e: e.dma_start(out=w3b[0:64, :], in_=f_w3[j]), w3t)

        def sin_layer(src, lhsT, K, bcol, fcol, dst, srctok, dsttok):
            for tt in range(2):
                b = tt
                mm(PB[b][0:64, :], lhsT, src[0:K, tt * 512:(tt + 1) * 512], True, True, ["TM3", srctok], pb(b))
                V_(lambda e, b=b, tt=tt: e.tensor_scalar(out=dst[0:64, tt * 512:(tt + 1) * 512], in0=PB[b][0:64, :], scalar1=PV[0:64, bcol:bcol + 1], scalar2=PV[0:64, fcol:fcol + 1],
                                                         op0=ALU.add, op1=ALU.mult), [pb(b), "PV"], [dsttok])
            A_(lambda e: e.activation(out=dst[0:64, 0:1024], in_=dst[0:64, 0:1024], func=ACT.Sin, scale=1.0 / 3.0), [dsttok], [dsttok])
            s2 = TM[2]
            V_(lambda e: e.tensor_tensor(out=s2[0:64, 0:1024], in0=dst[0:64, 0:1024], in1=dst[0:64, 0:1024], op=ALU.mult), [dsttok], ["TM2"])
            V_(lambda e: e.tensor_scalar(out=s2[0:64, 0:1024], in0=s2[0:64, 0:1024], scalar1=-4.0, scalar2=3.0, op0=ALU.mult, op1=ALU.add), ["TM2"], ["TM2"])
            V_(lambda e: e.tensor_tensor(out=dst[0:64, 0:1024], in0=dst[0:64, 0:1024], in1=s2[0:64, 0:1024], op=ALU.mult), [dsttok, "TM2"], [dsttok])

        h1 = TM[1]
        sin_layer(ft, w1s[0:33, 0:64], 33, R_FB1 + j, R_FREQ + 2 * j, h1, "TM0", "TM1")
        h2 = TM[0]
        sin_layer(h1, w1s[0:64, 64:128], 64, R_FB2 + j, R_FREQ + 2 * j + 1, h2, "TM1", "TM0")
        for tt in range(2):
            V_(lambda e, tt=tt: e.tensor_copy(out=EB[tt][0:64, :], in_=h2[0:64, tt * 512:(tt + 1) * 512]), ["TM0"], [("EB", tt)])
        sumw = UT
        P.dma_in("sync", lambda e: e.dma_start(out=sumw[:], in_=sumw_d), "UT")
        wins = {}
        fbuf = [(TM[2][:, 0:512], "TM2", TM[3][:, 0:512], "TM3", SQ[0], ("SQ", 0), SQ[1], ("SQ", 1)),
                (ZV[:, 0:512], "ZV", ZV[:, 512:1024], "ZV", SQ[2], ("SQ", 2), EB[2], ("EB", 2))]

        def fA(it):
            tc, ct = it // 2, it % 2
            if ct == 0:
                wins[tc] = wload_f32(win_d[tc * 128:(tc + 1) * 128, :, :], 2, 1024)
            wv, wt = wins[tc]
            hf, hft, hb, hbt, sq0, sq0t, sq1, sq1t = fbuf[it % 2]
            bf_, bb_ = 2 * (it % 2), 2 * (it % 2) + 1
            h2b = EB[tc // 4][0:64, (tc % 4) * 128:(tc % 4 + 1) * 128]
            mm(PB[bf_][:], h2b, w3b[0:64, ct * 512:(ct + 1) * 512], True, True, [("EB", tc // 4), w3t, bg(33)], pb(bf_))
            mm(PB[bb_][:], h2b, w3b[0:64, 1024 + ct * 512:1024 + (ct + 1) * 512], True, True, [("EB", tc // 4), w3t, bg(33)], pb(bb_))
            V_(lambda e: e.tensor_tensor(out=hf, in0=PB[bf_][:], in1=wv[:, 0, ct * 512:(ct + 1) * 512], op=ALU.mult), [pb(bf_), wt], [hft])
            V_(lambda e: e.tensor_tensor(out=hb, in0=PB[bb_][:], in1=wv[:, 1, ct * 512:(ct + 1) * 512], op=ALU.mult), [pb(bb_), wt], [hbt])
            A_(lambda e: e.activation(out=sq0[:], in_=hf, func=ACT.Square), [hft], [sq0t])
            A_(lambda e: e.activation(out=sq1[:], in_=hb, func=ACT.Square), [hbt], [sq1t])
            V_(lambda e: e.tensor_tensor(out=BIG[:, 16 + tc, ct * 512:(ct + 1) * 512], in0=hf, in1=hb, op=ALU.add), [hft, hbt], [bg(16 + tc)])
            V_(lambda e: e.tensor_tensor(out=BIG[:, 24 + tc, ct * 512:(ct + 1) * 512], in0=hb, in1=hf, op=ALU.subtract), [hft, hbt], [bg(24 + tc)])

        def fB(it):
            tc, ct = it // 2, it % 2
            hf, hft, hb, hbt, sq0, sq0t, sq1, sq1t = fbuf[it % 2]
            bn = 4 + ct
            mm(PB[bn][:], sumw[:, tc * 128:(tc + 1) * 128], sq0[:], tc == 0, False, ["UT", sq0t], pb(bn))
            mm(PB[bn][:], sumw[:, tc * 128:(tc + 1) * 128], sq1[:], False, tc == 7, ["UT", sq1t], pb(bn))

        for it in range(17):
            if it < 16:
                fA(it)
            if it >= 1:
                fB(it - 1)
            if it % 2 == 1 and inter:
                inter.pop(0)()
        RN = TM[1]
        for ct in range(2):
            A_(lambda e, ct=ct: e.activation(out=LN[:], in_=PB[4 + ct][:], func=ACT.Ln, scale=1.0, bias=cst[:, 1:2]), [pb(4 + ct), "cst"], ["LN"])
            A_(lambda e, ct=ct: e.activation(out=RN[:, ct * 512:(ct + 1) * 512], in_=LN[:], func=ACT.Exp, scale=-0.5), ["LN"], ["TM1"])

        while inter:
            inter.pop(0)()

    def hyena(i, j, par, prefilt=False):
        X0 = BIG
        nslot = {}

        def get_w(n):
            s = n // 4
            if s not in nslot:
                nslot[s] = wload_cast(kview(hy_w_in[j])[:, :, s * 512:(s + 1) * 512], 8, 512)
            return nslot[s]
        norm_mod(0, 0, par)
        if not prefilt:
            filter_gen(j, [])
        RN = TM[1]
        wv3 = []
        for s in range(6):
            wv3.append(None)
        PSB = [TM[2], TM[3]]
        X1 = TM[0]
        npb = 0
        def conv_chunk(n, dst_fn, dst_tok, idx):
            v, tk = get_w(n)
            n4 = n % 4
            psb = PSB[idx % 2]
            ptok = f"TM{2 + idx % 2}"
            pv = psb[:, 0:1032].rearrange("p (b w) -> p b w", b=4)
            for tt in range(2):
                b = idx % 2 * 2 + tt
                for k in range(8):
                    mm(PB[b][:], v[:, k, n4 * 128:(n4 + 1) * 128], H[:, k, tt * 512:(tt + 1) * 512], k == 0, k == 7, [tk, Ht(k, tt)], pb(b))
                A_(lambda e, b=b, tt=tt: e.activation(out=pv[:, 2 * tt:2 * tt + 2, 1:257], in_=PB[b][:].rearrange("p (b w) -> p b w", b=2), func=ACT.Copy), [pb(b)], [ptok])
                A_(lambda e, b=b, tt=tt: e.activation(out=RSB[:, tt * 512:(tt + 1) * 512].rearrange("p (b w) -> p b w", b=2), in_=PB[b][:].rearrange("p (b w) -> p b w", b=2),
                                                     func=ACT.Identity, scale=PV[:, R_CONVW + j * 72 + 24 + n:R_CONVW + j * 72 + 25 + n],
                                                     bias=PV[:, R_CONVB + j * 24 + n:R_CONVB + j * 24 + n + 1]), [pb(b), "PV"], [("RS", tt)])
            V_(lambda e: e.memset(pv[:, 0:1, 0:1], 0.0), [], [ptok])
            V_(lambda e: e.memset(pv[:, 3:4, 257:258], 0.0), [], [ptok])
            V_(lambda e: e.tensor_scalar(out=pv[:, 1:4, 0:1], in0=pv[:, 0:3, 256:257], scalar1=cst[:, 0:1], scalar2=None, op0=ALU.mult), [ptok, "cst"], [ptok])
            V_(lambda e: e.tensor_scalar(out=pv[:, 0:3, 257:258], in0=pv[:, 1:4, 1:2], scalar1=cst[:, 0:1], scalar2=None, op0=ALU.mult), [ptok, "cst"], [ptok])
            cw = R_CONVW + j * 72
            acc = TM[1] if False else None
            dst = dst_fn()
            accv = ACC[:, 0:1024].rearrange("p (b w) -> p b w", b=4)
            V_(lambda e: e.scalar_tensor_tensor(out=accv, in0=pv[:, :, 0:256], scalar=PV[:, cw + n:cw + n + 1], in1=accv, op0=ALU.mult, op1=ALU.add), [ptok, "PV", ("RS", 0), ("RS", 1)], [("RS", 0), ("RS", 1)])
            V_(lambda e: e.scalar_tensor_tensor(out=dst, in0=pv[:, :, 2:258], scalar=PV[:, cw + 48 + n:cw + 49 + n], in1=accv, op0=ALU.mult, op1=ALU.add), [ptok, "PV", ("RS", 0), ("RS", 1)], dst_tok)
            pump(1)

        ACC = RSB
        idx = 0
        pendT = None

        def mkT(c):
            def run():
                bT = 4 + c % 2
                pbf = PB[bT][:].bitcast(BF16)
                for tcn in range(8):
                    T_(lambda e, tcn=tcn: e.transpose(pbf[:, tcn * 128:(tcn + 1) * 128], UT[:, tcn * 128:(tcn + 1) * 128], identb[:]), ["UT", "identb"], [pb(bT)])
                A_(lambda e: e.activation(out=BIG[:, 8:16, c * 128:(c + 1) * 128], in_=pbf.rearrange("p (j w) -> p j w", j=8), func=ACT.Copy),
                   [pb(bT)], [bg(8 + q) for q in range(8)])
            return run

        for c in range(8):
            conv_chunk(c, lambda c=c: X0[:, c, :].rearrange("p (b w) -> p b w", b=4), [bg(c)], idx); idx += 1
            if pendT is not None:
                pendT()
                pendT = None
            conv_chunk(8 + c, lambda: X1[:, 0:1024].rearrange("p (b w) -> p b w", b=4), ["TM0"], idx); idx += 1
            conv_chunk(16 + c, lambda: ZV[:, 0:1024].rearrange("p (b w) -> p b w", b=4), ["ZV"], idx); idx += 1
            V_(lambda e: e.tensor_tensor(out=UT[:], in0=ZV[:, 0:1024], in1=X1[:, 0:1024], op=ALU.mult), ["ZV", "TM0"], ["UT"])
            pendT = mkT(c)
        pendT()

        YR = H
        cslots = {}

        def get_ct(fc):
            s = fc // 4
            if s not in cslots:
                a = wload_bf(kview(CT_d)[:, :, s * 512:(s + 1) * 512], 8, 512)
                b_ = wload_bf(kview(ST_d)[:, :, s * 512:(s + 1) * 512], 8, 512)
                cslots[s] = (a, b_)
            return cslots[s]

        it = 0
        for fc in range(8):
            (cv, ctk), (sv, stk) = get_ct(fc)
            f4 = fc % 4
            for ct in range(2):
                bA, bB, bP, bQ = (0, 1, 2, 3) if it % 2 == 0 else (4, 5, 6, 3)
                it += 1
                cs = slice(ct * 512, (ct + 1) * 512)
                for m in range(8):
                    lc = cv[:, m, f4 * 128:(f4 + 1) * 128]
                    ls = sv[:, m, f4 * 128:(f4 + 1) * 128]
                    mm(PB[bA][:], lc, BIG[:, 8 + m, cs], m == 0, m == 7, [ctk, bg(8 + m)], pb(bA))
                    mm(PB[bP][:], lc, BIG[:, 16 + m, cs], m == 0, m == 7, [ctk, bg(16 + m)], pb(bP))
                    mm(PB[bB][:], ls, BIG[:, 8 + m, cs], m == 0, m == 7, [stk, bg(8 + m)], pb(bB))
                    mm(PB[bQ][:], ls, BIG[:, 24 + m, cs], m == 0, m == 7, [stk, bg(24 + m)], pb(bQ))
                Pn = TM[2]
                Qn = TM[3]
                V_(lambda e, bP=bP, cs=cs: e.tensor_tensor(out=Pn[:, 0:512], in0=PB[bP][:], in1=RN[:, cs], op=ALU.mult), [pb(bP), "TM1"], ["TM2"])
                V_(lambda e, cs=cs: e.tensor_tensor(out=Pn[:, 0:512], in0=Pn[:, 0:512], in1=BIASB[:, cs], op=ALU.add), ["TM2", "BIASB"], ["TM2"])
                V_(lambda e, bQ=bQ, cs=cs: e.tensor_tensor(out=Qn[:, 0:512], in0=PB[bQ][:], in1=RN[:, cs], op=ALU.mult), [pb(bQ), "TM1"], ["TM3"])
                t1 = TM[0]
                V_(lambda e, bA=bA: e.tensor_tensor(out=t1[:, 0:512], in0=PB[bA][:], in1=Pn[:, 0:512], op=ALU.mult), [pb(bA), "TM2"], ["TM0"])
                V_(lambda e, bB=bB: e.tensor_tensor(out=t1[:, 512:1024], in0=PB[bB][:], in1=Qn[:, 0:512], op=ALU.mult), [pb(bB), "TM3"], ["TM0"])
                V_(lambda e, fc=fc, cs=cs: e.tensor_tensor(out=YR[:, fc, cs], in0=t1[:, 0:512], in1=t1[:, 512:1024], op=ALU.add), ["TM0"], [Ht(fc, ct)])
                t2 = ZV
                V_(lambda e, bB=bB: e.tensor_tensor(out=t2[:, 0:512], in0=PB[bB][:], in1=Pn[:, 0:512], op=ALU.mult), [pb(bB), "TM2"], ["ZV"])
                V_(lambda e, bA=bA: e.tensor_tensor(out=t2[:, 512:1024], in0=PB[bA][:], in1=Qn[:, 0:512], op=ALU.mult), [pb(bA), "TM3"], ["ZV"])
                V_(lambda e, fc=fc, cs=cs: e.tensor_tensor(out=BIG[:, 32 + fc, cs], in0=t2[:, 0:512], in1=t2[:, 512:1024], op=ALU.subtract), ["ZV"], [bg(32 + fc)])
                pump(2)

        nb = 0
        for tt in range(2):
            (civ, citk) = wload_bf(kview(CI_d)[:, :, tt * 512:(tt + 1) * 512], 8, 512)
            (siv, sitk) = wload_bf(kview(SI_d)[:, :, tt * 512:(tt + 1) * 512], 8, 512)
            for c in range(8):
                b = nb % 4
                nb += 1
                for fc in range(8):
                    mm(PB[b][:], YR[:, fc, c * 128:(c + 1) * 128], civ[:, fc, :], fc == 0, False, [citk, Ht(fc, c // 4)], pb(b))
                    mm(PB[b][:], BIG[:, 32 + fc, c * 128:(c + 1) * 128], siv[:, fc, :], False, fc == 7, [sitk, bg(32 + fc)], pb(b))
                V_(lambda e, b=b, c=c, tt=tt: e.tensor_tensor(out=BIG[:, 8 + c, tt * 512:(tt + 1) * 512], in0=PB[b][:], in1=X0[:, c, tt * 512:(tt + 1) * 512], op=ALU.mult),
                   [pb(b), bg(c)], [bg(8 + c)])
        out_proj(hy_w_out[j], lambda k, tt: BIG[:, 8 + k, tt * 512:(tt + 1) * 512], lambda k, tt: bg(8 + k), 16, par)

    def attention(i, j, par):
        wq = [wload_cast(kview(at_w_qkv[j])[:, :, s * 512:(s + 1) * 512], 8, 512) for s in range(3)]
        norm_mod(0, 0, par)
        QT = BIG
        KT = BIGflat[:, 16 * 1024:19 * 1024].rearrange("p (k s) -> p k s", k=2)
        VV = BIGflat[:, 19 * 1024:22 * 1024].rearrange("p (k e) -> p k e", k=12)
        ktoks = [bg(16), bg(17), bg(18)]
        vtoks = [bg(19), bg(20), bg(21)]
        CS = BIGF[:, 24 * 512:28 * 512]
        P.dma_in("sync", lambda e: e.dma_start(out=CS[:, 0:1024], in_=cosF_d), bg(24))
        P.dma_in("sync", lambda e: e.dma_start(out=CS[:, 1024:2048], in_=sinF_d), bg(26))
        cstoks = [bg(24), bg(26)]
        NK = BIGF[:, 28 * 512:32 * 512].rearrange("p (t e) -> p t e", t=8)
        NV = BIGF[:, 32 * 512:36 * 512].rearrange("p (t e) -> p t e", t=8)
        nktoks = [bg(28 + q) for q in range(4)]
        nvtoks = [bg(32 + q) for q in range(4)]
        ckl = TM[3][:, 0:1024].rearrange("p (c e) -> p c e", c=4)
        P.dma_in("sync", lambda e: e.dma_start(out=ckl, in_=cache_k[j].rearrange("(c p) e -> p c e", p=128)), "TM3")
        P.dma_in("gpsimd", lambda e: e.dma_start(out=VV[:, 8:12, :], in_=cache_v[j].rearrange("(c p) e -> p c e", p=128)), bg(21))
        items = [(tt, n) for tt in range(2) for n in range(10)]
        NI = len(items)

        def stageA(it):
            tt, n = items[it]
            v, tk = wq[n // 4]
            n4 = n % 4
            b = it % 3
            for k in range(8):
                mm(PB[b][:], v[:, k, n4 * 128:(n4 + 1) * 128], H[:, k, tt * 512:(tt + 1) * 512], k == 0, k == 7, [tk, Ht(k, tt)], pb(b))
            s = SQ[it % 3]
            stok = ("SQ", it % 3)
            A_(lambda e: e.activation(out=s[:], in_=PB[b][:], func=ACT.Square), [pb(b)], [stok])

        def stageB(it):
            tt, n = items[it]
            gcol = (R_QN + j) if n < 8 else (R_KN + j)
            b = it % 3
            bs = 3 + it % 2
            s = SQ[it % 3]
            stok = ("SQ", it % 3)
            mm(PB[bs][:], onesb[:], s[:], True, True, [stok, "onesb"], pb(bs))
            A_(lambda e: e.activation(out=LN[:], in_=PB[bs][:], func=ACT.Ln, scale=1.0 / 128, bias=cst[:, 1:2]), [pb(bs), "cst"], ["LN"])
            rs = RS[it % 2]
            A_(lambda e: e.activation(out=rs[:], in_=LN[:], func=ACT.Exp, scale=-0.5), ["LN"], [("RS", it % 2)])
            qn = TM[it % 2]
            qtok = f"TM{it % 2}"
            V_(lambda e: e.scalar_tensor_tensor(out=qn[:, 0:512], in0=PB[b][:], scalar=PV[:, gcol:gcol + 1], in1=rs[:], op0=ALU.mult, op1=ALU.mult),
               [pb(b), "PV", ("RS", it % 2)], [qtok])

        def stageC(it):
            tt, n = items[it]
            br = 5
            qn = TM[it % 2]
            qtok = f"TM{it % 2}"
            mm(PB[br][:], rotT[:], qn[:, 0:512], True, True, ["rotT", qtok], pb(br))
            t1 = ZV
            V_(lambda e: e.tensor_tensor(out=t1[:, 0:512], in0=qn[:, 0:512], in1=CS[:, tt * 512:(tt + 1) * 512], op=ALU.mult), [qtok] + cstoks, ["ZV"])
            V_(lambda e: e.tensor_tensor(out=t1[:, 512:1024], in0=PB[br][:], in1=CS[:, 1024 + tt * 512:1024 + (tt + 1) * 512], op=ALU.mult), [pb(br)] + cstoks, ["ZV"])
            if n < 8:
                V_(lambda e: e.tensor_tensor(out=QT[:, n, tt * 512:(tt + 1) * 512], in0=t1[:, 0:512], in1=t1[:, 512:1024], op=ALU.add), ["ZV"], [bg(n)])
            else:
                kv = n - 8
                V_(lambda e: e.tensor_tensor(out=KT[:, kv, tt * 512:(tt + 1) * 512], in0=t1[:, 0:512], in1=t1[:, 512:1024], op=ALU.add), ["ZV"], ktoks)
                bt = 6
                for q in range(4):
                    T_(lambda e, q=q: e.transpose(PB[bt][:, q * 128:(q + 1) * 128], qn[:, q * 128:(q + 1) * 128], ident[:]), [qtok, "ident"], [pb(bt)])
                A_(lambda e: e.activation(out=NK[:, tt * 4:(tt + 1) * 4, kv * 128:(kv + 1) * 128], in_=PB[bt][:].rearrange("p (q d) -> p q d", q=4), func=ACT.Copy),
                   [pb(bt)], nktoks)

        for step in range(NI + 2):
            if step < NI:
                stageA(step)
            if 0 <= step - 1 < NI:
                stageB(step - 1)
            if 0 <= step - 2 < NI:
                stageC(step - 2)
            pump(1)
        for kv in range(2):
            b = kv
            for cc in range(4):
                T_(lambda e, b=b, cc=cc, kv=kv: e.transpose(PB[b][:, cc * 128:(cc + 1) * 128], ckl[:, cc, kv * 128:(kv + 1) * 128], ident[:]), ["TM3", "ident"], [pb(b)])
            V_(lambda e, b=b, kv=kv: e.tensor_copy(out=KT[:, kv, 1024:1536], in_=PB[b][:]), [pb(b)], ktoks)
        v2, tk2 = wq[2]
        for tcn in range(8):
            b = 6 if tcn % 2 == 0 else 4
            for k in range(8):
                mm(PB[b][:, 0:256], H[:, k, tcn * 128:(tcn + 1) * 128], v2[:, k, 256:512], k == 0, k == 7, [tk2, Ht(k, tcn // 4)], pb(b))
            A_(lambda e, b=b, tcn=tcn: e.activation(out=NV[:, tcn, :], in_=PB[b][:, 0:256], func=ACT.Copy), [pb(b)], nvtoks)
            V_(lambda e, b=b, tcn=tcn: e.tensor_copy(out=VV[:, tcn, :], in_=NV[:, tcn, :]), nvtoks, vtoks)
        jj = j
        P.dma_out("sync", lambda e: e.dma_start(out=nk_d[jj].rearrange("(t p) e -> p t e", p=128), in_=NK), nktoks[0])
        P.dma_out("sync", lambda e: e.dma_start(out=nv_d[jj].rearrange("(t p) e -> p t e", p=128), in_=NV), nvtoks[0])
        for tks, oidx in ((nktoks, -2), (nvtoks, -1)):
            o = P.ops[oidx]
            for t in tks[1:]:
                tt_ = P.tok(t)
                if tt_.w is not None:
                    o.deps.add(tt_.w)
                tt_.readers.append(o)

        sb_i = 0
        acc_i = 0
        for kv in range(2):
            for g in range(4):
                h = kv * 4 + g
                for tt in range(2):
                    bO = 4 + acc_i % 2
                    bS = 3 if acc_i % 2 == 0 else 6
                    acc_i += 1
                    qs = slice(tt * 512, (tt + 1) * 512)
                    pend = []

                    def s_step(kc):
                        nonlocal sb_i
                        b = sb_i % 3
                        eb = EB[sb_i % 4]
                        etok = ("EB", sb_i % 4)
                        sb_i += 1
                        mm(PB[b][:], KT[:, kv, kc * 128:(kc + 1) * 128], QT[:, h, qs], True, True, ktoks + [bg(h)], pb(b))
                        for half in range(2):
                            qb = tt * 2 + half
                            A_(lambda e, b=b, eb=eb, half=half, qb=qb, kc=kc: e.activation(out=eb[:, half * 256:(half + 1) * 256], in_=PB[b][:, half * 256:(half + 1) * 256], func=ACT.Exp,
                                                                                         scale=SM_SCALE, bias=maskb[:, kc * 4 + qb:kc * 4 + qb + 1]), [pb(b), "maskb"], [etok])
                        return (kc, eb, etok)

                    def pv_step(kc, eb, etok):
                        mm(PB[bO][:], VV[:, kc, kv * 128:(kv + 1) * 128], eb[:], kc == 0, kc == 11, vtoks + [etok], pb(bO))
                        mm(PB[bS][:], onesb[:], eb[:], kc == 0, kc == 11, ["onesb", etok], pb(bS))

                    for kc in range(12):
                        pend.append(s_step(kc))
                        if len(pend) > 2:
                            pv_step(*pend.pop(0))
                    while pend:
                        pv_step(*pend.pop(0))
                    rsum = TM[acc_i % 2]
                    rtok = f"TM{acc_i % 2}"
                    V_(lambda e, rsum=rsum, bS=bS: e.reciprocal(out=rsum[:, 0:512], in_=PB[bS][:]), [pb(bS)], [rtok])
                    V_(lambda e, rsum=rsum, bO=bO, h=h, qs=qs: e.tensor_tensor(out=BIG[:, 8 + h, qs], in0=PB[bO][:], in1=rsum[:, 0:512], op=ALU.mult), [pb(bO), rtok], [bg(8 + h)])
                    pump(2)
        out_proj(at_w_out[j], lambda k, tt: BIG[:, 8 + k, tt * 512:(tt + 1) * 512], lambda k, tt: bg(8 + k), 16, par)

    ZV = P.sb("ZVt", [128, 1024], F32)
    if nsteps > 0:
        filter_gen(0, mod_tasks(0))
    else:
        for t in mod_tasks(0):
            t()
    for i in range(NLAYER):
        par = i % 2
        j = i // 2
        need_next = (i + 1 < NLAYER) and (2 * (i + 1) < nsteps)
        if need_next:
            pend_tasks.extend(mod_tasks_small(i + 1))
        if 2 * i < nsteps:
            if i % 2 == 0:
                hyena(i, j, par, prefilt=(i == 0))
            else:
                attention(i, j, par)
        pump(1000)
        if 2 * i + 1 < nsteps:
            ffn(i, par, [])

    OS = BIGF[:, 0:8192].rearrange("p (j d) -> p j d", j=8)
    for tt in range(2):
        b = 6
        for c in range(8):
            s = SQ[c % 3]
            A_(lambda e, s=s, c=c, tt=tt: e.activation(out=s[:], in_=Y[:, c, tt * 512:(tt + 1) * 512], func=ACT.Square), [Yt(c, tt)], [("SQ", c % 3)])
            mm(PB[b][:], onesb[:], s[:], c == 0, c == 7, [("SQ", c % 3), "onesb"], pb(b))
        A_(lambda e: e.activation(out=LN[:], in_=PB[b][:], func=ACT.Ln, scale=1.0 / 1024, bias=cst[:, 1:2]), [pb(b), "cst"], ["LN"])
        A_(lambda e, tt=tt: e.activation(out=RS[tt][:], in_=LN[:], func=ACT.Exp, scale=-0.5), ["LN"], [("RS", tt)])
        for c in range(8):
            V_(lambda e, c=c, tt=tt: e.scalar_tensor_tensor(out=Y[:, c, tt * 512:(tt + 1) * 512], in0=Y[:, c, tt * 512:(tt + 1) * 512], scalar=PV[:, R_FIN + c:R_FIN + c + 1],
                                                            in1=RS[tt][:], op0=ALU.mult, op1=ALU.mult), [Yt(c, tt), "PV", ("RS", tt)], [Yt(c, tt)])
    nb = 0
    for tcn in range(8):
        for half in range(2):
            b = nb % 4
            nb += 1
            for c4 in range(4):
                c = half * 4 + c4
                T_(lambda e, b=b, c4=c4, c=c, tcn=tcn: e.transpose(PB[b][:, c4 * 128:(c4 + 1) * 128], Y[:, c, tcn * 128:(tcn + 1) * 128], ident[:]), [Yt(c, tcn // 4), "ident"], [pb(b)])
            if nb % 2 == 0:
                V_(lambda e, b=b, tcn=tcn, half=half: e.tensor_copy(out=OS[:, tcn, half * 512:(half + 1) * 512], in_=PB[b][:]), [pb(b)], [bg(2 * tcn + half)])
            else:
                A_(lambda e, b=b, tcn=tcn, half=half: e.activation(out=OS[:, tcn, half * 512:(half + 1) * 512], in_=PB[b][:], func=ACT.Copy), [pb(b)], [bg(2 * tcn + half)])
        P.dma_out("sync", lambda e, tcn=tcn: e.dma_start(out=y_d[tcn * 128:(tcn + 1) * 128, :], in_=OS[:, tcn, :]), bg(2 * tcn))
        o = P.ops[-1]
        t2_ = P.tok(bg(2 * tcn + 1))
        if t2_.w is not None:
            o.deps.add(t2_.w)
        t2_.readers.append(o)
    P.emit()
    return nc, P


NSTEPS = 8
_CACHE = {}


def _consts(L, nrep):
    n = 2 * L
    f = np.arange(L, dtype=np.float64)
    m = np.arange(L, dtype=np.float64)
    th = np.pi * (2.0 * f[None, :] + 1.0) * m[:, None] / n
    CTb = np.cos(th)
    STb = np.sin(th)
    CIb = (2.0 / n) * np.cos(th.T)
    SIb = (2.0 / n) * np.sin(th.T)

    def bd(B):
        out = np.zeros((1024, 1024), np.float64)
        for r in range(nrep):
            out[r * L:(r + 1) * L, r * L:(r + 1) * L] = B
        return out.astype(ml_dtypes.bfloat16)
    t = np.arange(L, dtype=np.float32) / np.float32(L)
    bands = np.arange(1, 17, dtype=np.float32)
    ang = (2.0 * np.float32(math.pi)) * t[:, None] * bands[None, :]
    feats = np.concatenate([t[:, None], np.cos(ang), np.sin(ang)], axis=-1).astype(np.float32)
    featsT = np.tile(feats.T, (1, nrep)).astype(np.float32)
    MIN_DECAY = math.log(1e-2) / 1.5
    MAX_DECAY = math.log(1e-2) / 0.3
    deltas = np.abs(np.linspace(MIN_DECAY, MAX_DECAY, 1024, dtype=np.float32))
    win = np.exp(-t[:, None] * deltas[None, :]).astype(np.float32)
    winb = win * (np.arange(L) > 0).astype(np.float32)[:, None]
    w2 = np.stack([win, winb], axis=1)
    w2 = np.tile(w2, (nrep, 1, 1)).astype(np.float32)
    sumw = np.zeros((128, 8, 128), np.float32)
    sumw[:, 0:L // 128, :] = 1.0
    sumw = sumw.reshape(128, 1024).astype(ml_dtypes.bfloat16)
    return dict(CT=bd(CTb), ST=bd(STb), CI=bd(CIb), SI=bd(SIb), featsT=featsT, win=w2, sumw=sumw)


def _rope_tables():
    T = 1024
    GRID_W = 64
    ROWS = T // GRID_W
    row = np.repeat(np.arange(ROWS), GRID_W).astype(np.float32)
    col = np.tile(np.arange(GRID_W), ROWS).astype(np.float32)
    half = 64
    freqs = (np.float32(10000.0) ** (-np.arange(0, half, 2, dtype=np.float32) / np.float32(half))).astype(np.float32)
    ang = np.concatenate([row[:, None] * freqs[None, :], col[:, None] * freqs[None, :]], axis=-1)
    cos = np.cos(ang).astype(np.float32)
    sin = np.sin(ang).astype(np.float32)
    cosF = np.repeat(cos.T, 2, axis=0)
    sinF = np.repeat(sin.T, 2, axis=0)
    return np.ascontiguousarray(cosF), np.ascontiguousarray(sinF)


def _static():
    if "st" in _CACHE:
        return _CACHE["st"]
    ident = np.eye(128, dtype=np.float32)
    rotT = np.zeros((128, 128), np.float32)
    for i in range(64):
        rotT[2 * i + 1, 2 * i] = -1.0
        rotT[2 * i, 2 * i + 1] = 1.0
    cosF, sinF = _rope_tables()
    samp = _consts(1024, 1)
    prm = _consts(256, 4)
    maskb_s = np.zeros((128, 48), np.float32)
    maskb_p = np.full((128, 48), -30000.0, np.float32)
    for kc in range(8):
        maskb_p[:, kc * 4 + kc // 2] = 0.0
    cst_s = np.zeros((128, 8), np.float32)
    cst_s[:, 0] = 1.0
    cst_s[:, 1] = EPS
    cst_p = cst_s.copy()
    cst_p[:, 0] = 0.0
    st = dict(ident=ident, rotT=rotT,
              samp=dict(samp, cosF=cosF, sinF=sinF, maskb=maskb_s, cst=cst_s),
              prm=dict(prm, cosF=np.ones_like(cosF), sinF=np.zeros_like(sinF), maskb=maskb_p, cst=cst_p))
    _CACHE["st"] = st
    return st


def _pvec(cvec, mod_b, norm_mix, norm_ffn, final_norm, hy_conv_w, hy_conv_b, at_q_norm, at_k_norm, hy_f_b1, hy_f_freq, hy_f_b2):
    pv = np.zeros((NROW, 128), np.float32)
    pv[R_CVEC:R_CVEC + 8] = cvec.reshape(8, 128)
    pv[R_MODB:R_MODB + 192] = mod_b.reshape(4 * 48, 128)
    pv[R_NMIX:R_NMIX + 32] = norm_mix.reshape(32, 128)
    pv[R_NFFN:R_NFFN + 32] = norm_ffn.reshape(32, 128)
    pv[R_FIN:R_FIN + 8] = final_norm.reshape(8, 128)
    pv[R_CONVW:R_CONVW + 144] = hy_conv_w.reshape(2 * 3 * 24, 128)
    pv[R_CONVB:R_CONVB + 48] = hy_conv_b.reshape(48, 128)
    pv[R_QN:R_QN + 2] = at_q_norm
    pv[R_KN:R_KN + 2] = at_k_norm
    pv[R_FB1:R_FB1 + 2, 0:64] = hy_f_b1
    pv[R_FREQ:R_FREQ + 4, 0:64] = hy_f_freq.reshape(4, 64)
    pv[R_FB2:R_FB2 + 2, 0:64] = hy_f_b2
    return pv


def kernel(x_prompt, x_sample, cache_k, cache_v, c, c_ctx,
           mod_w, mod_b, norm_mix, norm_ffn,
           hy_w_in, hy_conv_w, hy_conv_b, hy_f_w1, hy_f_b1, hy_f_freq, hy_f_w2, hy_f_b2, hy_f_w3,
           hy_bias, hy_w_out,
           at_w_qkv, at_q_norm, at_k_norm, at_w_out,
           ffn_w_gate, ffn_w_up, ffn_w_down, final_norm, _nsteps=None):
    nsteps = NSTEPS if _nsteps is None else _nsteps
    f = lambda a: np.ascontiguousarray(np.asarray(a, dtype=np.float32))
    x_prompt, x_sample, cache_k, cache_v, c, c_ctx = map(f, (x_prompt, x_sample, cache_k, cache_v, c, c_ctx))
    st = _static()
    key = ("nc", nsteps)
    if key not in _CACHE:
        _CACHE[key] = build(nsteps)[0]
    nc = _CACHE[key]
    shared = dict(mod_w=f(mod_w), hy_w_in=f(hy_w_in), hy_w_out=f(hy_w_out), at_w_qkv=f(at_w_qkv), at_w_out=f(at_w_out),
                  ffn_wg=f(ffn_w_gate), ffn_wu=f(ffn_w_up), ffn_wd=f(ffn_w_down), f_w1=f(hy_f_w1), f_w2=f(hy_f_w2), f_w3=f(hy_f_w3),
                  hy_bias=f(hy_bias), ident=st["ident"], rotT=st["rotT"])
    pargs = [f(a) for a in (mod_b, norm_mix, norm_ffn, final_norm, hy_conv_w, hy_conv_b, at_q_norm, at_k_norm, hy_f_b1, hy_f_freq, hy_f_b2)]
    in_maps = []
    zero_cache = np.zeros((2, 512, 256), np.float32)
    for r in range(8):
        m = dict(shared)
        if r < 4:
            m["x"] = x_prompt[4 * r:4 * r + 4].reshape(1024, 1024)
            cv = c_ctx
            tb = st["prm"]
            m["cache_k"] = zero_cache
            m["cache_v"] = zero_cache
        else:
            b = r - 4
            m["x"] = x_sample[b]
            cv = c[b]
            tb = st["samp"]
            m["cache_k"] = np.ascontiguousarray(cache_k[b].reshape(2, 512, 256))
            m["cache_v"] = np.ascontiguousarray(cache_v[b].reshape(2, 512, 256))
        m["pvec"] = _pvec(cv, *pargs)
        for k in ("cosF", "sinF", "maskb", "cst", "featsT", "win", "sumw", "CT", "ST", "CI", "SI"):
            m[k] = tb[k]
        in_maps.append(m)
    res = run_bass_kernel_spmd(nc, in_maps, core_ids=list(range(8)))
    outs = res.results
    y_prompt = np.stack([outs[r]["y"] for r in range(4)], 0).reshape(16, 256, 1024).astype(np.float32)
    y_sample = np.stack([outs[r]["y"] for r in range(4, 8)], 0).reshape(4, 1024, 1024).astype(np.float32)
    nk = np.stack([outs[r]["new_k"] for r in range(4)], 0).reshape(4, 2, 4, 256, 2, 128).transpose(0, 2, 1, 3, 4, 5).reshape(16, 2, 256, 2, 128)
    nv = np.stack([outs[r]["new_v"] for r in range(4)], 0).reshape(4, 2, 4, 256, 2, 128).transpose(0, 2, 1, 3, 4, 5).reshape(16, 2, 256, 2, 128)
    return (y_prompt, y_sample, np.ascontiguousarray(nk, dtype=np.float32), np.ascontiguousarray(nv, dtype=np.float32))
```

```python
from contextlib import ExitStack
import concourse.bass as bass
import concourse.mybir as mybir

F32 = mybir.dt.float32
BF16 = mybir.dt.bfloat16
ACT = mybir.ActivationFunctionType
ALU = mybir.AluOpType

SAME_ENG_SYNC = True


import types


def _freeze(fn):
    cl = fn.__closure__
    if not cl:
        return fn
    cells = []
    for c in cl:
        try:
            cells.append(types.CellType(c.cell_contents))
        except ValueError:
            cells.append(c)
    g = types.FunctionType(fn.__code__, fn.__globals__, fn.__name__, fn.__defaults__, tuple(cells))
    g.__kwdefaults__ = fn.__kwdefaults__
    return g


class Tok:
    __slots__ = ("name", "w", "readers", "dma_w", "sem", "dma_r", "rsem", "n_dma_w", "n_dma_r")

    def __init__(self, name):
        self.name = name
        self.w = None
        self.readers = []
        self.dma_w = []
        self.sem = None
        self.rsem = None
        self.n_dma_w = 0
        self.n_dma_r = 0


class Op:
    __slots__ = ("eng", "fn", "deps", "dmadeps", "signal", "signo", "is_dma", "dma_sem", "dma_val", "idx")

    def __init__(self, eng, fn):
        self.eng = eng
        self.fn = _freeze(fn)
        self.deps = set()
        self.dmadeps = {}
        self.signal = False
        self.signo = 0
        self.is_dma = False
        self.dma_sem = None
        self.dma_val = 0


class Prog:
    ENGS = ("tensor", "vector", "scalar", "gpsimd", "sync")

    def __init__(self, nc):
        self.nc = nc
        self.es = ExitStack()
        self.ops = []
        self.toks = {}
        self.final_waits = {}
        self.nsem = 0

    def sb(self, name, shape, dt):
        return self.es.enter_context(self.nc.sbuf_tensor(name, shape, dt))

    def ps(self, name, shape, dt):
        return self.es.enter_context(self.nc.psum_tensor(name, shape, dt))

    def newsem(self, name):
        self.nsem += 1
        return self.es.enter_context(self.nc.semaphore(name))

    def tok(self, key):
        t = self.toks.get(key)
        if t is None:
            t = Tok(key)
            self.toks[key] = t
        return t

    def _toks(self, lst):
        out = []
        for k in lst:
            out.append(k if isinstance(k, Tok) else self.tok(k))
        return out

    def op(self, eng, fn, reads=(), writes=()):
        o = Op(eng, fn)
        o.idx = len(self.ops)
        for t in self._toks(reads):
            if t.w is not None:
                o.deps.add(t.w)
            for d in t.dma_w:
                o.dmadeps[d.dma_sem] = max(o.dmadeps.get(d.dma_sem, 0), d.dma_val)
            t.readers = [r for r in t.readers if r.is_dma or r.eng != o.eng] + [o]
        for t in self._toks(writes):
            if t.w is not None:
                o.deps.add(t.w)
            for d in t.dma_w:
                o.dmadeps[d.dma_sem] = max(o.dmadeps.get(d.dma_sem, 0), d.dma_val)
            for r in t.readers:
                if r is o:
                    continue
                if r.is_dma:
                    o.dmadeps[r.dma_sem] = max(o.dmadeps.get(r.dma_sem, 0), r.dma_val)
                else:
                    o.deps.add(r)
            t.w = o
            t.readers = []
            t.dma_w = []
        self.ops.append(o)
        return o

    def dma_in(self, eng, fn, dst, reads=()):
        t = self._toks([dst])[0]
        o = Op(eng, fn)
        o.idx = len(self.ops)
        o.is_dma = True
        if t.sem is None:
            t.sem = {}
        if eng not in t.sem:
            t.sem[eng] = [self.newsem("d_" + eng[0] + str(t.name)), 0]
        if t.w is not None:
            o.deps.add(t.w)
        for r in t.readers:
            if r.is_dma:
                o.dmadeps[r.dma_sem] = max(o.dmadeps.get(r.dma_sem, 0), r.dma_val)
            else:
                o.deps.add(r)
        t.sem[eng][1] += 1
        o.dma_sem = t.sem[eng][0]
        o.dma_val = 16 * t.sem[eng][1]
        t.w = None
        t.readers = []
        t.dma_w = [d for d in t.dma_w if d.dma_sem is not o.dma_sem] + [o]
        self.ops.append(o)
        return o

    def dma_out(self, eng, fn, src):
        t = self._toks([src])[0]
        o = Op(eng, fn)
        o.idx = len(self.ops)
        o.is_dma = True
        if t.rsem is None:
            t.rsem = self.newsem("r_" + str(t.name))
        if t.w is not None:
            o.deps.add(t.w)
        for d in t.dma_w:
            o.dmadeps[d.dma_sem] = max(o.dmadeps.get(d.dma_sem, 0), d.dma_val)
        t.n_dma_r += 1
        o.dma_sem = t.rsem
        o.dma_val = 16 * t.n_dma_r
        t.readers.append(o)
        self.final_waits[t.rsem] = o.dma_val
        self.ops.append(o)
        return o

    def emit(self):
        nc = self.nc
        esem = {e: self.newsem("e_" + e) for e in self.ENGS}
        for o in self.ops:
            for d in o.deps:
                if d.eng == o.eng and (o.eng == "tensor" or not SAME_ENG_SYNC):
                    continue
                d.signal = True
        cnt = {e: 0 for e in self.ENGS}
        per = {e: [] for e in self.ENGS}
        for o in self.ops:
            if o.signal and not o.is_dma:
                cnt[o.eng] += 1
                o.signo = cnt[o.eng]
            per[o.eng].append(o)
        final_waits = self.final_waits
        stats = {e: [0, 0] for e in self.ENGS}

        def run(engname, eng):
            seen = {}
            for o in per[engname]:
                waits = {}
                for d in o.deps:
                    if d.eng == engname and (engname == "tensor" or not SAME_ENG_SYNC):
                        continue
                    s = esem[d.eng]
                    waits[s] = max(waits.get(s, 0), d.signo)
                for s, v in o.dmadeps.items():
                    waits[s] = max(waits.get(s, 0), v)
                for s, v in waits.items():
                    if seen.get(s, 0) >= v:
                        continue
                    eng.wait_ge(s, v)
                    seen[s] = v
                    stats[engname][1] += 1
                ins = o.fn(eng)
                stats[engname][0] += 1
                if o.is_dma:
                    ins.then_inc(o.dma_sem, 16)
                elif o.signal:
                    ins.then_inc(esem[engname], 1)
            if engname == "sync":
                for s, v in final_waits.items():
                    eng.wait_ge(s, v)

        with nc.Block() as block:
            @block.tensor
            def _(e):
                run("tensor", e)

            @block.vector
            def _(e):
                run("vector", e)

            @block.scalar
            def _(e):
                run("scalar", e)

            @block.gpsimd
            def _(e):
                run("gpsimd", e)

            @block.sync
            def _(e):
                run("sync", e)
        self.stats = stats
        self.es.close()


import math
import numpy as np
import ml_dtypes
from concourse.bass_utils import run_bass_kernel_spmd

NLAYER = 4
EPS = 1e-6
D_FF = 2816
NFF = 22
SM_SCALE = 128 ** -0.5

R_CVEC = 0
R_MODB = 8
R_NMIX = 200
R_NFFN = 232
R_FIN = 264
R_CONVW = 272
R_CONVB = 416
R_QN = 464
R_KN = 466
R_FB1 = 468
R_FREQ = 470
R_FB2 = 474
NROW = 512


def build(nsteps=8):
    nc = bass.Bass("TRN2", target_bir_lowering=False)

    def DI(name, shape, dt=F32):
        return nc.dram_tensor(name, shape, dt, kind="ExternalInput").ap()

    x_d = DI("x", [1024, 1024])
    pvec_d = DI("pvec", [NROW, 128])
    mod_w = DI("mod_w", [4, 1024, 6144])
    hy_w_in = DI("hy_w_in", [2, 1024, 3072])
    hy_w_out = DI("hy_w_out", [2, 1024, 1024])
    at_w_qkv = DI("at_w_qkv", [2, 1024, 1536])
    at_w_out = DI("at_w_out", [2, 1024, 1024])
    ffn_wg = DI("ffn_wg", [4, 1024, D_FF])
    ffn_wu = DI("ffn_wu", [4, 1024, D_FF])
    ffn_wd = DI("ffn_wd", [4, D_FF, 1024])
    f_w1 = DI("f_w1", [2, 33, 64])
    f_w2 = DI("f_w2", [2, 64, 64])
    f_w3 = DI("f_w3", [2, 64, 2048])
    hy_bias = DI("hy_bias", [2, 1024])
    cache_k = DI("cache_k", [2, 512, 256])
    cache_v = DI("cache_v", [2, 512, 256])
    ident_d = DI("ident", [128, 128])
    rotT_d = DI("rotT", [128, 128])
    cosF_d = DI("cosF", [128, 1024])
    sinF_d = DI("sinF", [128, 1024])
    maskb_d = DI("maskb", [128, 48])
    cst_d = DI("cst", [128, 8])
    featsT_d = DI("featsT", [33, 1024])
    win_d = DI("win", [1024, 2, 1024])
    sumw_d = DI("sumw", [128, 1024], BF16)
    CT_d = DI("CT", [1024, 1024], BF16)
    ST_d = DI("ST", [1024, 1024], BF16)
    CI_d = DI("CI", [1024, 1024], BF16)
    SI_d = DI("SI", [1024, 1024], BF16)

    y_d = nc.dram_tensor("y", [1024, 1024], F32, kind="ExternalOutput").ap()
    nk_d = nc.dram_tensor("new_k", [2, 1024, 256], F32, kind="ExternalOutput").ap()
    nv_d = nc.dram_tensor("new_v", [2, 1024, 256], F32, kind="ExternalOutput").ap()

    P = Prog(nc)
    Y = P.sb("Y", [128, 8, 1024], F32)
    H = P.sb("H", [128, 8, 1024], BF16)
    BIG = P.sb("BIG", [128, 40, 1024], BF16)
    WS = [P.sb(f"WS{i}", [128, 4096], BF16) for i in range(4)]
    PV = P.sb("PV", [128, 512], F32)
    MODA = [P.sb(f"MODA{i}", [128, 48], F32) for i in range(2)]
    AG = [P.sb(f"AG{i}", [128, 16], F32) for i in range(2)]
    ident = P.sb("identf", [128, 128], F32)
    identb = P.sb("identb", [128, 128], BF16)
    rotT = P.sb("rotTs", [128, 128], F32)
    onesb = P.sb("onesb", [128, 128], BF16)
    onesf = P.sb("onesf", [128, 128], F32)
    maskb = P.sb("maskbs", [128, 48], F32)
    cst = P.sb("csts", [128, 8], F32)
    scT = P.sb("scT", [128, 8], BF16)
    RSB = P.sb("RSB", [128, 1024], F32)
    RS = [RSB[:, 0:512], RSB[:, 512:1024]]
    LN = P.sb("LNt", [128, 512], F32)
    SQ = [P.sb(f"SQ{i}", [128, 512], BF16) for i in range(3)]
    TM = [P.sb(f"TM{i}", [128, 1032], F32) for i in range(4)]
    EB = [P.sb(f"EB{i}", [128, 512], BF16) for i in range(4)]
    UT = P.sb("UTt", [128, 1024], BF16)
    BIASB = P.sb("BIASB", [128, 1024], BF16)
    MS = [P.sb(f"MS{i}", [128, 8, 128], BF16) for i in range(2)]
    PB = [P.ps(f"PB{i}", [128, 512], F32) for i in range(8)]

    def pb(i):
        return ("pb", i)

    def bg(g):
        return ("B", g)

    def Yt(c, tt):
        return ("Y", c, tt)

    def Ht(c, tt):
        return ("H", c, tt)

    ws_state = {"n": 0}

    def wslot():
        s = ws_state["n"] % 4
        ws_state["n"] += 1
        return s

    def wload_cast(src_view, a, b):
        s = wslot()
        v = WS[s][:, 0:a * b].rearrange("p (a b) -> p a b", a=a)
        P.dma_in("gpsimd", lambda e: e.dma_start(out=v, in_=src_view), ("ws", s))
        return v, ("ws", s)

    def wload_bf(src_view, a, b):
        s = wslot()
        v = WS[s][:, 0:a * b].rearrange("p (a b) -> p a b", a=a)
        P.dma_in("sync", lambda e: e.dma_start(out=v, in_=src_view), ("ws", s))
        return v, ("ws", s)

    def wload_f32(src_view, a, b):
        s = wslot()
        v = WS[s][:].bitcast(F32)[:, 0:a * b].rearrange("p (a b) -> p a b", a=a)
        P.dma_in("sync", lambda e: e.dma_start(out=v, in_=src_view), ("ws", s))
        return v, ("ws", s)

    def kview(w2d):
        return w2d.rearrange("(k p) n -> p k n", p=128)

    def V_(fn, reads, writes):
        return P.op("vector", fn, reads, writes)

    def A_(fn, reads, writes):
        return P.op("scalar", fn, reads, writes)

    def T_(fn, reads, writes):
        return P.op("tensor", fn, reads, writes)

    def G_(fn, reads, writes):
        return P.op("gpsimd", fn, reads, writes)

    def mm(out, lhsT, rhs, start, stop, reads, wtok):
        T_(lambda e: e.matmul(out, lhsT, rhs, start=start, stop=stop), reads, [wtok])

    P.dma_in("sync", lambda e: e.dma_start(out=ident[:], in_=ident_d), "ident")
    P.dma_in("sync", lambda e: e.dma_start(out=rotT[:], in_=rotT_d), "rotT")
    P.dma_in("sync", lambda e: e.dma_start(out=maskb[:], in_=maskb_d), "maskb")
    P.dma_in("sync", lambda e: e.dma_start(out=cst[:], in_=cst_d), "cst")
    V_(lambda e: e.memset(onesb[:], 1.0), [], ["onesb"])
    V_(lambda e: e.memset(onesf[:], 1.0), [], ["onesf"])
    V_(lambda e: e.tensor_copy(out=identb[:], in_=ident[:]), ["ident"], ["identb"])
    pvs = TM[0][:, 0:512].rearrange("p (j d) -> p j d", j=4)
    P.dma_in("sync", lambda e: e.dma_start(out=pvs, in_=pvec_d.rearrange("(j p) d -> p j d", p=128)), "TM0")
    for j in range(4):
        T_(lambda e, j=j: e.transpose(PB[0][:, j * 128:(j + 1) * 128], pvs[:, j, :], ident[:]), ["TM0", "ident"], [pb(0)])
    V_(lambda e: e.tensor_copy(out=PV[:], in_=PB[0][:]), [pb(0)], ["PV"])
    A_(lambda e: e.activation(out=scT[:], in_=PV[:, R_CVEC:R_CVEC + 8], func=ACT.Silu), ["PV"], ["scT"])

    BIGflat = BIG[:].rearrange("p a b -> p (a b)")
    BIGF = BIGflat.bitcast(F32)
    XS = BIGF[:, 0:8192].rearrange("p (j d) -> p j d", j=8)
    for j in range(8):
        P.dma_in("sync", lambda e, j=j: e.dma_start(out=XS[:, j, :], in_=x_d[j * 128:(j + 1) * 128, :]), bg(2 * j))
    nb = 0
    for c in range(8):
        for tt in range(2):
            b = nb % 4
            nb += 1
            for j4 in range(4):
                j = tt * 4 + j4
                T_(lambda e, b=b, j4=j4, j=j, c=c: e.transpose(PB[b][:, j4 * 128:(j4 + 1) * 128], XS[:, j, c * 128:(c + 1) * 128], ident[:]),
                   [bg(2 * j), bg(2 * j + 1), "ident"], [pb(b)])
            if (c + tt) % 2 == 0:
                V_(lambda e, b=b, c=c, tt=tt: e.tensor_copy(out=Y[:, c, tt * 512:(tt + 1) * 512], in_=PB[b][:]), [pb(b)], [Yt(c, tt)])
            else:
                A_(lambda e, b=b, c=c, tt=tt: e.activation(out=Y[:, c, tt * 512:(tt + 1) * 512], in_=PB[b][:], func=ACT.Copy), [pb(b)], [Yt(c, tt)])

    def mod_tasks(i):
        par = i % 2
        tasks = []

        def blk_task(blk):
            def run():
                v, tk = wload_cast(kview(mod_w[i])[:, :, blk * 512:(blk + 1) * 512], 8, 512)
                for n4 in range(4):
                    n = blk * 4 + n4
                    for k in range(8):
                        mm(PB[7][:, n:n + 1], v[:, k, n4 * 128:(n4 + 1) * 128], scT[:, k:k + 1], k == 0, k == 7, [tk, "scT"], pb(7))
                if blk == 11:
                    V_(lambda e: e.tensor_tensor(out=MODA[par][:], in0=PB[7][:, 0:48], in1=PV[:, R_MODB + i * 48:R_MODB + (i + 1) * 48], op=ALU.add),
                       [pb(7), "PV"], [("MODA", par)])
                    V_(lambda e: e.scalar_tensor_tensor(out=AG[par][:, 0:8], in0=MODA[par][:, 8:16], scalar=1.0, in1=PV[:, R_NMIX + i * 8:R_NMIX + i * 8 + 8], op0=ALU.add, op1=ALU.mult),
                       [("MODA", par), "PV"], [("AG", par)])
                    V_(lambda e: e.scalar_tensor_tensor(out=AG[par][:, 8:16], in0=MODA[par][:, 32:40], scalar=1.0, in1=PV[:, R_NFFN + i * 8:R_NFFN + i * 8 + 8], op0=ALU.add, op1=ALU.mult),
                       [("MODA", par), "PV"], [("AG", par)])
            return run
        for blk in range(12):
            tasks.append(blk_task(blk))
        return tasks

    ms_state = {"n": 0}

    def mod_tasks_small(i):
        par = i % 2
        tasks = []

        def mk(n):
            def run():
                sl = ms_state["n"] % 2
                ms_state["n"] += 1
                v = MS[sl]
                tk = ("ms", sl)
                P.dma_in("gpsimd", lambda e: e.dma_start(out=v[:], in_=kview(mod_w[i])[:, :, n * 128:(n + 1) * 128]), tk)
                for k in range(8):
                    mm(PB[7][:, n:n + 1], v[:, k, :], scT[:, k:k + 1], k == 0, k == 7, [tk, "scT"], pb(7))
                if n == 47:
                    V_(lambda e: e.tensor_tensor(out=MODA[par][:], in0=PB[7][:, 0:48], in1=PV[:, R_MODB + i * 48:R_MODB + (i + 1) * 48], op=ALU.add),
                       [pb(7), "PV"], [("MODA", par)])
                    V_(lambda e: e.scalar_tensor_tensor(out=AG[par][:, 0:8], in0=MODA[par][:, 8:16], scalar=1.0, in1=PV[:, R_NMIX + i * 8:R_NMIX + i * 8 + 8], op0=ALU.add, op1=ALU.mult),
                       [("MODA", par), "PV"], [("AG", par)])
                    V_(lambda e: e.scalar_tensor_tensor(out=AG[par][:, 8:16], in0=MODA[par][:, 32:40], scalar=1.0, in1=PV[:, R_NFFN + i * 8:R_NFFN + i * 8 + 8], op0=ALU.add, op1=ALU.mult),
                       [("MODA", par), "PV"], [("AG", par)])
            return run
        for n in range(48):
            tasks.append(mk(n))
        return tasks

    pend_tasks = []

    def pump(k):
        for _ in range(k):
            if pend_tasks:
                pend_tasks.pop(0)()

    def norm_mod(agcol, shcol, par, tiles=(0, 1)):
        for tt in tiles:
            b = 6
            for c in range(8):
                s = SQ[c % 3]
                A_(lambda e, s=s, c=c, tt=tt: e.activation(out=s[:], in_=Y[:, c, tt * 512:(tt + 1) * 512], func=ACT.Square), [Yt(c, tt)], [("SQ", c % 3)])
                mm(PB[b][:], onesb[:], s[:], c == 0, c == 7, [("SQ", c % 3), "onesb"], pb(b))
            A_(lambda e: e.activation(out=LN[:], in_=PB[b][:], func=ACT.Ln, scale=1.0 / 1024, bias=cst[:, 1:2]), [pb(b), "cst"], ["LN"])
            A_(lambda e, tt=tt: e.activation(out=RS[tt][:], in_=LN[:], func=ACT.Exp, scale=-0.5), ["LN"], [("RS", tt)])
            for c in range(8):
                tm = TM[2 + c % 2]
                V_(lambda e, tm=tm, c=c, tt=tt: e.scalar_tensor_tensor(out=tm[:, 0:512], in0=Y[:, c, tt * 512:(tt + 1) * 512], scalar=AG[par][:, agcol + c:agcol + c + 1],
                                                                     in1=RS[tt][:], op0=ALU.mult, op1=ALU.mult),
                   [Yt(c, tt), ("AG", par), ("RS", tt)], [f"TM{2 + c % 2}"])
                A_(lambda e, tm=tm, c=c, tt=tt: e.activation(out=H[:, c, tt * 512:(tt + 1) * 512], in_=tm[:, 0:512], func=ACT.Identity,
                                                             bias=MODA[par][:, shcol + c:shcol + c + 1], scale=1.0),
                   [f"TM{2 + c % 2}", ("MODA", par)], [Ht(c, tt)])

    def out_proj(w2d, src_view, src_toks, gcol, par, mid=None):
        vs = [wload_cast(kview(w2d)[:, :, half * 512:(half + 1) * 512], 8, 512) for half in range(2)]
        nb = 0
        for tt in range(2):
            for n in range(8):
                if mid is not None and tt == 1 and n == 2:
                    mid()
                v, tk = vs[n // 4]
                n4 = n % 4
                b = nb % 4
                nb += 1
                for k in range(8):
                    mm(PB[b][:], v[:, k, n4 * 128:(n4 + 1) * 128], src_view(k, tt), k == 0, k == 7, [tk, src_toks(k, tt)], pb(b))
                V_(lambda e, b=b, n=n, tt=tt: e.scalar_tensor_tensor(out=Y[:, n, tt * 512:(tt + 1) * 512], in0=PB[b][:], scalar=MODA[par][:, gcol + n:gcol + n + 1],
                                                                     in1=Y[:, n, tt * 512:(tt + 1) * 512], op0=ALU.mult, op1=ALU.add),
                   [pb(b), ("MODA", par), Yt(n, tt)], [Yt(n, tt)])

    def ffn(i, par, inter_tasks, skip0=False, mid_next=None):
        groups = [(0, 512), (512, 512), (1024, 512), (1536, 512), (2048, 512), (2560, 256)]
        pre = {0: (wload_cast(kview(ffn_wg[i])[:, :, 0:512], 8, 512), wload_cast(kview(ffn_wu[i])[:, :, 0:512], 8, 512))}
        norm_mod(8, 24, par, tiles=((1,) if skip0 else (0, 1)))
        Abuf = BIG
        nb = 0
        for (c0, cw) in groups:
            if c0 in pre:
                (vg, tg), (vu, tu) = pre[c0]
            else:
                vg, tg = wload_cast(kview(ffn_wg[i])[:, :, c0:c0 + cw], 8, cw)
                vu, tu = wload_cast(kview(ffn_wu[i])[:, :, c0:c0 + cw], 8, cw)
            for tt in range(2):
                for f4 in range(cw // 128):
                    f = c0 // 128 + f4
                    bgt = nb % 2
                    but = 2 + nb % 2
                    nb += 1
                    for k in range(8):
                        mm(PB[bgt][:], vg[:, k, f4 * 128:(f4 + 1) * 128], H[:, k, tt * 512:(tt + 1) * 512], k == 0, k == 7, [tg, Ht(k, tt)], pb(bgt))
                    for k in range(8):
                        mm(PB[but][:], vu[:, k, f4 * 128:(f4 + 1) * 128], H[:, k, tt * 512:(tt + 1) * 512], k == 0, k == 7, [tu, Ht(k, tt)], pb(but))
                    sg = TM[2 + nb % 2]
                    A_(lambda e, sg=sg, bgt=bgt: e.activation(out=sg[:, 0:512], in_=PB[bgt][:], func=ACT.Silu), [pb(bgt)], [f"TM{2 + nb % 2}"])
                    V_(lambda e, sg=sg, but=but, f=f, tt=tt: e.tensor_tensor(out=Abuf[:, f, tt * 512:(tt + 1) * 512], in0=PB[but][:], in1=sg[:, 0:512], op=ALU.mult),
                       [pb(but), f"TM{2 + nb % 2}"], [bg(f)])
        nb = 0

        def down_one(v, tk, n, tt):
            nonlocal nb
            b = 4 + nb % 2
            nb += 1
            for f in range(NFF):
                mm(PB[b][:], v[:, f, :], Abuf[:, f, tt * 512:(tt + 1) * 512], f == 0, f == NFF - 1, [tk, bg(f)], pb(b))
            V_(lambda e: e.scalar_tensor_tensor(out=Y[:, n, tt * 512:(tt + 1) * 512], in0=PB[b][:], scalar=MODA[par][:, 40 + n:41 + n],
                                                in1=Y[:, n, tt * 512:(tt + 1) * 512], op0=ALU.mult, op1=ALU.add),
               [pb(b), ("MODA", par), Yt(n, tt)], [Yt(n, tt)])

        for n in range(4):
            v, tk = wload_cast(kview(ffn_wd[i])[:, :, n * 128:(n + 1) * 128], 22, 128)
            for tt in range(2):
                down_one(v, tk, n, tt)
        vs = [wload_cast(kview(ffn_wd[i])[:, :, n * 128:(n + 1) * 128], 22, 128) for n in range(4, 8)]
        for tt in range(2):
            for n in range(4, 8):
                if mid_next is not None and tt == 1 and n == 5:
                    mid_next()
                down_one(vs[n - 4][0], vs[n - 4][1], n, tt)

    def filter_gen(j, inter):
        ft = TM[0]
        P.dma_in("sync", lambda e: e.dma_start(out=ft[0:33, 0:1024], in_=featsT_d), "TM0")
        w1s = TM[3]
        P.dma_in("sync", lambda e: e.dma_start(out=w1s[0:33, 0:64], in_=f_w1[j]), "TM3")
        P.dma_in("sync", lambda e: e.dma_start(out=w1s[0:64, 64:128], in_=f_w2[j]), "TM3")
        P.dma_in("gpsimd", lambda e: e.dma_start(out=BIASB[:], in_=hy_bias[j:j + 1, :].partition_broadcast(128)), "BIASB")
        w3b = BIGflat[:, 32 * 1024:34 * 1024]
        w3t = bg(32)
        P.dma_in("gpsimd", lambda e: e.dma_start(out=w3b[0:64, :], in_=f_w3[j]), w3t)

        def sin_layer(src, lhsT, K, bcol, fcol, dst, srctok, dsttok):
            for tt in range(2):
                b = tt
                mm(PB[b][0:64, :], lhsT, src[0:K, tt * 512:(tt + 1) * 512], True, True, ["TM3", srctok], pb(b))
                V_(lambda e, b=b, tt=tt: e.tensor_scalar(out=dst[0:64, tt * 512:(tt + 1) * 512], in0=PB[b][0:64, :], scalar1=PV[0:64, bcol:bcol + 1], scalar2=PV[0:64, fcol:fcol + 1],
                                                         op0=ALU.add, op1=ALU.mult), [pb(b), "PV"], [dsttok])
            A_(lambda e: e.activation(out=dst[0:64, 0:1024], in_=dst[0:64, 0:1024], func=ACT.Sin, scale=1.0 / 3.0), [dsttok], [dsttok])
            s2 = TM[2]
            V_(lambda e: e.tensor_tensor(out=s2[0:64, 0:1024], in0=dst[0:64, 0:1024], in1=dst[0:64, 0:1024], op=ALU.mult), [dsttok], ["TM2"])
            V_(lambda e: e.tensor_scalar(out=s2[0:64, 0:1024], in0=s2[0:64, 0:1024], scalar1=-4.0, scalar2=3.0, op0=ALU.mult, op1=ALU.add), ["TM2"], ["TM2"])
            V_(lambda e: e.tensor_tensor(out=dst[0:64, 0:1024], in0=dst[0:64, 0:1024], in1=s2[0:64, 0:1024], op=ALU.mult), [dsttok, "TM2"], [dsttok])

        h1 = TM[1]
        sin_layer(ft, w1s[0:33, 0:64], 33, R_FB1 + j, R_FREQ + 2 * j, h1, "TM0", "TM1")
        h2 = TM[0]
        sin_layer(h1, w1s[0:64, 64:128], 64, R_FB2 + j, R_FREQ + 2 * j + 1, h2, "TM1", "TM0")
        for tt in range(2):
            V_(lambda e, tt=tt: e.tensor_copy(out=EB[tt][0:64, :], in_=h2[0:64, tt * 512:(tt + 1) * 512]), ["TM0"], [("EB", tt)])
        sumw = UT
        P.dma_in("sync", lambda e: e.dma_start(out=sumw[:], in_=sumw_d), "UT")
        wins = {}
        fbuf = [(TM[2][:, 0:512], "TM2", TM[3][:, 0:512], "TM3", SQ[0], ("SQ", 0), SQ[1], ("SQ", 1)),
                (ZV[:, 0:512], "ZV", ZV[:, 512:1024], "ZV", SQ[2], ("SQ", 2), EB[2], ("EB", 2))]

        def fA(it):
            tc, ct = it // 2, it % 2
            if ct == 0:
                wins[tc] = wload_f32(win_d[tc * 128:(tc + 1) * 128, :, :], 2, 1024)
            wv, wt = wins[tc]
            hf, hft, hb, hbt, sq0, sq0t, sq1, sq1t = fbuf[it % 2]
            bf_, bb_ = 2 * (it % 2), 2 * (it % 2) + 1
            h2b = EB[tc // 4][0:64, (tc % 4) * 128:(tc % 4 + 1) * 128]
            mm(PB[bf_][:], h2b, w3b[0:64, ct * 512:(ct + 1) * 512], True, True, [("EB", tc // 4), w3t, bg(33)], pb(bf_))
            mm(PB[bb_][:], h2b, w3b[0:64, 1024 + ct * 512:1024 + (ct + 1) * 512], True, True, [("EB", tc // 4), w3t, bg(33)], pb(bb_))
            V_(lambda e: e.tensor_tensor(out=hf, in0=PB[bf_][:], in1=wv[:, 0, ct * 512:(ct + 1) * 512], op=ALU.mult), [pb(bf_), wt], [hft])
            V_(lambda e: e.tensor_tensor(out=hb, in0=PB[bb_][:], in1=wv[:, 1, ct * 512:(ct + 1) * 512], op=ALU.mult), [pb(bb_), wt], [hbt])
            A_(lambda e: e.activation(out=sq0[:], in_=hf, func=ACT.Square), [hft], [sq0t])
            A_(lambda e: e.activation(out=sq1[:], in_=hb, func=ACT.Square), [hbt], [sq1t])
            V_(lambda e: e.tensor_tensor(out=BIG[:, 16 + tc, ct * 512:(ct + 1) * 512], in0=hf, in1=hb, op=ALU.add), [hft, hbt], [bg(16 + tc)])
            V_(lambda e: e.tensor_tensor(out=BIG[:, 24 + tc, ct * 512:(ct + 1) * 512], in0=hb, in1=hf, op=ALU.subtract), [hft, hbt], [bg(24 + tc)])

        def fB(it):
            tc, ct = it // 2, it % 2
            hf, hft, hb, hbt, sq0, sq0t, sq1, sq1t = fbuf[it % 2]
            bn = 4 + ct
            mm(PB[bn][:], sumw[:, tc * 128:(tc + 1) * 128], sq0[:], tc == 0, False, ["UT", sq0t], pb(bn))
            mm(PB[bn][:], sumw[:, tc * 128:(tc + 1) * 128], sq1[:], False, tc == 7, ["UT", sq1t], pb(bn))

        for it in range(17):
            if it < 16:
                fA(it)
            if it >= 1:
                fB(it - 1)
            if it % 2 == 1 and inter:
                inter.pop(0)()
        RN = TM[1]
        for ct in range(2):
            A_(lambda e, ct=ct: e.activation(out=LN[:], in_=PB[4 + ct][:], func=ACT.Ln, scale=1.0, bias=cst[:, 1:2]), [pb(4 + ct), "cst"], ["LN"])
            A_(lambda e, ct=ct: e.activation(out=RN[:, ct * 512:(ct + 1) * 512], in_=LN[:], func=ACT.Exp, scale=-0.5), ["LN"], ["TM1"])

        while inter:
            inter.pop(0)()

    def hyena(i, j, par, prefilt=False, skip0=False, ffn_next=False):
        X0 = BIG
        nslot = {}

        def get_w(n):
            s = n // 4
            if s not in nslot:
                nslot[s] = wload_cast(kview(hy_w_in[j])[:, :, s * 512:(s + 1) * 512], 8, 512)
            return nslot[s]
        norm_mod(0, 0, par, tiles=((1,) if skip0 else (0, 1)))
        if not prefilt:
            filter_gen(j, [])
        RN = TM[1]
        wv3 = []
        for s in range(6):
            wv3.append(None)
        PSB = [TM[2], TM[3]]
        X1 = TM[0]
        npb = 0
        def conv_chunk(n, dst_fn, dst_tok, idx):
            v, tk = get_w(n)
            n4 = n % 4
            psb = PSB[idx % 2]
            ptok = f"TM{2 + idx % 2}"
            pv = psb[:, 0:1032].rearrange("p (b w) -> p b w", b=4)
            for tt in range(2):
                b = idx % 2 * 2 + tt
                for k in range(8):
                    mm(PB[b][:], v[:, k, n4 * 128:(n4 + 1) * 128], H[:, k, tt * 512:(tt + 1) * 512], k == 0, k == 7, [tk, Ht(k, tt)], pb(b))
                A_(lambda e, b=b, tt=tt: e.activation(out=pv[:, 2 * tt:2 * tt + 2, 1:257], in_=PB[b][:].rearrange("p (b w) -> p b w", b=2), func=ACT.Copy), [pb(b)], [ptok])
                A_(lambda e, b=b, tt=tt: e.activation(out=RSB[:, tt * 512:(tt + 1) * 512].rearrange("p (b w) -> p b w", b=2), in_=PB[b][:].rearrange("p (b w) -> p b w", b=2),
                                                     func=ACT.Identity, scale=PV[:, R_CONVW + j * 72 + 24 + n:R_CONVW + j * 72 + 25 + n],
                                                     bias=PV[:, R_CONVB + j * 24 + n:R_CONVB + j * 24 + n + 1]), [pb(b), "PV"], [("RS", tt)])
            V_(lambda e: e.memset(pv[:, 0:1, 0:1], 0.0), [], [ptok])
            V_(lambda e: e.memset(pv[:, 3:4, 257:258], 0.0), [], [ptok])
            V_(lambda e: e.tensor_scalar(out=pv[:, 1:4, 0:1], in0=pv[:, 0:3, 256:257], scalar1=cst[:, 0:1], scalar2=None, op0=ALU.mult), [ptok, "cst"], [ptok])
            V_(lambda e: e.tensor_scalar(out=pv[:, 0:3, 257:258], in0=pv[:, 1:4, 1:2], scalar1=cst[:, 0:1], scalar2=None, op0=ALU.mult), [ptok, "cst"], [ptok])
            cw = R_CONVW + j * 72
            acc = TM[1] if False else None
            dst = dst_fn()
            accv = ACC[:, 0:1024].rearrange("p (b w) -> p b w", b=4)
            V_(lambda e: e.scalar_tensor_tensor(out=accv, in0=pv[:, :, 0:256], scalar=PV[:, cw + n:cw + n + 1], in1=accv, op0=ALU.mult, op1=ALU.add), [ptok, "PV", ("RS", 0), ("RS", 1)], [("RS", 0), ("RS", 1)])
            V_(lambda e: e.scalar_tensor_tensor(out=dst, in0=pv[:, :, 2:258], scalar=PV[:, cw + 48 + n:cw + 49 + n], in1=accv, op0=ALU.mult, op1=ALU.add), [ptok, "PV", ("RS", 0), ("RS", 1)], dst_tok)
            pump(1)

        ACC = RSB
        idx = 0
        pendT = None

        def mkT(c):
            def run():
                bT = 4 + c % 2
                pbf = PB[bT][:].bitcast(BF16)
                for tcn in range(8):
                    T_(lambda e, tcn=tcn: e.transpose(pbf[:, tcn * 128:(tcn + 1) * 128], UT[:, tcn * 128:(tcn + 1) * 128], identb[:]), ["UT", "identb"], [pb(bT)])
                A_(lambda e: e.activation(out=BIG[:, 8:16, c * 128:(c + 1) * 128], in_=pbf.rearrange("p (j w) -> p j w", j=8), func=ACT.Copy),
                   [pb(bT)], [bg(8 + q) for q in range(8)])
            return run

        for c in range(8):
            conv_chunk(c, lambda c=c: X0[:, c, :].rearrange("p (b w) -> p b w", b=4), [bg(c)], idx); idx += 1
            if pendT is not None:
                pendT()
                pendT = None
            conv_chunk(8 + c, lambda: X1[:, 0:1024].rearrange("p (b w) -> p b w", b=4), ["TM0"], idx); idx += 1
            conv_chunk(16 + c, lambda: ZV[:, 0:1024].rearrange("p (b w) -> p b w", b=4), ["ZV"], idx); idx += 1
            V_(lambda e: e.tensor_tensor(out=UT[:], in0=ZV[:, 0:1024], in1=X1[:, 0:1024], op=ALU.mult), ["ZV", "TM0"], ["UT"])
            pendT = mkT(c)
        pendT()

        YR = H
        cslots = {}

        def get_ct(fc):
            s = fc // 4
            if s not in cslots:
                a = wload_bf(kview(CT_d)[:, :, s * 512:(s + 1) * 512], 8, 512)
                b_ = wload_bf(kview(ST_d)[:, :, s * 512:(s + 1) * 512], 8, 512)
                cslots[s] = (a, b_)
            return cslots[s]

        it = 0
        for fc in range(8):
            (cv, ctk), (sv, stk) = get_ct(fc)
            f4 = fc % 4
            for ct in range(2):
                bA, bB, bP, bQ = (0, 1, 2, 3) if it % 2 == 0 else (4, 5, 6, 3)
                it += 1
                cs = slice(ct * 512, (ct + 1) * 512)
                for m in range(8):
                    lc = cv[:, m, f4 * 128:(f4 + 1) * 128]
                    ls = sv[:, m, f4 * 128:(f4 + 1) * 128]
                    mm(PB[bA][:], lc, BIG[:, 8 + m, cs], m == 0, m == 7, [ctk, bg(8 + m)], pb(bA))
                    mm(PB[bP][:], lc, BIG[:, 16 + m, cs], m == 0, m == 7, [ctk, bg(16 + m)], pb(bP))
                    mm(PB[bB][:], ls, BIG[:, 8 + m, cs], m == 0, m == 7, [stk, bg(8 + m)], pb(bB))
                    mm(PB[bQ][:], ls, BIG[:, 24 + m, cs], m == 0, m == 7, [stk, bg(24 + m)], pb(bQ))
                Pn = TM[2]
                Qn = TM[3]
                V_(lambda e, bP=bP, cs=cs: e.tensor_tensor(out=Pn[:, 0:512], in0=PB[bP][:], in1=RN[:, cs], op=ALU.mult), [pb(bP), "TM1"], ["TM2"])
                V_(lambda e, cs=cs: e.tensor_tensor(out=Pn[:, 0:512], in0=Pn[:, 0:512], in1=BIASB[:, cs], op=ALU.add), ["TM2", "BIASB"], ["TM2"])
                V_(lambda e, bQ=bQ, cs=cs: e.tensor_tensor(out=Qn[:, 0:512], in0=PB[bQ][:], in1=RN[:, cs], op=ALU.mult), [pb(bQ), "TM1"], ["TM3"])
                t1 = TM[0]
                V_(lambda e, bA=bA: e.tensor_tensor(out=t1[:, 0:512], in0=PB[bA][:], in1=Pn[:, 0:512], op=ALU.mult), [pb(bA), "TM2"], ["TM0"])
                V_(lambda e, bB=bB: e.tensor_tensor(out=t1[:, 512:1024], in0=PB[bB][:], in1=Qn[:, 0:512], op=ALU.mult), [pb(bB), "TM3"], ["TM0"])
                V_(lambda e, fc=fc, cs=cs: e.tensor_tensor(out=YR[:, fc, cs], in0=t1[:, 0:512], in1=t1[:, 512:1024], op=ALU.add), ["TM0"], [Ht(fc, ct)])
                t2 = ZV
                V_(lambda e, bB=bB: e.tensor_tensor(out=t2[:, 0:512], in0=PB[bB][:], in1=Pn[:, 0:512], op=ALU.mult), [pb(bB), "TM2"], ["ZV"])
                V_(lambda e, bA=bA: e.tensor_tensor(out=t2[:, 512:1024], in0=PB[bA][:], in1=Qn[:, 0:512], op=ALU.mult), [pb(bA), "TM3"], ["ZV"])
                V_(lambda e, fc=fc, cs=cs: e.tensor_tensor(out=BIG[:, 32 + fc, cs], in0=t2[:, 0:512], in1=t2[:, 512:1024], op=ALU.subtract), ["ZV"], [bg(32 + fc)])
                pump(2)

        nb = 0
        for tt in range(2):
            (civ, citk) = wload_bf(kview(CI_d)[:, :, tt * 512:(tt + 1) * 512], 8, 512)
            (siv, sitk) = wload_bf(kview(SI_d)[:, :, tt * 512:(tt + 1) * 512], 8, 512)
            for c in range(8):
                b = nb % 4
                nb += 1
                for fc in range(8):
                    mm(PB[b][:], YR[:, fc, c * 128:(c + 1) * 128], civ[:, fc, :], fc == 0, False, [citk, Ht(fc, c // 4)], pb(b))
                    mm(PB[b][:], BIG[:, 32 + fc, c * 128:(c + 1) * 128], siv[:, fc, :], False, fc == 7, [sitk, bg(32 + fc)], pb(b))
                V_(lambda e, b=b, c=c, tt=tt: e.tensor_tensor(out=BIG[:, 8 + c, tt * 512:(tt + 1) * 512], in0=PB[b][:], in1=X0[:, c, tt * 512:(tt + 1) * 512], op=ALU.mult),
                   [pb(b), bg(c)], [bg(8 + c)])
        out_proj(hy_w_out[j], lambda k, tt: BIG[:, 8 + k, tt * 512:(tt + 1) * 512], lambda k, tt: bg(8 + k), 16, par,
                 mid=(lambda: norm_mod(8, 24, par, tiles=(0,))) if ffn_next else None)

    def attention(i, j, par, skip0=False, ffn_next=False):
        wq = [wload_cast(kview(at_w_qkv[j])[:, :, s * 512:(s + 1) * 512], 8, 512) for s in range(3)]
        norm_mod(0, 0, par, tiles=((1,) if skip0 else (0, 1)))
        QT = BIG
        KT = BIGflat[:, 16 * 1024:19 * 1024].rearrange("p (k s) -> p k s", k=2)
        VV = BIGflat[:, 19 * 1024:22 * 1024].rearrange("p (k e) -> p k e", k=12)
        ktoks = [bg(16), bg(17), bg(18)]
        vtoks = [bg(19), bg(20), bg(21)]
        CS = BIGF[:, 24 * 512:28 * 512]
        P.dma_in("sync", lambda e: e.dma_start(out=CS[:, 0:1024], in_=cosF_d), bg(24))
        P.dma_in("sync", lambda e: e.dma_start(out=CS[:, 1024:2048], in_=sinF_d), bg(26))
        cstoks = [bg(24), bg(26)]
        NK = BIGF[:, 28 * 512:32 * 512].rearrange("p (t e) -> p t e", t=8)
        NV = BIGF[:, 32 * 512:36 * 512].rearrange("p (t e) -> p t e", t=8)
        nktoks = [bg(28 + q) for q in range(4)]
        nvtoks = [bg(32 + q) for q in range(4)]
        ckl = TM[3][:, 0:1024].rearrange("p (c e) -> p c e", c=4)
        P.dma_in("sync", lambda e: e.dma_start(out=ckl, in_=cache_k[j].rearrange("(c p) e -> p c e", p=128)), "TM3")
        P.dma_in("gpsimd", lambda e: e.dma_start(out=VV[:, 8:12, :], in_=cache_v[j].rearrange("(c p) e -> p c e", p=128)), bg(21))
        items = [(tt, n) for tt in range(2) for n in range(10)]
        NI = len(items)

        def stageA(it):
            tt, n = items[it]
            v, tk = wq[n // 4]
            n4 = n % 4
            b = it % 3
            for k in range(8):
                mm(PB[b][:], v[:, k, n4 * 128:(n4 + 1) * 128], H[:, k, tt * 512:(tt + 1) * 512], k == 0, k == 7, [tk, Ht(k, tt)], pb(b))
            s = SQ[it % 3]
            stok = ("SQ", it % 3)
            A_(lambda e: e.activation(out=s[:], in_=PB[b][:], func=ACT.Square), [pb(b)], [stok])

        def stageB(it):
            tt, n = items[it]
            gcol = (R_QN + j) if n < 8 else (R_KN + j)
            b = it % 3
            bs = 3 + it % 2
            s = SQ[it % 3]
            stok = ("SQ", it % 3)
            mm(PB[bs][:], onesb[:], s[:], True, True, [stok, "onesb"], pb(bs))
            A_(lambda e: e.activation(out=LN[:], in_=PB[bs][:], func=ACT.Ln, scale=1.0 / 128, bias=cst[:, 1:2]), [pb(bs), "cst"], ["LN"])
            rs = RS[it % 2]
            A_(lambda e: e.activation(out=rs[:], in_=LN[:], func=ACT.Exp, scale=-0.5), ["LN"], [("RS", it % 2)])
            qn = TM[it % 2]
            qtok = f"TM{it % 2}"
            V_(lambda e: e.scalar_tensor_tensor(out=qn[:, 0:512], in0=PB[b][:], scalar=PV[:, gcol:gcol + 1], in1=rs[:], op0=ALU.mult, op1=ALU.mult),
               [pb(b), "PV", ("RS", it % 2)], [qtok])

        def stageC(it):
            tt, n = items[it]
            br = 5
            qn = TM[it % 2]
            qtok = f"TM{it % 2}"
            mm(PB[br][:], rotT[:], qn[:, 0:512], True, True, ["rotT", qtok], pb(br))
            t1 = ZV
            V_(lambda e: e.tensor_tensor(out=t1[:, 0:512], in0=qn[:, 0:512], in1=CS[:, tt * 512:(tt + 1) * 512], op=ALU.mult), [qtok] + cstoks, ["ZV"])
            V_(lambda e: e.tensor_tensor(out=t1[:, 512:1024], in0=PB[br][:], in1=CS[:, 1024 + tt * 512:1024 + (tt + 1) * 512], op=ALU.mult), [pb(br)] + cstoks, ["ZV"])
            if n < 8:
                V_(lambda e: e.tensor_tensor(out=QT[:, n, tt * 512:(tt + 1) * 512], in0=t1[:, 0:512], in1=t1[:, 512:1024], op=ALU.add), ["ZV"], [bg(n)])
            else:
                kv = n - 8
                V_(lambda e: e.tensor_tensor(out=KT[:, kv, tt * 512:(tt + 1) * 512], in0=t1[:, 0:512], in1=t1[:, 512:1024], op=ALU.add), ["ZV"], ktoks)
                bt = 6
                for q in range(4):
                    T_(lambda e, q=q: e.transpose(PB[bt][:, q * 128:(q + 1) * 128], qn[:, q * 128:(q + 1) * 128], ident[:]), [qtok, "ident"], [pb(bt)])
                A_(lambda e: e.activation(out=NK[:, tt * 4:(tt + 1) * 4, kv * 128:(kv + 1) * 128], in_=PB[bt][:].rearrange("p (q d) -> p q d", q=4), func=ACT.Copy),
                   [pb(bt)], nktoks)

        for step in range(NI + 2):
            if step < NI:
                stageA(step)
            if 0 <= step - 1 < NI:
                stageB(step - 1)
            if 0 <= step - 2 < NI:
                stageC(step - 2)
            pump(1)
        for kv in range(2):
            b = kv
            for cc in range(4):
                T_(lambda e, b=b, cc=cc, kv=kv: e.transpose(PB[b][:, cc * 128:(cc + 1) * 128], ckl[:, cc, kv * 128:(kv + 1) * 128], ident[:]), ["TM3", "ident"], [pb(b)])
            V_(lambda e, b=b, kv=kv: e.tensor_copy(out=KT[:, kv, 1024:1536], in_=PB[b][:]), [pb(b)], ktoks)
        v2, tk2 = wq[2]
        for tcn in range(8):
            b = 6 if tcn % 2 == 0 else 4
            for k in range(8):
                mm(PB[b][:, 0:256], H[:, k, tcn * 128:(tcn + 1) * 128], v2[:, k, 256:512], k == 0, k == 7, [tk2, Ht(k, tcn // 4)], pb(b))
            A_(lambda e, b=b, tcn=tcn: e.activation(out=NV[:, tcn, :], in_=PB[b][:, 0:256], func=ACT.Copy), [pb(b)], nvtoks)
            V_(lambda e, b=b, tcn=tcn: e.tensor_copy(out=VV[:, tcn, :], in_=NV[:, tcn, :]), nvtoks, vtoks)
        jj = j
        P.dma_out("sync", lambda e: e.dma_start(out=nk_d[jj].rearrange("(t p) e -> p t e", p=128), in_=NK), nktoks[0])
        P.dma_out("sync", lambda e: e.dma_start(out=nv_d[jj].rearrange("(t p) e -> p t e", p=128), in_=NV), nvtoks[0])
        for tks, oidx in ((nktoks, -2), (nvtoks, -1)):
            o = P.ops[oidx]
            for t in tks[1:]:
                tt_ = P.tok(t)
                if tt_.w is not None:
                    o.deps.add(tt_.w)
                tt_.readers.append(o)

        sb_i = 0
        acc_i = 0
        for kv in range(2):
            for g in range(4):
                h = kv * 4 + g
                for tt in range(2):
                    bO = 4 + acc_i % 2
                    bS = 3 if acc_i % 2 == 0 else 6
                    acc_i += 1
                    qs = slice(tt * 512, (tt + 1) * 512)
                    pend = []

                    def s_step(kc):
                        nonlocal sb_i
                        b = sb_i % 3
                        eb = EB[sb_i % 4]
                        etok = ("EB", sb_i % 4)
                        sb_i += 1
                        mm(PB[b][:], KT[:, kv, kc * 128:(kc + 1) * 128], QT[:, h, qs], True, True, ktoks + [bg(h)], pb(b))
                        for half in range(2):
                            qb = tt * 2 + half
                            A_(lambda e, b=b, eb=eb, half=half, qb=qb, kc=kc: e.activation(out=eb[:, half * 256:(half + 1) * 256], in_=PB[b][:, half * 256:(half + 1) * 256], func=ACT.Exp,
                                                                                         scale=SM_SCALE, bias=maskb[:, kc * 4 + qb:kc * 4 + qb + 1]), [pb(b), "maskb"], [etok])
                        return (kc, eb, etok)

                    def pv_step(kc, eb, etok):
                        mm(PB[bO][:], VV[:, kc, kv * 128:(kv + 1) * 128], eb[:], kc == 0, kc == 11, vtoks + [etok], pb(bO))
                        mm(PB[bS][:], onesb[:], eb[:], kc == 0, kc == 11, ["onesb", etok], pb(bS))

                    for kc in range(12):
                        pend.append(s_step(kc))
                        if len(pend) > 2:
                            pv_step(*pend.pop(0))
                    while pend:
                        pv_step(*pend.pop(0))
                    rsum = TM[acc_i % 2]
                    rtok = f"TM{acc_i % 2}"
                    V_(lambda e, rsum=rsum, bS=bS: e.reciprocal(out=rsum[:, 0:512], in_=PB[bS][:]), [pb(bS)], [rtok])
                    V_(lambda e, rsum=rsum, bO=bO, h=h, qs=qs: e.tensor_tensor(out=BIG[:, 8 + h, qs], in0=PB[bO][:], in1=rsum[:, 0:512], op=ALU.mult), [pb(bO), rtok], [bg(8 + h)])
                    pump(2)
        out_proj(at_w_out[j], lambda k, tt: BIG[:, 8 + k, tt * 512:(tt + 1) * 512], lambda k, tt: bg(8 + k), 16, par,
                 mid=(lambda: norm_mod(8, 24, par, tiles=(0,))) if ffn_next else None)

    ZV = P.sb("ZVt", [128, 1024], F32)
    if nsteps > 0:
        filter_gen(0, mod_tasks(0))
    else:
        for t in mod_tasks(0):
            t()
    mixer_skip0 = False
    for i in range(NLAYER):
        par = i % 2
        j = i // 2
        need_next = (i + 1 < NLAYER) and (2 * (i + 1) < nsteps)
        if need_next:
            pend_tasks.extend(mod_tasks_small(i + 1))
        ffn_runs = 2 * i + 1 < nsteps
        if 2 * i < nsteps:
            if i % 2 == 0:
                hyena(i, j, par, prefilt=(i == 0), skip0=mixer_skip0, ffn_next=ffn_runs)
            else:
                attention(i, j, par, skip0=mixer_skip0, ffn_next=ffn_runs)
        pump(1000)
        mixer_skip0 = False
        if ffn_runs:
            nxt = None
            if need_next:
                pn = (i + 1) % 2
                nxt = (lambda pn=pn: norm_mod(0, 0, pn, tiles=(0,)))
                mixer_skip0 = True
            ffn(i, par, [], skip0=True, mid_next=nxt)

    OS = BIGF[:, 0:8192].rearrange("p (j d) -> p j d", j=8)
    for tt in range(2):
        b = 6
        for c in range(8):
            s = SQ[c % 3]
            A_(lambda e, s=s, c=c, tt=tt: e.activation(out=s[:], in_=Y[:, c, tt * 512:(tt + 1) * 512], func=ACT.Square), [Yt(c, tt)], [("SQ", c % 3)])
            mm(PB[b][:], onesb[:], s[:], c == 0, c == 7, [("SQ", c % 3), "onesb"], pb(b))
        A_(lambda e: e.activation(out=LN[:], in_=PB[b][:], func=ACT.Ln, scale=1.0 / 1024, bias=cst[:, 1:2]), [pb(b), "cst"], ["LN"])
        A_(lambda e, tt=tt: e.activation(out=RS[tt][:], in_=LN[:], func=ACT.Exp, scale=-0.5), ["LN"], [("RS", tt)])
        for c in range(8):
            V_(lambda e, c=c, tt=tt: e.scalar_tensor_tensor(out=Y[:, c, tt * 512:(tt + 1) * 512], in0=Y[:, c, tt * 512:(tt + 1) * 512], scalar=PV[:, R_FIN + c:R_FIN + c + 1],
                                                            in1=RS[tt][:], op0=ALU.mult, op1=ALU.mult), [Yt(c, tt), "PV", ("RS", tt)], [Yt(c, tt)])
    nb = 0
    for tcn in range(8):
        for half in range(2):
            b = nb % 4
            nb += 1
            for c4 in range(4):
                c = half * 4 + c4
                T_(lambda e, b=b, c4=c4, c=c, tcn=tcn: e.transpose(PB[b][:, c4 * 128:(c4 + 1) * 128], Y[:, c, tcn * 128:(tcn + 1) * 128], ident[:]), [Yt(c, tcn // 4), "ident"], [pb(b)])
            if nb % 2 == 0:
                V_(lambda e, b=b, tcn=tcn, half=half: e.tensor_copy(out=OS[:, tcn, half * 512:(half + 1) * 512], in_=PB[b][:]), [pb(b)], [bg(2 * tcn + half)])
            else:
                A_(lambda e, b=b, tcn=tcn, half=half: e.activation(out=OS[:, tcn, half * 512:(half + 1) * 512], in_=PB[b][:], func=ACT.Copy), [pb(b)], [bg(2 * tcn + half)])
        P.dma_out("sync", lambda e, tcn=tcn: e.dma_start(out=y_d[tcn * 128:(tcn + 1) * 128, :], in_=OS[:, tcn, :]), bg(2 * tcn))
        o = P.ops[-1]
        t2_ = P.tok(bg(2 * tcn + 1))
        if t2_.w is not None:
            o.deps.add(t2_.w)
        t2_.readers.append(o)
    P.emit()
    return nc, P


NSTEPS = 8
_CACHE = {}


def _consts(L, nrep):
    n = 2 * L
    f = np.arange(L, dtype=np.float64)
    m = np.arange(L, dtype=np.float64)
    th = np.pi * (2.0 * f[None, :] + 1.0) * m[:, None] / n
    CTb = np.cos(th)
    STb = np.sin(th)
    CIb = (2.0 / n) * np.cos(th.T)
    SIb = (2.0 / n) * np.sin(th.T)

    def bd(B):
        out = np.zeros((1024, 1024), np.float64)
        for r in range(nrep):
            out[r * L:(r + 1) * L, r * L:(r + 1) * L] = B
        return out.astype(ml_dtypes.bfloat16)
    t = np.arange(L, dtype=np.float32) / np.float32(L)
    bands = np.arange(1, 17, dtype=np.float32)
    ang = (2.0 * np.float32(math.pi)) * t[:, None] * bands[None, :]
    feats = np.concatenate([t[:, None], np.cos(ang), np.sin(ang)], axis=-1).astype(np.float32)
    featsT = np.tile(feats.T, (1, nrep)).astype(np.float32)
    MIN_DECAY = math.log(1e-2) / 1.5
    MAX_DECAY = math.log(1e-2) / 0.3
    deltas = np.abs(np.linspace(MIN_DECAY, MAX_DECAY, 1024, dtype=np.float32))
    win = np.exp(-t[:, None] * deltas[None, :]).astype(np.float32)
    winb = win * (np.arange(L) > 0).astype(np.float32)[:, None]
    w2 = np.stack([win, winb], axis=1)
    w2 = np.tile(w2, (nrep, 1, 1)).astype(np.float32)
    sumw = np.zeros((128, 8, 128), np.float32)
    sumw[:, 0:L // 128, :] = 1.0
    sumw = sumw.reshape(128, 1024).astype(ml_dtypes.bfloat16)
    return dict(CT=bd(CTb), ST=bd(STb), CI=bd(CIb), SI=bd(SIb), featsT=featsT, win=w2, sumw=sumw)


def _rope_tables():
    T = 1024
    GRID_W = 64
    ROWS = T // GRID_W
    row = np.repeat(np.arange(ROWS), GRID_W).astype(np.float32)
    col = np.tile(np.arange(GRID_W), ROWS).astype(np.float32)
    half = 64
    freqs = (np.float32(10000.0) ** (-np.arange(0, half, 2, dtype=np.float32) / np.float32(half))).astype(np.float32)
    ang = np.concatenate([row[:, None] * freqs[None, :], col[:, None] * freqs[None, :]], axis=-1)
    cos = np.cos(ang).astype(np.float32)
    sin = np.sin(ang).astype(np.float32)
    cosF = np.repeat(cos.T, 2, axis=0)
    sinF = np.repeat(sin.T, 2, axis=0)
    return np.ascontiguousarray(cosF), np.ascontiguousarray(sinF)


def _static():
    if "st" in _CACHE:
        return _CACHE["st"]
    ident = np.eye(128, dtype=np.float32)
    rotT = np.zeros((128, 128), np.float32)
    for i in range(64):
        rotT[2 * i + 1, 2 * i] = -1.0
        rotT[2 * i, 2 * i + 1] = 1.0
    cosF, sinF = _rope_tables()
    samp = _consts(1024, 1)
    prm = _consts(256, 4)
    maskb_s = np.zeros((128, 48), np.float32)
    maskb_p = np.full((128, 48), -30000.0, np.float32)
    for kc in range(8):
        maskb_p[:, kc * 4 + kc // 2] = 0.0
    cst_s = np.zeros((128, 8), np.float32)
    cst_s[:, 0] = 1.0
    cst_s[:, 1] = EPS
    cst_p = cst_s.copy()
    cst_p[:, 0] = 0.0
    st = dict(ident=ident, rotT=rotT,
              samp=dict(samp, cosF=cosF, sinF=sinF, maskb=maskb_s, cst=cst_s),
              prm=dict(prm, cosF=np.ones_like(cosF), sinF=np.zeros_like(sinF), maskb=maskb_p, cst=cst_p))
    _CACHE["st"] = st
    return st


def _pvec(cvec, mod_b, norm_mix, norm_ffn, final_norm, hy_conv_w, hy_conv_b, at_q_norm, at_k_norm, hy_f_b1, hy_f_freq, hy_f_b2):
    pv = np.zeros((NROW, 128), np.float32)
    pv[R_CVEC:R_CVEC + 8] = cvec.reshape(8, 128)
    pv[R_MODB:R_MODB + 192] = mod_b.reshape(4 * 48, 128)
    pv[R_NMIX:R_NMIX + 32] = norm_mix.reshape(32, 128)
    pv[R_NFFN:R_NFFN + 32] = norm_ffn.reshape(32, 128)
    pv[R_FIN:R_FIN + 8] = final_norm.reshape(8, 128)
    pv[R_CONVW:R_CONVW + 144] = hy_conv_w.reshape(2 * 3 * 24, 128)
    pv[R_CONVB:R_CONVB + 48] = hy_conv_b.reshape(48, 128)
    pv[R_QN:R_QN + 2] = at_q_norm
    pv[R_KN:R_KN + 2] = at_k_norm
    pv[R_FB1:R_FB1 + 2, 0:64] = hy_f_b1
    pv[R_FREQ:R_FREQ + 4, 0:64] = hy_f_freq.reshape(4, 64)
    pv[R_FB2:R_FB2 + 2, 0:64] = hy_f_b2
    return pv


def kernel(x_prompt, x_sample, cache_k, cache_v, c, c_ctx,
           mod_w, mod_b, norm_mix, norm_ffn,
           hy_w_in, hy_conv_w, hy_conv_b, hy_f_w1, hy_f_b1, hy_f_freq, hy_f_w2, hy_f_b2, hy_f_w3,
           hy_bias, hy_w_out,
           at_w_qkv, at_q_norm, at_k_norm, at_w_out,
           ffn_w_gate, ffn_w_up, ffn_w_down, final_norm, _nsteps=None):
    nsteps = NSTEPS if _nsteps is None else _nsteps
    f = lambda a: np.ascontiguousarray(np.asarray(a, dtype=np.float32))
    x_prompt, x_sample, cache_k, cache_v, c, c_ctx = map(f, (x_prompt, x_sample, cache_k, cache_v, c, c_ctx))
    st = _static()
    key = ("nc", nsteps)
    if key not in _CACHE:
        _CACHE[key] = build(nsteps)[0]
    nc = _CACHE[key]
    shared = dict(mod_w=f(mod_w), hy_w_in=f(hy_w_in), hy_w_out=f(hy_w_out), at_w_qkv=f(at_w_qkv), at_w_out=f(at_w_out),
                  ffn_wg=f(ffn_w_gate), ffn_wu=f(ffn_w_up), ffn_wd=f(ffn_w_down), f_w1=f(hy_f_w1), f_w2=f(hy_f_w2), f_w3=f(hy_f_w3),
                  hy_bias=f(hy_bias), ident=st["ident"], rotT=st["rotT"])
    pargs = [f(a) for a in (mod_b, norm_mix, norm_ffn, final_norm, hy_conv_w, hy_conv_b, at_q_norm, at_k_norm, hy_f_b1, hy_f_freq, hy_f_b2)]
    in_maps = []
    zero_cache = np.zeros((2, 512, 256), np.float32)
    for r in range(8):
        m = dict(shared)
        if r < 4:
            m["x"] = x_prompt[4 * r:4 * r + 4].reshape(1024, 1024)
            cv = c_ctx
            tb = st["prm"]
            m["cache_k"] = zero_cache
            m["cache_v"] = zero_cache
        else:
            b = r - 4
            m["x"] = x_sample[b]
            cv = c[b]
            tb = st["samp"]
            m["cache_k"] = np.ascontiguousarray(cache_k[b].reshape(2, 512, 256))
            m["cache_v"] = np.ascontiguousarray(cache_v[b].reshape(2, 512, 256))
        m["pvec"] = _pvec(cv, *pargs)
        for k in ("cosF", "sinF", "maskb", "cst", "featsT", "win", "sumw", "CT", "ST", "CI", "SI"):
            m[k] = tb[k]
        in_maps.append(m)
    res = run_bass_kernel_spmd(nc, in_maps, core_ids=list(range(8)))
    outs = res.results
    y_prompt = np.stack([outs[r]["y"] for r in range(4)], 0).reshape(16, 256, 1024).astype(np.float32)
    y_sample = np.stack([outs[r]["y"] for r in range(4, 8)], 0).reshape(4, 1024, 1024).astype(np.float32)
    nk = np.stack([outs[r]["new_k"] for r in range(4)], 0).reshape(4, 2, 4, 256, 2, 128).transpose(0, 2, 1, 3, 4, 5).reshape(16, 2, 256, 2, 128)
    nv = np.stack([outs[r]["new_v"] for r in range(4)], 0).reshape(4, 2, 4, 256, 2, 128).transpose(0, 2, 1, 3, 4, 5).reshape(16, 2, 256, 2, 128)
    return (y_prompt, y_sample, np.ascontiguousarray(nk, dtype=np.float32), np.ascontiguousarray(nv, dtype=np.float32))
```

```python
from contextlib import ExitStack
import concourse.bass as bass
import concourse.mybir as mybir

F32 = mybir.dt.float32
BF16 = mybir.dt.bfloat16
ACT = mybir.ActivationFunctionType
ALU = mybir.AluOpType

SAME_ENG_SYNC = True


import types


def _freeze(fn):
    cl = fn.__closure__
    if not cl:
        return fn
    cells = []
    for c in cl:
        try:
            cells.append(types.CellType(c.cell_contents))
        except ValueError:
            cells.append(c)
    g = types.FunctionType(fn.__code__, fn.__globals__, fn.__name__, fn.__defaults__, tuple(cells))
    g.__kwdefaults__ = fn.__kwdefaults__
    return g


class Tok:
    __slots__ = ("name", "w", "readers", "dma_w", "sem", "dma_r", "rsem", "n_dma_w", "n_dma_r")

    def __init__(self, name):
        self.name = name
        self.w = None
        self.readers = []
        self.dma_w = []
        self.sem = None
        self.rsem = None
        self.n_dma_w = 0
        self.n_dma_r = 0


class Op:
    __slots__ = ("eng", "fn", "deps", "dmadeps", "signal", "signo", "is_dma", "dma_sem", "dma_val", "idx")

    def __init__(self, eng, fn):
        self.eng = eng
        self.fn = _freeze(fn)
        self.deps = set()
        self.dmadeps = {}
        self.signal = False
        self.signo = 0
        self.is_dma = False
        self.dma_sem = None
        self.dma_val = 0


class Prog:
    ENGS = ("tensor", "vector", "scalar", "gpsimd", "sync")

    def __init__(self, nc):
        self.nc = nc
        self.es = ExitStack()
        self.ops = []
        self.toks = {}
        self.final_waits = {}
        self.nsem = 0

    def sb(self, name, shape, dt):
        return self.es.enter_context(self.nc.sbuf_tensor(name, shape, dt))

    def ps(self, name, shape, dt):
        return self.es.enter_context(self.nc.psum_tensor(name, shape, dt))

    def newsem(self, name):
        self.nsem += 1
        return self.es.enter_context(self.nc.semaphore(name))

    def tok(self, key):
        t = self.toks.get(key)
        if t is None:
            t = Tok(key)
            self.toks[key] = t
        return t

    def _toks(self, lst):
        out = []
        for k in lst:
            out.append(k if isinstance(k, Tok) else self.tok(k))
        return out

    def op(self, eng, fn, reads=(), writes=()):
        o = Op(eng, fn)
        o.idx = len(self.ops)
        for t in self._toks(reads):
            if t.w is not None:
                o.deps.add(t.w)
            for d in t.dma_w:
                o.dmadeps[d.dma_sem] = max(o.dmadeps.get(d.dma_sem, 0), d.dma_val)
            t.readers = [r for r in t.readers if r.is_dma or r.eng != o.eng] + [o]
        for t in self._toks(writes):
            if t.w is not None:
                o.deps.add(t.w)
            for d in t.dma_w:
                o.dmadeps[d.dma_sem] = max(o.dmadeps.get(d.dma_sem, 0), d.dma_val)
            for r in t.readers:
                if r is o:
                    continue
                if r.is_dma:
                    o.dmadeps[r.dma_sem] = max(o.dmadeps.get(r.dma_sem, 0), r.dma_val)
                else:
                    o.deps.add(r)
            t.w = o
            t.readers = []
            t.dma_w = []
        self.ops.append(o)
        return o

    def dma_in(self, eng, fn, dst, reads=()):
        t = self._toks([dst])[0]
        o = Op(eng, fn)
        o.idx = len(self.ops)
        o.is_dma = True
        if t.sem is None:
            t.sem = {}
        if eng not in t.sem:
            t.sem[eng] = [self.newsem("d_" + eng[0] + str(t.name)), 0]
        if t.w is not None:
            o.deps.add(t.w)
        for r in t.readers:
            if r.is_dma:
                o.dmadeps[r.dma_sem] = max(o.dmadeps.get(r.dma_sem, 0), r.dma_val)
            else:
                o.deps.add(r)
        t.sem[eng][1] += 1
        o.dma_sem = t.sem[eng][0]
        o.dma_val = 16 * t.sem[eng][1]
        t.w = None
        t.readers = []
        t.dma_w = [d for d in t.dma_w if d.dma_sem is not o.dma_sem] + [o]
        self.ops.append(o)
        return o

    def dma_out(self, eng, fn, src):
        t = self._toks([src])[0]
        o = Op(eng, fn)
        o.idx = len(self.ops)
        o.is_dma = True
        if t.rsem is None:
            t.rsem = self.newsem("r_" + str(t.name))
        if t.w is not None:
            o.deps.add(t.w)
        for d in t.dma_w:
            o.dmadeps[d.dma_sem] = max(o.dmadeps.get(d.dma_sem, 0), d.dma_val)
        t.n_dma_r += 1
        o.dma_sem = t.rsem
        o.dma_val = 16 * t.n_dma_r
        t.readers.append(o)
        self.final_waits[t.rsem] = o.dma_val
        self.ops.append(o)
        return o

    def emit(self):
        nc = self.nc
        esem = {e: self.newsem("e_" + e) for e in self.ENGS}
        for o in self.ops:
            for d in o.deps:
                if d.eng == o.eng and (o.eng == "tensor" or not SAME_ENG_SYNC):
                    continue
                d.signal = True
        cnt = {e: 0 for e in self.ENGS}
        per = {e: [] for e in self.ENGS}
        for o in self.ops:
            if o.signal and not o.is_dma:
                cnt[o.eng] += 1
                o.signo = cnt[o.eng]
            per[o.eng].append(o)
        final_waits = self.final_waits
        stats = {e: [0, 0] for e in self.ENGS}

        def run(engname, eng):
            seen = {}
            for o in per[engname]:
                waits = {}
                for d in o.deps:
                    if d.eng == engname and (engname == "tensor" or not SAME_ENG_SYNC):
                        continue
                    s = esem[d.eng]
                    waits[s] = max(waits.get(s, 0), d.signo)
                for s, v in o.dmadeps.items():
                    waits[s] = max(waits.get(s, 0), v)
                for s, v in waits.items():
                    if seen.get(s, 0) >= v:
                        continue
                    eng.wait_ge(s, v)
                    seen[s] = v
                    stats[engname][1] += 1
                ins = o.fn(eng)
                stats[engname][0] += 1
                if o.is_dma:
                    ins.then_inc(o.dma_sem, 16)
                elif o.signal:
                    ins.then_inc(esem[engname], 1)
            if engname == "sync":
                for s, v in final_waits.items():
                    eng.wait_ge(s, v)

        with nc.Block() as block:
            @block.tensor
            def _(e):
                run("tensor", e)

            @block.vector
            def _(e):
                run("vector", e)

            @block.scalar
            def _(e):
                run("scalar", e)

            @block.gpsimd
            def _(e):
                run("gpsimd", e)

            @block.sync
            def _(e):
                run("sync", e)
        self.stats = stats
        self.es.close()


import math
import numpy as np
import ml_dtypes
from concourse.bass_utils import run_bass_kernel_spmd

NLAYER = 4
EPS = 1e-6
D_FF = 2816
NFF = 22
SM_SCALE = 128 ** -0.5

R_CVEC = 0
R_MODB = 8
R_NMIX = 200
R_NFFN = 232
R_FIN = 264
R_CONVW = 272
R_CONVB = 416
R_QN = 464
R_KN = 466
R_FB1 = 468
R_FREQ = 470
R_FB2 = 474
NROW = 512


def build(nsteps=8):
    nc = bass.Bass("TRN2", target_bir_lowering=False)

    def DI(name, shape, dt=F32):
        return nc.dram_tensor(name, shape, dt, kind="ExternalInput").ap()

    x_d = DI("x", [1024, 1024])
    pvec_d = DI("pvec", [NROW, 128])
    mod_w = DI("mod_w", [4, 1024, 6144])
    hy_w_in = DI("hy_w_in", [2, 1024, 3072])
    hy_w_out = DI("hy_w_out", [2, 1024, 1024])
    at_w_qkv = DI("at_w_qkv", [2, 1024, 1536])
    at_w_out = DI("at_w_out", [2, 1024, 1024])
    ffn_wg = DI("ffn_wg", [4, 1024, D_FF])
    ffn_wu = DI("ffn_wu", [4, 1024, D_FF])
    ffn_wd = DI("ffn_wd", [4, D_FF, 1024])
    f_w1 = DI("f_w1", [2, 33, 64])
    f_w2 = DI("f_w2", [2, 64, 64])
    f_w3 = DI("f_w3", [2, 64, 2048])
    hy_bias = DI("hy_bias", [2, 1024])
    cache_k = DI("cache_k", [2, 512, 256])
    cache_v = DI("cache_v", [2, 512, 256])
    ident_d = DI("ident", [128, 128])
    rotT_d = DI("rotT", [128, 128])
    cosF_d = DI("cosF", [128, 1024])
    sinF_d = DI("sinF", [128, 1024])
    maskb_d = DI("maskb", [128, 48])
    cst_d = DI("cst", [128, 8])
    featsT_d = DI("featsT", [33, 1024])
    win_d = DI("win", [1024, 2, 1024])
    sumw_d = DI("sumw", [128, 1024], BF16)
    CT_d = DI("CT", [1024, 1024], BF16)
    ST_d = DI("ST", [1024, 1024], BF16)
    CI_d = DI("CI", [1024, 1024], BF16)
    SI_d = DI("SI", [1024, 1024], BF16)

    y_d = nc.dram_tensor("y", [1024, 1024], F32, kind="ExternalOutput").ap()
    nk_d = nc.dram_tensor("new_k", [2, 1024, 256], F32, kind="ExternalOutput").ap()
    nv_d = nc.dram_tensor("new_v", [2, 1024, 256], F32, kind="ExternalOutput").ap()

    P = Prog(nc)
    Y = P.sb("Y", [128, 8, 1024], F32)
    H = P.sb("H", [128, 8, 1024], BF16)
    BIG = P.sb("BIG", [128, 40, 1024], BF16)
    WS = [P.sb(f"WS{i}", [128, 4096], BF16) for i in range(4)]
    PV = P.sb("PV", [128, 512], F32)
    MODA = [P.sb(f"MODA{i}", [128, 48], F32) for i in range(2)]
    AG = [P.sb(f"AG{i}", [128, 16], F32) for i in range(2)]
    ident = P.sb("identf", [128, 128], F32)
    identb = P.sb("identb", [128, 128], BF16)
    rotT = P.sb("rotTs", [128, 128], F32)
    onesb = P.sb("onesb", [128, 128], BF16)
    onesf = P.sb("onesf", [128, 128], F32)
    maskb = P.sb("maskbs", [128, 48], F32)
    cst = P.sb("csts", [128, 8], F32)
    scT = P.sb("scT", [128, 8], BF16)
    RSB = P.sb("RSB", [128, 1024], F32)
    RS = [RSB[:, 0:512], RSB[:, 512:1024]]
    LN = P.sb("LNt", [128, 512], F32)
    SQ = [P.sb(f"SQ{i}", [128, 512], BF16) for i in range(3)]
    TM = [P.sb(f"TM{i}", [128, 1032], F32) for i in range(4)]
    EB = [P.sb(f"EB{i}", [128, 512], BF16) for i in range(4)]
    UT = P.sb("UTt", [128, 1024], BF16)
    BIASB = P.sb("BIASB", [128, 1024], BF16)
    MS = [P.sb(f"MS{i}", [128, 8, 128], BF16) for i in range(2)]
    PB = [P.ps(f"PB{i}", [128, 512], F32) for i in range(8)]

    def pb(i):
        return ("pb", i)

    def bg(g):
        return ("B", g)

    def Yt(c, tt):
        return ("Y", c, tt)

    def Ht(c, tt):
        return ("H", c, tt)

    ws_state = {"n": 0}

    def wslot():
        s = ws_state["n"] % 4
        ws_state["n"] += 1
        return s

    def wload_cast(src_view, a, b):
        s = wslot()
        v = WS[s][:, 0:a * b].rearrange("p (a b) -> p a b", a=a)
        P.dma_in("gpsimd", lambda e: e.dma_start(out=v, in_=src_view), ("ws", s))
        return v, ("ws", s)

    def wload_bf(src_view, a, b):
        s = wslot()
        v = WS[s][:, 0:a * b].rearrange("p (a b) -> p a b", a=a)
        P.dma_in("sync", lambda e: e.dma_start(out=v, in_=src_view), ("ws", s))
        return v, ("ws", s)

    def wload_f32(src_view, a, b):
        s = wslot()
        v = WS[s][:].bitcast(F32)[:, 0:a * b].rearrange("p (a b) -> p a b", a=a)
        P.dma_in("sync", lambda e: e.dma_start(out=v, in_=src_view), ("ws", s))
        return v, ("ws", s)

    def kview(w2d):
        return w2d.rearrange("(k p) n -> p k n", p=128)

    def V_(fn, reads, writes):
        return P.op("vector", fn, reads, writes)

    def A_(fn, reads, writes):
        return P.op("scalar", fn, reads, writes)

    def T_(fn, reads, writes):
        return P.op("tensor", fn, reads, writes)

    def G_(fn, reads, writes):
        return P.op("gpsimd", fn, reads, writes)

    def mm(out, lhsT, rhs, start, stop, reads, wtok):
        T_(lambda e: e.matmul(out, lhsT, rhs, start=start, stop=stop), reads, [wtok])

    P.dma_in("sync", lambda e: e.dma_start(out=ident[:], in_=ident_d), "ident")
    P.dma_in("sync", lambda e: e.dma_start(out=rotT[:], in_=rotT_d), "rotT")
    P.dma_in("sync", lambda e: e.dma_start(out=maskb[:], in_=maskb_d), "maskb")
    P.dma_in("sync", lambda e: e.dma_start(out=cst[:], in_=cst_d), "cst")
    V_(lambda e: e.memset(onesb[:], 1.0), [], ["onesb"])
    V_(lambda e: e.memset(onesf[:], 1.0), [], ["onesf"])
    V_(lambda e: e.tensor_copy(out=identb[:], in_=ident[:]), ["ident"], ["identb"])
    pvs = TM[0][:, 0:512].rearrange("p (j d) -> p j d", j=4)
    P.dma_in("sync", lambda e: e.dma_start(out=pvs, in_=pvec_d.rearrange("(j p) d -> p j d", p=128)), "TM0")
    for j in range(4):
        T_(lambda e, j=j: e.transpose(PB[0][:, j * 128:(j + 1) * 128], pvs[:, j, :], ident[:]), ["TM0", "ident"], [pb(0)])
    V_(lambda e: e.tensor_copy(out=PV[:], in_=PB[0][:]), [pb(0)], ["PV"])
    A_(lambda e: e.activation(out=scT[:], in_=PV[:, R_CVEC:R_CVEC + 8], func=ACT.Silu), ["PV"], ["scT"])

    BIGflat = BIG[:].rearrange("p a b -> p (a b)")
    BIGF = BIGflat.bitcast(F32)
    XS = BIGF[:, 0:8192].rearrange("p (j d) -> p j d", j=8)
    for j in range(8):
        P.dma_in("sync", lambda e, j=j: e.dma_start(out=XS[:, j, :], in_=x_d[j * 128:(j + 1) * 128, :]), bg(2 * j))
    nb = 0
    for c in range(8):
        for tt in range(2):
            b = nb % 4
            nb += 1
            for j4 in range(4):
                j = tt * 4 + j4
                T_(lambda e, b=b, j4=j4, j=j, c=c: e.transpose(PB[b][:, j4 * 128:(j4 + 1) * 128], XS[:, j, c * 128:(c + 1) * 128], ident[:]),
                   [bg(2 * j), bg(2 * j + 1), "ident"], [pb(b)])
            if (c + tt) % 2 == 0:
                V_(lambda e, b=b, c=c, tt=tt: e.tensor_copy(out=Y[:, c, tt * 512:(tt + 1) * 512], in_=PB[b][:]), [pb(b)], [Yt(c, tt)])
            else:
                A_(lambda e, b=b, c=c, tt=tt: e.activation(out=Y[:, c, tt * 512:(tt + 1) * 512], in_=PB[b][:], func=ACT.Copy), [pb(b)], [Yt(c, tt)])

    def mod_tasks(i):
        par = i % 2
        tasks = []

        def blk_task(blk):
            def run():
                v, tk = wload_cast(kview(mod_w[i])[:, :, blk * 512:(blk + 1) * 512], 8, 512)
                for n4 in range(4):
                    n = blk * 4 + n4
                    for k in range(8):
                        mm(PB[7][:, n:n + 1], v[:, k, n4 * 128:(n4 + 1) * 128], scT[:, k:k + 1], k == 0, k == 7, [tk, "scT"], pb(7))
                if blk == 11:
                    V_(lambda e: e.tensor_tensor(out=MODA[par][:], in0=PB[7][:, 0:48], in1=PV[:, R_MODB + i * 48:R_MODB + (i + 1) * 48], op=ALU.add),
                       [pb(7), "PV"], [("MODA", par)])
                    V_(lambda e: e.scalar_tensor_tensor(out=AG[par][:, 0:8], in0=MODA[par][:, 8:16], scalar=1.0, in1=PV[:, R_NMIX + i * 8:R_NMIX + i * 8 + 8], op0=ALU.add, op1=ALU.mult),
                       [("MODA", par), "PV"], [("AG", par)])
                    V_(lambda e: e.scalar_tensor_tensor(out=AG[par][:, 8:16], in0=MODA[par][:, 32:40], scalar=1.0, in1=PV[:, R_NFFN + i * 8:R_NFFN + i * 8 + 8], op0=ALU.add, op1=ALU.mult),
                       [("MODA", par), "PV"], [("AG", par)])
            return run
        for blk in range(12):
            tasks.append(blk_task(blk))
        return tasks

    ms_state = {"n": 0}

    def mod_tasks_small(i):
        par = i % 2
        tasks = []

        def mk(n):
            def run():
                sl = ms_state["n"] % 2
                ms_state["n"] += 1
                v = MS[sl]
                tk = ("ms", sl)
                P.dma_in("gpsimd", lambda e: e.dma_start(out=v[:], in_=kview(mod_w[i])[:, :, n * 128:(n + 1) * 128]), tk)
                for k in range(8):
                    mm(PB[7][:, n:n + 1], v[:, k, :], scT[:, k:k + 1], k == 0, k == 7, [tk, "scT"], pb(7))
                if n == 47:
                    V_(lambda e: e.tensor_tensor(out=MODA[par][:], in0=PB[7][:, 0:48], in1=PV[:, R_MODB + i * 48:R_MODB + (i + 1) * 48], op=ALU.add),
                       [pb(7), "PV"], [("MODA", par)])
                    V_(lambda e: e.scalar_tensor_tensor(out=AG[par][:, 0:8], in0=MODA[par][:, 8:16], scalar=1.0, in1=PV[:, R_NMIX + i * 8:R_NMIX + i * 8 + 8], op0=ALU.add, op1=ALU.mult),
                       [("MODA", par), "PV"], [("AG", par)])
                    V_(lambda e: e.scalar_tensor_tensor(out=AG[par][:, 8:16], in0=MODA[par][:, 32:40], scalar=1.0, in1=PV[:, R_NFFN + i * 8:R_NFFN + i * 8 + 8], op0=ALU.add, op1=ALU.mult),
                       [("MODA", par), "PV"], [("AG", par)])
            return run
        for n in range(48):
            tasks.append(mk(n))
        return tasks

    pend_tasks = []

    def pump(k):
        for _ in range(k):
            if pend_tasks:
                pend_tasks.pop(0)()

    def norm_mod(agcol, shcol, par, tiles=(0, 1)):
        for tt in tiles:
            b = 6
            for c in range(8):
                s = SQ[c % 3]
                A_(lambda e, s=s, c=c, tt=tt: e.activation(out=s[:], in_=Y[:, c, tt * 512:(tt + 1) * 512], func=ACT.Square), [Yt(c, tt)], [("SQ", c % 3)])
                mm(PB[b][:], onesb[:], s[:], c == 0, c == 7, [("SQ", c % 3), "onesb"], pb(b))
            A_(lambda e: e.activation(out=LN[:], in_=PB[b][:], func=ACT.Ln, scale=1.0 / 1024, bias=cst[:, 1:2]), [pb(b), "cst"], ["LN"])
            A_(lambda e, tt=tt: e.activation(out=RS[tt][:], in_=LN[:], func=ACT.Exp, scale=-0.5), ["LN"], [("RS", tt)])
            for c in range(8):
                tm = TM[2 + c % 2]
                V_(lambda e, tm=tm, c=c, tt=tt: e.scalar_tensor_tensor(out=tm[:, 0:512], in0=Y[:, c, tt * 512:(tt + 1) * 512], scalar=AG[par][:, agcol + c:agcol + c + 1],
                                                                     in1=RS[tt][:], op0=ALU.mult, op1=ALU.mult),
                   [Yt(c, tt), ("AG", par), ("RS", tt)], [f"TM{2 + c % 2}"])
                A_(lambda e, tm=tm, c=c, tt=tt: e.activation(out=H[:, c, tt * 512:(tt + 1) * 512], in_=tm[:, 0:512], func=ACT.Identity,
                                                             bias=MODA[par][:, shcol + c:shcol + c + 1], scale=1.0),
                   [f"TM{2 + c % 2}", ("MODA", par)], [Ht(c, tt)])

    def out_proj(w2d, src_view, src_toks, gcol, par, mid=None):
        vs = [wload_cast(kview(w2d)[:, :, half * 512:(half + 1) * 512], 8, 512) for half in range(2)]
        nb = 0
        for tt in range(2):
            for n in range(8):
                if mid is not None and tt == 1 and n == 2:
                    mid()
                v, tk = vs[n // 4]
                n4 = n % 4
                b = nb % 4
                nb += 1
                for k in range(8):
                    mm(PB[b][:], v[:, k, n4 * 128:(n4 + 1) * 128], src_view(k, tt), k == 0, k == 7, [tk, src_toks(k, tt)], pb(b))
                V_(lambda e, b=b, n=n, tt=tt: e.scalar_tensor_tensor(out=Y[:, n, tt * 512:(tt + 1) * 512], in0=PB[b][:], scalar=MODA[par][:, gcol + n:gcol + n + 1],
                                                                     in1=Y[:, n, tt * 512:(tt + 1) * 512], op0=ALU.mult, op1=ALU.add),
                   [pb(b), ("MODA", par), Yt(n, tt)], [Yt(n, tt)])

    def ffn(i, par, inter_tasks, skip0=False, mid_next=None, pre_down=None):
        groups = [(0, 512), (512, 512), (1024, 512), (1536, 512), (2048, 512), (2560, 256)]
        pre = {0: (wload_cast(kview(ffn_wg[i])[:, :, 0:512], 8, 512), wload_cast(kview(ffn_wu[i])[:, :, 0:512], 8, 512))}
        norm_mod(8, 24, par, tiles=((1,) if skip0 else (0, 1)))
        Abuf = BIG
        nb = 0
        for (c0, cw) in groups:
            if c0 in pre:
                (vg, tg), (vu, tu) = pre[c0]
            else:
                vg, tg = wload_cast(kview(ffn_wg[i])[:, :, c0:c0 + cw], 8, cw)
                vu, tu = wload_cast(kview(ffn_wu[i])[:, :, c0:c0 + cw], 8, cw)
            for tt in range(2):
                for f4 in range(cw // 128):
                    f = c0 // 128 + f4
                    bgt = nb % 2
                    but = 2 + nb % 2
                    nb += 1
                    for k in range(8):
                        mm(PB[bgt][:], vg[:, k, f4 * 128:(f4 + 1) * 128], H[:, k, tt * 512:(tt + 1) * 512], k == 0, k == 7, [tg, Ht(k, tt)], pb(bgt))
                    for k in range(8):
                        mm(PB[but][:], vu[:, k, f4 * 128:(f4 + 1) * 128], H[:, k, tt * 512:(tt + 1) * 512], k == 0, k == 7, [tu, Ht(k, tt)], pb(but))
                    sg = TM[2 + nb % 2]
                    A_(lambda e, sg=sg, bgt=bgt: e.activation(out=sg[:, 0:512], in_=PB[bgt][:], func=ACT.Silu), [pb(bgt)], [f"TM{2 + nb % 2}"])
                    V_(lambda e, sg=sg, but=but, f=f, tt=tt: e.tensor_tensor(out=Abuf[:, f, tt * 512:(tt + 1) * 512], in0=PB[but][:], in1=sg[:, 0:512], op=ALU.mult),
                       [pb(but), f"TM{2 + nb % 2}"], [bg(f)])
        if pre_down is not None:
            pre_down()
        nb = 0

        def down_one(v, tk, n, tt):
            nonlocal nb
            b = 4 + nb % 2
            nb += 1
            for f in range(NFF):
                mm(PB[b][:], v[:, f, :], Abuf[:, f, tt * 512:(tt + 1) * 512], f == 0, f == NFF - 1, [tk, bg(f)], pb(b))
            V_(lambda e: e.scalar_tensor_tensor(out=Y[:, n, tt * 512:(tt + 1) * 512], in0=PB[b][:], scalar=MODA[par][:, 40 + n:41 + n],
                                                in1=Y[:, n, tt * 512:(tt + 1) * 512], op0=ALU.mult, op1=ALU.add),
               [pb(b), ("MODA", par), Yt(n, tt)], [Yt(n, tt)])

        for n in range(4):
            v, tk = wload_cast(kview(ffn_wd[i])[:, :, n * 128:(n + 1) * 128], 22, 128)
            for tt in range(2):
                down_one(v, tk, n, tt)
        vs = [wload_cast(kview(ffn_wd[i])[:, :, n * 128:(n + 1) * 128], 22, 128) for n in range(4, 8)]
        for tt in range(2):
            for n in range(4, 8):
                if mid_next is not None and tt == 1 and n == 5:
                    mid_next()
                down_one(vs[n - 4][0], vs[n - 4][1], n, tt)

    def filter_sin(j):
        ft = TM[0]
        P.dma_in("sync", lambda e: e.dma_start(out=ft[0:33, 0:1024], in_=featsT_d), "TM0")
        w1s = UT[:].bitcast(F32)
        P.dma_in("sync", lambda e: e.dma_start(out=w1s[0:33, 0:64], in_=f_w1[j]), "UT")
        P.dma_in("sync", lambda e: e.dma_start(out=w1s[0:64, 64:128], in_=f_w2[j]), "UT")
        P.dma_in("gpsimd", lambda e: e.dma_start(out=BIASB[:], in_=hy_bias[j:j + 1, :].partition_broadcast(128)), "BIASB")
        w3b = BIGflat[:, 32 * 1024:34 * 1024]
        P.dma_in("gpsimd", lambda e: e.dma_start(out=w3b[0:64, :], in_=f_w3[j]), bg(32))

        def sin_layer(src, lhsT, K, bcol, fcol, dst, srctok, dsttok):
            for tt in range(2):
                b = 6
                mm(PB[b][0:64, :], lhsT, src[0:K, tt * 512:(tt + 1) * 512], True, True, ["UT", srctok], pb(b))
                V_(lambda e, b=b, tt=tt: e.tensor_scalar(out=dst[0:64, tt * 512:(tt + 1) * 512], in0=PB[b][0:64, :], scalar1=PV[0:64, bcol:bcol + 1], scalar2=PV[0:64, fcol:fcol + 1],
                                                         op0=ALU.add, op1=ALU.mult), [pb(b), "PV"], [dsttok])
            A_(lambda e: e.activation(out=dst[0:64, 0:1024], in_=dst[0:64, 0:1024], func=ACT.Sin, scale=1.0 / 3.0), [dsttok], [dsttok])
            s2 = ZV
            V_(lambda e: e.tensor_tensor(out=s2[0:64, 0:1024], in0=dst[0:64, 0:1024], in1=dst[0:64, 0:1024], op=ALU.mult), [dsttok], ["ZV"])
            V_(lambda e: e.tensor_scalar(out=s2[0:64, 0:1024], in0=s2[0:64, 0:1024], scalar1=-4.0, scalar2=3.0, op0=ALU.mult, op1=ALU.add), ["ZV"], ["ZV"])
            V_(lambda e: e.tensor_tensor(out=dst[0:64, 0:1024], in0=dst[0:64, 0:1024], in1=s2[0:64, 0:1024], op=ALU.mult), [dsttok, "ZV"], [dsttok])

        h1 = TM[1]
        sin_layer(ft, w1s[0:33, 0:64], 33, R_FB1 + j, R_FREQ + 2 * j, h1, "TM0", "TM1")
        h2 = TM[0]
        sin_layer(h1, w1s[0:64, 64:128], 64, R_FB2 + j, R_FREQ + 2 * j + 1, h2, "TM1", "TM0")
        for tt in range(2):
            V_(lambda e, tt=tt: e.tensor_copy(out=EB[tt][0:64, :], in_=h2[0:64, tt * 512:(tt + 1) * 512]), ["TM0"], [("EB", tt)])

    def filter_gen(j, inter, do_sin=True):
        if do_sin:
            filter_sin(j)
        w3b = BIGflat[:, 32 * 1024:34 * 1024]
        w3t = bg(32)
        sumw = UT
        P.dma_in("sync", lambda e: e.dma_start(out=sumw[:], in_=sumw_d), "UT")
        wins = {}
        fbuf = [(TM[2][:, 0:512], "TM2", TM[3][:, 0:512], "TM3", SQ[0], ("SQ", 0), SQ[1], ("SQ", 1)),
                (ZV[:, 0:512], "ZV", ZV[:, 512:1024], "ZV", SQ[2], ("SQ", 2), EB[2], ("EB", 2))]

        def fA(it):
            tc, ct = it // 2, it % 2
            if ct == 0:
                wins[tc] = wload_f32(win_d[tc * 128:(tc + 1) * 128, :, :], 2, 1024)
            wv, wt = wins[tc]
            hf, hft, hb, hbt, sq0, sq0t, sq1, sq1t = fbuf[it % 2]
            bf_, bb_ = 2 * (it % 2), 2 * (it % 2) + 1
            h2b = EB[tc // 4][0:64, (tc % 4) * 128:(tc % 4 + 1) * 128]
            mm(PB[bf_][:], h2b, w3b[0:64, ct * 512:(ct + 1) * 512], True, True, [("EB", tc // 4), w3t, bg(33)], pb(bf_))
            mm(PB[bb_][:], h2b, w3b[0:64, 1024 + ct * 512:1024 + (ct + 1) * 512], True, True, [("EB", tc // 4), w3t, bg(33)], pb(bb_))
            V_(lambda e: e.tensor_tensor(out=hf, in0=PB[bf_][:], in1=wv[:, 0, ct * 512:(ct + 1) * 512], op=ALU.mult), [pb(bf_), wt], [hft])
            V_(lambda e: e.tensor_tensor(out=hb, in0=PB[bb_][:], in1=wv[:, 1, ct * 512:(ct + 1) * 512], op=ALU.mult), [pb(bb_), wt], [hbt])
            A_(lambda e: e.activation(out=sq0[:], in_=hf, func=ACT.Square), [hft], [sq0t])
            A_(lambda e: e.activation(out=sq1[:], in_=hb, func=ACT.Square), [hbt], [sq1t])
            V_(lambda e: e.tensor_tensor(out=BIG[:, 16 + tc, ct * 512:(ct + 1) * 512], in0=hf, in1=hb, op=ALU.add), [hft, hbt], [bg(16 + tc)])
            V_(lambda e: e.tensor_tensor(out=BIG[:, 24 + tc, ct * 512:(ct + 1) * 512], in0=hb, in1=hf, op=ALU.subtract), [hft, hbt], [bg(24 + tc)])

        def fB(it):
            tc, ct = it // 2, it % 2
            hf, hft, hb, hbt, sq0, sq0t, sq1, sq1t = fbuf[it % 2]
            bn = 4 + ct
            mm(PB[bn][:], sumw[:, tc * 128:(tc + 1) * 128], sq0[:], tc == 0, False, ["UT", sq0t], pb(bn))
            mm(PB[bn][:], sumw[:, tc * 128:(tc + 1) * 128], sq1[:], False, tc == 7, ["UT", sq1t], pb(bn))

        for it in range(17):
            if it < 16:
                fA(it)
            if it >= 1:
                fB(it - 1)
            if it % 2 == 1 and inter:
                inter.pop(0)()
        RN = TM[1]
        for ct in range(2):
            A_(lambda e, ct=ct: e.activation(out=LN[:], in_=PB[4 + ct][:], func=ACT.Ln, scale=1.0, bias=cst[:, 1:2]), [pb(4 + ct), "cst"], ["LN"])
            A_(lambda e, ct=ct: e.activation(out=RN[:, ct * 512:(ct + 1) * 512], in_=LN[:], func=ACT.Exp, scale=-0.5), ["LN"], ["TM1"])

        while inter:
            inter.pop(0)()

    def hyena(i, j, par, prefilt=False, skip0=False, ffn_next=False, presin=False):
        X0 = BIG
        nslot = {}

        def get_w(n):
            s = n // 4
            if s not in nslot:
                nslot[s] = wload_cast(kview(hy_w_in[j])[:, :, s * 512:(s + 1) * 512], 8, 512)
            return nslot[s]
        norm_mod(0, 0, par, tiles=((1,) if skip0 else (0, 1)))
        if not prefilt:
            filter_gen(j, [], do_sin=not presin)
        RN = TM[1]
        wv3 = []
        for s in range(6):
            wv3.append(None)
        PSB = [TM[2], TM[3]]
        X1 = TM[0]
        npb = 0
        def conv_chunk(n, dst_fn, dst_tok, idx):
            v, tk = get_w(n)
            n4 = n % 4
            psb = PSB[idx % 2]
            ptok = f"TM{2 + idx % 2}"
            pv = psb[:, 0:1032].rearrange("p (b w) -> p b w", b=4)
            for tt in range(2):
                b = idx % 2 * 2 + tt
                for k in range(8):
                    mm(PB[b][:], v[:, k, n4 * 128:(n4 + 1) * 128], H[:, k, tt * 512:(tt + 1) * 512], k == 0, k == 7, [tk, Ht(k, tt)], pb(b))
                A_(lambda e, b=b, tt=tt: e.activation(out=pv[:, 2 * tt:2 * tt + 2, 1:257], in_=PB[b][:].rearrange("p (b w) -> p b w", b=2), func=ACT.Copy), [pb(b)], [ptok])
                A_(lambda e, b=b, tt=tt: e.activation(out=RSB[:, tt * 512:(tt + 1) * 512].rearrange("p (b w) -> p b w", b=2), in_=PB[b][:].rearrange("p (b w) -> p b w", b=2),
                                                     func=ACT.Identity, scale=PV[:, R_CONVW + j * 72 + 24 + n:R_CONVW + j * 72 + 25 + n],
                                                     bias=PV[:, R_CONVB + j * 24 + n:R_CONVB + j * 24 + n + 1]), [pb(b), "PV"], [("RS", tt)])
            V_(lambda e: e.memset(pv[:, 0:1, 0:1], 0.0), [], [ptok])
            V_(lambda e: e.memset(pv[:, 3:4, 257:258], 0.0), [], [ptok])
            V_(lambda e: e.tensor_scalar(out=pv[:, 1:4, 0:1], in0=pv[:, 0:3, 256:257], scalar1=cst[:, 0:1], scalar2=None, op0=ALU.mult), [ptok, "cst"], [ptok])
            V_(lambda e: e.tensor_scalar(out=pv[:, 0:3, 257:258], in0=pv[:, 1:4, 1:2], scalar1=cst[:, 0:1], scalar2=None, op0=ALU.mult), [ptok, "cst"], [ptok])
            cw = R_CONVW + j * 72
            acc = TM[1] if False else None
            dst = dst_fn()
            accv = ACC[:, 0:1024].rearrange("p (b w) -> p b w", b=4)
            V_(lambda e: e.scalar_tensor_tensor(out=accv, in0=pv[:, :, 0:256], scalar=PV[:, cw + n:cw + n + 1], in1=accv, op0=ALU.mult, op1=ALU.add), [ptok, "PV", ("RS", 0), ("RS", 1)], [("RS", 0), ("RS", 1)])
            V_(lambda e: e.scalar_tensor_tensor(out=dst, in0=pv[:, :, 2:258], scalar=PV[:, cw + 48 + n:cw + 49 + n], in1=accv, op0=ALU.mult, op1=ALU.add), [ptok, "PV", ("RS", 0), ("RS", 1)], dst_tok)
            pump(1)

        ACC = RSB
        idx = 0
        pendT = None

        def mkT(c):
            def run():
                bT = 4 + c % 2
                pbf = PB[bT][:].bitcast(BF16)
                for tcn in range(8):
                    T_(lambda e, tcn=tcn: e.transpose(pbf[:, tcn * 128:(tcn + 1) * 128], UT[:, tcn * 128:(tcn + 1) * 128], identb[:]), ["UT", "identb"], [pb(bT)])
                A_(lambda e: e.activation(out=BIG[:, 8:16, c * 128:(c + 1) * 128], in_=pbf.rearrange("p (j w) -> p j w", j=8), func=ACT.Copy),
                   [pb(bT)], [bg(8 + q) for q in range(8)])
            return run

        for c in range(8):
            conv_chunk(c, lambda c=c: X0[:, c, :].rearrange("p (b w) -> p b w", b=4), [bg(c)], idx); idx += 1
            if pendT is not None:
                pendT()
                pendT = None
            conv_chunk(8 + c, lambda: X1[:, 0:1024].rearrange("p (b w) -> p b w", b=4), ["TM0"], idx); idx += 1
            conv_chunk(16 + c, lambda: ZV[:, 0:1024].rearrange("p (b w) -> p b w", b=4), ["ZV"], idx); idx += 1
            V_(lambda e: e.tensor_tensor(out=UT[:], in0=ZV[:, 0:1024], in1=X1[:, 0:1024], op=ALU.mult), ["ZV", "TM0"], ["UT"])
            pendT = mkT(c)
        pendT()

        YR = H
        cslots = {}

        def get_ct(fc):
            s = fc // 4
            if s not in cslots:
                a = wload_bf(kview(CT_d)[:, :, s * 512:(s + 1) * 512], 8, 512)
                b_ = wload_bf(kview(ST_d)[:, :, s * 512:(s + 1) * 512], 8, 512)
                cslots[s] = (a, b_)
            return cslots[s]

        it = 0
        for fc in range(8):
            (cv, ctk), (sv, stk) = get_ct(fc)
            f4 = fc % 4
            for ct in range(2):
                bA, bB, bP, bQ = (0, 1, 2, 3) if it % 2 == 0 else (4, 5, 6, 3)
                it += 1
                cs = slice(ct * 512, (ct + 1) * 512)
                for m in range(8):
                    lc = cv[:, m, f4 * 128:(f4 + 1) * 128]
                    ls = sv[:, m, f4 * 128:(f4 + 1) * 128]
                    mm(PB[bA][:], lc, BIG[:, 8 + m, cs], m == 0, m == 7, [ctk, bg(8 + m)], pb(bA))
                    mm(PB[bP][:], lc, BIG[:, 16 + m, cs], m == 0, m == 7, [ctk, bg(16 + m)], pb(bP))
                    mm(PB[bB][:], ls, BIG[:, 8 + m, cs], m == 0, m == 7, [stk, bg(8 + m)], pb(bB))
                    mm(PB[bQ][:], ls, BIG[:, 24 + m, cs], m == 0, m == 7, [stk, bg(24 + m)], pb(bQ))
                Pn = TM[2]
                Qn = TM[3]
                V_(lambda e, bP=bP, cs=cs: e.tensor_tensor(out=Pn[:, 0:512], in0=PB[bP][:], in1=RN[:, cs], op=ALU.mult), [pb(bP), "TM1"], ["TM2"])
                V_(lambda e, cs=cs: e.tensor_tensor(out=Pn[:, 0:512], in0=Pn[:, 0:512], in1=BIASB[:, cs], op=ALU.add), ["TM2", "BIASB"], ["TM2"])
                V_(lambda e, bQ=bQ, cs=cs: e.tensor_tensor(out=Qn[:, 0:512], in0=PB[bQ][:], in1=RN[:, cs], op=ALU.mult), [pb(bQ), "TM1"], ["TM3"])
                t1 = TM[0]
                V_(lambda e, bA=bA: e.tensor_tensor(out=t1[:, 0:512], in0=PB[bA][:], in1=Pn[:, 0:512], op=ALU.mult), [pb(bA), "TM2"], ["TM0"])
                V_(lambda e, bB=bB: e.tensor_tensor(out=t1[:, 512:1024], in0=PB[bB][:], in1=Qn[:, 0:512], op=ALU.mult), [pb(bB), "TM3"], ["TM0"])
                V_(lambda e, fc=fc, cs=cs: e.tensor_tensor(out=YR[:, fc, cs], in0=t1[:, 0:512], in1=t1[:, 512:1024], op=ALU.add), ["TM0"], [Ht(fc, ct)])
                t2 = ZV
                V_(lambda e, bB=bB: e.tensor_tensor(out=t2[:, 0:512], in0=PB[bB][:], in1=Pn[:, 0:512], op=ALU.mult), [pb(bB), "TM2"], ["ZV"])
                V_(lambda e, bA=bA: e.tensor_tensor(out=t2[:, 512:1024], in0=PB[bA][:], in1=Qn[:, 0:512], op=ALU.mult), [pb(bA), "TM3"], ["ZV"])
                V_(lambda e, fc=fc, cs=cs: e.tensor_tensor(out=BIG[:, 32 + fc, cs], in0=t2[:, 0:512], in1=t2[:, 512:1024], op=ALU.subtract), ["ZV"], [bg(32 + fc)])
                pump(2)

        nb = 0
        for tt in range(2):
            (civ, citk) = wload_bf(kview(CI_d)[:, :, tt * 512:(tt + 1) * 512], 8, 512)
            (siv, sitk) = wload_bf(kview(SI_d)[:, :, tt * 512:(tt + 1) * 512], 8, 512)
            for c in range(8):
                b = nb % 4
                nb += 1
                for fc in range(8):
                    mm(PB[b][:], YR[:, fc, c * 128:(c + 1) * 128], civ[:, fc, :], fc == 0, False, [citk, Ht(fc, c // 4)], pb(b))
                    mm(PB[b][:], BIG[:, 32 + fc, c * 128:(c + 1) * 128], siv[:, fc, :], False, fc == 7, [sitk, bg(32 + fc)], pb(b))
                V_(lambda e, b=b, c=c, tt=tt: e.tensor_tensor(out=BIG[:, 8 + c, tt * 512:(tt + 1) * 512], in0=PB[b][:], in1=X0[:, c, tt * 512:(tt + 1) * 512], op=ALU.mult),
                   [pb(b), bg(c)], [bg(8 + c)])
        out_proj(hy_w_out[j], lambda k, tt: BIG[:, 8 + k, tt * 512:(tt + 1) * 512], lambda k, tt: bg(8 + k), 16, par,
                 mid=(lambda: norm_mod(8, 24, par, tiles=(0,))) if ffn_next else None)

    def attention(i, j, par, skip0=False, ffn_next=False):
        wq = [wload_cast(kview(at_w_qkv[j])[:, :, s * 512:(s + 1) * 512], 8, 512) for s in range(3)]
        norm_mod(0, 0, par, tiles=((1,) if skip0 else (0, 1)))
        QT = BIG
        KT = BIGflat[:, 16 * 1024:19 * 1024].rearrange("p (k s) -> p k s", k=2)
        VV = BIGflat[:, 19 * 1024:22 * 1024].rearrange("p (k e) -> p k e", k=12)
        ktoks = [bg(16), bg(17), bg(18)]
        vtoks = [bg(19), bg(20), bg(21)]
        CS = BIGF[:, 24 * 512:28 * 512]
        P.dma_in("sync", lambda e: e.dma_start(out=CS[:, 0:1024], in_=cosF_d), bg(24))
        P.dma_in("sync", lambda e: e.dma_start(out=CS[:, 1024:2048], in_=sinF_d), bg(26))
        cstoks = [bg(24), bg(26)]
        NK = BIGF[:, 28 * 512:32 * 512].rearrange("p (t e) -> p t e", t=8)
        NV = BIGF[:, 32 * 512:36 * 512].rearrange("p (t e) -> p t e", t=8)
        nktoks = [bg(28 + q) for q in range(4)]
        nvtoks = [bg(32 + q) for q in range(4)]
        ckl = TM[3][:, 0:1024].rearrange("p (c e) -> p c e", c=4)
        P.dma_in("sync", lambda e: e.dma_start(out=ckl, in_=cache_k[j].rearrange("(c p) e -> p c e", p=128)), "TM3")
        P.dma_in("gpsimd", lambda e: e.dma_start(out=VV[:, 8:12, :], in_=cache_v[j].rearrange("(c p) e -> p c e", p=128)), bg(21))
        items = [(tt, n) for tt in range(2) for n in range(10)]
        NI = len(items)

        def stageA(it):
            tt, n = items[it]
            v, tk = wq[n // 4]
            n4 = n % 4
            b = it % 3
            for k in range(8):
                mm(PB[b][:], v[:, k, n4 * 128:(n4 + 1) * 128], H[:, k, tt * 512:(tt + 1) * 512], k == 0, k == 7, [tk, Ht(k, tt)], pb(b))
            s = SQ[it % 3]
            stok = ("SQ", it % 3)
            A_(lambda e: e.activation(out=s[:], in_=PB[b][:], func=ACT.Square), [pb(b)], [stok])

        def stageB(it):
            tt, n = items[it]
            gcol = (R_QN + j) if n < 8 else (R_KN + j)
            b = it % 3
            bs = 3 + it % 2
            s = SQ[it % 3]
            stok = ("SQ", it % 3)
            mm(PB[bs][:], onesb[:], s[:], True, True, [stok, "onesb"], pb(bs))
            A_(lambda e: e.activation(out=LN[:], in_=PB[bs][:], func=ACT.Ln, scale=1.0 / 128, bias=cst[:, 1:2]), [pb(bs), "cst"], ["LN"])
            rs = RS[it % 2]
            A_(lambda e: e.activation(out=rs[:], in_=LN[:], func=ACT.Exp, scale=-0.5), ["LN"], [("RS", it % 2)])
            qn = TM[it % 2]
            qtok = f"TM{it % 2}"
            V_(lambda e: e.scalar_tensor_tensor(out=qn[:, 0:512], in0=PB[b][:], scalar=PV[:, gcol:gcol + 1], in1=rs[:], op0=ALU.mult, op1=ALU.mult),
               [pb(b), "PV", ("RS", it % 2)], [qtok])

        def stageC(it):
            tt, n = items[it]
            br = 5
            qn = TM[it % 2]
            qtok = f"TM{it % 2}"
            mm(PB[br][:], rotT[:], qn[:, 0:512], True, True, ["rotT", qtok], pb(br))
            t1 = ZV
            V_(lambda e: e.tensor_tensor(out=t1[:, 0:512], in0=qn[:, 0:512], in1=CS[:, tt * 512:(tt + 1) * 512], op=ALU.mult), [qtok] + cstoks, ["ZV"])
            V_(lambda e: e.tensor_tensor(out=t1[:, 512:1024], in0=PB[br][:], in1=CS[:, 1024 + tt * 512:1024 + (tt + 1) * 512], op=ALU.mult), [pb(br)] + cstoks, ["ZV"])
            if n < 8:
                V_(lambda e: e.tensor_tensor(out=QT[:, n, tt * 512:(tt + 1) * 512], in0=t1[:, 0:512], in1=t1[:, 512:1024], op=ALU.add), ["ZV"], [bg(n)])
            else:
                kv = n - 8
                V_(lambda e: e.tensor_tensor(out=KT[:, kv, tt * 512:(tt + 1) * 512], in0=t1[:, 0:512], in1=t1[:, 512:1024], op=ALU.add), ["ZV"], ktoks)
                bt = 6
                for q in range(4):
                    T_(lambda e, q=q: e.transpose(PB[bt][:, q * 128:(q + 1) * 128], qn[:, q * 128:(q + 1) * 128], ident[:]), [qtok, "ident"], [pb(bt)])
                A_(lambda e: e.activation(out=NK[:, tt * 4:(tt + 1) * 4, kv * 128:(kv + 1) * 128], in_=PB[bt][:].rearrange("p (q d) -> p q d", q=4), func=ACT.Copy),
                   [pb(bt)], nktoks)

        for step in range(NI + 2):
            if step < NI:
                stageA(step)
            if 0 <= step - 1 < NI:
                stageB(step - 1)
            if 0 <= step - 2 < NI:
                stageC(step - 2)
            pump(1)
        for kv in range(2):
            b = kv
            for cc in range(4):
                T_(lambda e, b=b, cc=cc, kv=kv: e.transpose(PB[b][:, cc * 128:(cc + 1) * 128], ckl[:, cc, kv * 128:(kv + 1) * 128], ident[:]), ["TM3", "ident"], [pb(b)])
            V_(lambda e, b=b, kv=kv: e.tensor_copy(out=KT[:, kv, 1024:1536], in_=PB[b][:]), [pb(b)], ktoks)
        v2, tk2 = wq[2]
        for tcn in range(8):
            b = 6 if tcn % 2 == 0 else 4
            for k in range(8):
                mm(PB[b][:, 0:256], H[:, k, tcn * 128:(tcn + 1) * 128], v2[:, k, 256:512], k == 0, k == 7, [tk2, Ht(k, tcn // 4)], pb(b))
            A_(lambda e, b=b, tcn=tcn: e.activation(out=NV[:, tcn, :], in_=PB[b][:, 0:256], func=ACT.Copy), [pb(b)], nvtoks)
            V_(lambda e, b=b, tcn=tcn: e.tensor_copy(out=VV[:, tcn, :], in_=NV[:, tcn, :]), nvtoks, vtoks)
        jj = j
        P.dma_out("sync", lambda e: e.dma_start(out=nk_d[jj].rearrange("(t p) e -> p t e", p=128), in_=NK), nktoks[0])
        P.dma_out("sync", lambda e: e.dma_start(out=nv_d[jj].rearrange("(t p) e -> p t e", p=128), in_=NV), nvtoks[0])
        for tks, oidx in ((nktoks, -2), (nvtoks, -1)):
            o = P.ops[oidx]
            for t in tks[1:]:
                tt_ = P.tok(t)
                if tt_.w is not None:
                    o.deps.add(tt_.w)
                tt_.readers.append(o)

        sb_i = 0
        acc_i = 0
        for kv in range(2):
            for g in range(4):
                h = kv * 4 + g
                for tt in range(2):
                    bO = 4 + acc_i % 2
                    bS = 3 if acc_i % 2 == 0 else 6
                    acc_i += 1
                    qs = slice(tt * 512, (tt + 1) * 512)
                    pend = []

                    def s_step(kc):
                        nonlocal sb_i
                        b = sb_i % 3
                        eb = EB[sb_i % 4]
                        etok = ("EB", sb_i % 4)
                        sb_i += 1
                        mm(PB[b][:], KT[:, kv, kc * 128:(kc + 1) * 128], QT[:, h, qs], True, True, ktoks + [bg(h)], pb(b))
                        for half in range(2):
                            qb = tt * 2 + half
                            A_(lambda e, b=b, eb=eb, half=half, qb=qb, kc=kc: e.activation(out=eb[:, half * 256:(half + 1) * 256], in_=PB[b][:, half * 256:(half + 1) * 256], func=ACT.Exp,
                                                                                         scale=SM_SCALE, bias=maskb[:, kc * 4 + qb:kc * 4 + qb + 1]), [pb(b), "maskb"], [etok])
                        return (kc, eb, etok)

                    def pv_step(kc, eb, etok):
                        mm(PB[bO][:], VV[:, kc, kv * 128:(kv + 1) * 128], eb[:], kc == 0, kc == 11, vtoks + [etok], pb(bO))
                        mm(PB[bS][:], onesb[:], eb[:], kc == 0, kc == 11, ["onesb", etok], pb(bS))

                    for kc in range(12):
                        pend.append(s_step(kc))
                        if len(pend) > 2:
                            pv_step(*pend.pop(0))
                    while pend:
                        pv_step(*pend.pop(0))
                    rsum = TM[acc_i % 2]
                    rtok = f"TM{acc_i % 2}"
                    V_(lambda e, rsum=rsum, bS=bS: e.reciprocal(out=rsum[:, 0:512], in_=PB[bS][:]), [pb(bS)], [rtok])
                    V_(lambda e, rsum=rsum, bO=bO, h=h, qs=qs: e.tensor_tensor(out=BIG[:, 8 + h, qs], in0=PB[bO][:], in1=rsum[:, 0:512], op=ALU.mult), [pb(bO), rtok], [bg(8 + h)])
                    pump(2)
        out_proj(at_w_out[j], lambda k, tt: BIG[:, 8 + k, tt * 512:(tt + 1) * 512], lambda k, tt: bg(8 + k), 16, par,
                 mid=(lambda: norm_mod(8, 24, par, tiles=(0,))) if ffn_next else None)

    ZV = P.sb("ZVt", [128, 1024], F32)
    if nsteps > 0:
        filter_gen(0, mod_tasks(0))
    else:
        for t in mod_tasks(0):
            t()
    mixer_skip0 = False
    sin_done = False
    for i in range(NLAYER):
        par = i % 2
        j = i // 2
        need_next = (i + 1 < NLAYER) and (2 * (i + 1) < nsteps)
        if need_next:
            pend_tasks.extend(mod_tasks_small(i + 1))
        ffn_runs = 2 * i + 1 < nsteps
        if 2 * i < nsteps:
            if i % 2 == 0:
                hyena(i, j, par, prefilt=(i == 0), skip0=mixer_skip0, ffn_next=ffn_runs, presin=(i == 2 and sin_done))
            else:
                attention(i, j, par, skip0=mixer_skip0, ffn_next=ffn_runs)
        pump(1000)
        mixer_skip0 = False
        if ffn_runs:
            nxt = None
            if need_next:
                pn = (i + 1) % 2
                nxt = (lambda pn=pn: norm_mod(0, 0, pn, tiles=(0,)))
                mixer_skip0 = True
            pd = None
            if need_next and (i + 1) % 2 == 0:
                pd = (lambda jn=(i + 1) // 2: filter_sin(jn))
                sin_done = True
            ffn(i, par, [], skip0=True, mid_next=nxt, pre_down=pd)

    OS = BIGF[:, 0:8192].rearrange("p (j d) -> p j d", j=8)
    for tt in range(2):
        b = 6
        for c in range(8):
            s = SQ[c % 3]
            A_(lambda e, s=s, c=c, tt=tt: e.activation(out=s[:], in_=Y[:, c, tt * 512:(tt + 1) * 512], func=ACT.Square), [Yt(c, tt)], [("SQ", c % 3)])
            mm(PB[b][:], onesb[:], s[:], c == 0, c == 7, [("SQ", c % 3), "onesb"], pb(b))
        A_(lambda e: e.activation(out=LN[:], in_=PB[b][:], func=ACT.Ln, scale=1.0 / 1024, bias=cst[:, 1:2]), [pb(b), "cst"], ["LN"])
        A_(lambda e, tt=tt: e.activation(out=RS[tt][:], in_=LN[:], func=ACT.Exp, scale=-0.5), ["LN"], [("RS", tt)])
        for c in range(8):
            V_(lambda e, c=c, tt=tt: e.scalar_tensor_tensor(out=Y[:, c, tt * 512:(tt + 1) * 512], in0=Y[:, c, tt * 512:(tt + 1) * 512], scalar=PV[:, R_FIN + c:R_FIN + c + 1],
                                                            in1=RS[tt][:], op0=ALU.mult, op1=ALU.mult), [Yt(c, tt), "PV", ("RS", tt)], [Yt(c, tt)])
    nb = 0
    for tcn in range(8):
        for half in range(2):
            b = nb % 4
            nb += 1
            for c4 in range(4):
                c = half * 4 + c4
                T_(lambda e, b=b, c4=c4, c=c, tcn=tcn: e.transpose(PB[b][:, c4 * 128:(c4 + 1) * 128], Y[:, c, tcn * 128:(tcn + 1) * 128], ident[:]), [Yt(c, tcn // 4), "ident"], [pb(b)])
            if nb % 2 == 0:
                V_(lambda e, b=b, tcn=tcn, half=half: e.tensor_copy(out=OS[:, tcn, half * 512:(half + 1) * 512], in_=PB[b][:]), [pb(b)], [bg(2 * tcn + half)])
            else:
                A_(lambda e, b=b, tcn=tcn, half=half: e.activation(out=OS[:, tcn, half * 512:(half + 1) * 512], in_=PB[b][:], func=ACT.Copy), [pb(b)], [bg(2 * tcn + half)])
        P.dma_out("sync", lambda e, tcn=tcn: e.dma_start(out=y_d[tcn * 128:(tcn + 1) * 128, :], in_=OS[:, tcn, :]), bg(2 * tcn))
        o = P.ops[-1]
        t2_ = P.tok(bg(2 * tcn + 1))
        if t2_.w is not None:
            o.deps.add(t2_.w)
        t2_.readers.append(o)
    P.emit()
    return nc, P


NSTEPS = 8
_CACHE = {}


def _consts(L, nrep):
    n = 2 * L
    f = np.arange(L, dtype=np.float64)
    m = np.arange(L, dtype=np.float64)
    th = np.pi * (2.0 * f[None, :] + 1.0) * m[:, None] / n
    CTb = np.cos(th)
    STb = np.sin(th)
    CIb = (2.0 / n) * np.cos(th.T)
    SIb = (2.0 / n) * np.sin(th.T)

    def bd(B):
        out = np.zeros((1024, 1024), np.float64)
        for r in range(nrep):
            out[r * L:(r + 1) * L, r * L:(r + 1) * L] = B
        return out.astype(ml_dtypes.bfloat16)
    t = np.arange(L, dtype=np.float32) / np.float32(L)
    bands = np.arange(1, 17, dtype=np.float32)
    ang = (2.0 * np.float32(math.pi)) * t[:, None] * bands[None, :]
    feats = np.concatenate([t[:, None], np.cos(ang), np.sin(ang)], axis=-1).astype(np.float32)
    featsT = np.tile(feats.T, (1, nrep)).astype(np.float32)
    MIN_DECAY = math.log(1e-2) / 1.5
    MAX_DECAY = math.log(1e-2) / 0.3
    deltas = np.abs(np.linspace(MIN_DECAY, MAX_DECAY, 1024, dtype=np.float32))
    win = np.exp(-t[:, None] * deltas[None, :]).astype(np.float32)
    winb = win * (np.arange(L) > 0).astype(np.float32)[:, None]
    w2 = np.stack([win, winb], axis=1)
    w2 = np.tile(w2, (nrep, 1, 1)).astype(np.float32)
    sumw = np.zeros((128, 8, 128), np.float32)
    sumw[:, 0:L // 128, :] = 1.0
    sumw = sumw.reshape(128, 1024).astype(ml_dtypes.bfloat16)
    return dict(CT=bd(CTb), ST=bd(STb), CI=bd(CIb), SI=bd(SIb), featsT=featsT, win=w2, sumw=sumw)


def _rope_tables():
    T = 1024
    GRID_W = 64
    ROWS = T // GRID_W
    row = np.repeat(np.arange(ROWS), GRID_W).astype(np.float32)
    col = np.tile(np.arange(GRID_W), ROWS).astype(np.float32)
    half = 64
    freqs = (np.float32(10000.0) ** (-np.arange(0, half, 2, dtype=np.float32) / np.float32(half))).astype(np.float32)
    ang = np.concatenate([row[:, None] * freqs[None, :], col[:, None] * freqs[None, :]], axis=-1)
    cos = np.cos(ang).astype(np.float32)
    sin = np.sin(ang).astype(np.float32)
    cosF = np.repeat(cos.T, 2, axis=0)
    sinF = np.repeat(sin.T, 2, axis=0)
    return np.ascontiguousarray(cosF), np.ascontiguousarray(sinF)


def _static():
    if "st" in _CACHE:
        return _CACHE["st"]
    ident = np.eye(128, dtype=np.float32)
    rotT = np.zeros((128, 128), np.float32)
    for i in range(64):
        rotT[2 * i + 1, 2 * i] = -1.0
        rotT[2 * i, 2 * i + 1] = 1.0
    cosF, sinF = _rope_tables()
    samp = _consts(1024, 1)
    prm = _consts(256, 4)
    maskb_s = np.zeros((128, 48), np.float32)
    maskb_p = np.full((128, 48), -30000.0, np.float32)
    for kc in range(8):
        maskb_p[:, kc * 4 + kc // 2] = 0.0
    cst_s = np.zeros((128, 8), np.float32)
    cst_s[:, 0] = 1.0
    cst_s[:, 1] = EPS
    cst_p = cst_s.copy()
    cst_p[:, 0] = 0.0
    st = dict(ident=ident, rotT=rotT,
              samp=dict(samp, cosF=cosF, sinF=sinF, maskb=maskb_s, cst=cst_s),
              prm=dict(prm, cosF=np.ones_like(cosF), sinF=np.zeros_like(sinF), maskb=maskb_p, cst=cst_p))
    _CACHE["st"] = st
    return st


def _pvec(cvec, mod_b, norm_mix, norm_ffn, final_norm, hy_conv_w, hy_conv_b, at_q_norm, at_k_norm, hy_f_b1, hy_f_freq, hy_f_b2):
    pv = np.zeros((NROW, 128), np.float32)
    pv[R_CVEC:R_CVEC + 8] = cvec.reshape(8, 128)
    pv[R_MODB:R_MODB + 192] = mod_b.reshape(4 * 48, 128)
    pv[R_NMIX:R_NMIX + 32] = norm_mix.reshape(32, 128)
    pv[R_NFFN:R_NFFN + 32] = norm_ffn.reshape(32, 128)
    pv[R_FIN:R_FIN + 8] = final_norm.reshape(8, 128)
    pv[R_CONVW:R_CONVW + 144] = hy_conv_w.reshape(2 * 3 * 24, 128)
    pv[R_CONVB:R_CONVB + 48] = hy_conv_b.reshape(48, 128)
    pv[R_QN:R_QN + 2] = at_q_norm
    pv[R_KN:R_KN + 2] = at_k_norm
    pv[R_FB1:R_FB1 + 2, 0:64] = hy_f_b1
    pv[R_FREQ:R_FREQ + 4, 0:64] = hy_f_freq.reshape(4, 64)
    pv[R_FB2:R_FB2 + 2, 0:64] = hy_f_b2
    return pv


def kernel(x_prompt, x_sample, cache_k, cache_v, c, c_ctx,
           mod_w, mod_b, norm_mix, norm_ffn,
           hy_w_in, hy_conv_w, hy_conv_b, hy_f_w1, hy_f_b1, hy_f_freq, hy_f_w2, hy_f_b2, hy_f_w3,
           hy_bias, hy_w_out,
           at_w_qkv, at_q_norm, at_k_norm, at_w_out,
           ffn_w_gate, ffn_w_up, ffn_w_down, final_norm, _nsteps=None):
    nsteps = NSTEPS if _nsteps is None else _nsteps
    f = lambda a: np.ascontiguousarray(np.asarray(a, dtype=np.float32))
    x_prompt, x_sample, cache_k, cache_v, c, c_ctx = map(f, (x_prompt, x_sample, cache_k, cache_v, c, c_ctx))
    st = _static()
    key = ("nc", nsteps)
    if key not in _CACHE:
        _CACHE[key] = build(nsteps)[0]
    nc = _CACHE[key]
    shared = dict(mod_w=f(mod_w), hy_w_in=f(hy_w_in), hy_w_out=f(hy_w_out), at_w_qkv=f(at_w_qkv), at_w_out=f(at_w_out),
                  ffn_wg=f(ffn_w_gate), ffn_wu=f(ffn_w_up), ffn_wd=f(ffn_w_down), f_w1=f(hy_f_w1), f_w2=f(hy_f_w2), f_w3=f(hy_f_w3),
                  hy_bias=f(hy_bias), ident=st["ident"], rotT=st["rotT"])
    pargs = [f(a) for a in (mod_b, norm_mix, norm_ffn, final_norm, hy_conv_w, hy_conv_b, at_q_norm, at_k_norm, hy_f_b1, hy_f_freq, hy_f_b2)]
    in_maps = []
    zero_cache = np.zeros((2, 512, 256), np.float32)
    for r in range(8):
        m = dict(shared)
        if r < 4:
            m["x"] = x_prompt[4 * r:4 * r + 4].reshape(1024, 1024)
            cv = c_ctx
            tb = st["prm"]
            m["cache_k"] = zero_cache
            m["cache_v"] = zero_cache
        else:
            b = r - 4
            m["x"] = x_sample[b]
            cv = c[b]
            tb = st["samp"]
            m["cache_k"] = np.ascontiguousarray(cache_k[b].reshape(2, 512, 256))
            m["cache_v"] = np.ascontiguousarray(cache_v[b].reshape(2, 512, 256))
        m["pvec"] = _pvec(cv, *pargs)
        for k in ("cosF", "sinF", "maskb", "cst", "featsT", "win", "sumw", "CT", "ST", "CI", "SI"):
            m[k] = tb[k]
        in_maps.append(m)
    res = run_bass_kernel_spmd(nc, in_maps, core_ids=list(range(8)))
    outs = res.results
    y_prompt = np.stack([outs[r]["y"] for r in range(4)], 0).reshape(16, 256, 1024).astype(np.float32)
    y_sample = np.stack([outs[r]["y"] for r in range(4, 8)], 0).reshape(4, 1024, 1024).astype(np.float32)
    nk = np.stack([outs[r]["new_k"] for r in range(4)], 0).reshape(4, 2, 4, 256, 2, 128).transpose(0, 2, 1, 3, 4, 5).reshape(16, 2, 256, 2, 128)
    nv = np.stack([outs[r]["new_v"] for r in range(4)], 0).reshape(4, 2, 4, 256, 2, 128).transpose(0, 2, 1, 3, 4, 5).reshape(16, 2, 256, 2, 128)
    return (y_prompt, y_sample, np.ascontiguousarray(nk, dtype=np.float32), np.ascontiguousarray(nv, dtype=np.float32))
```
